# Optimizing a Trainium2 kernel written in Bass

```python
import math
import jax, jax.numpy as jnp
from jax import lax
import numpy as np

D_MODEL = 1024
BATCH = 4
SEQ = 4096
DEPTH = 1

HEAD_DIM = 64
N_SWA_HEADS = 8
N_SWA_KV_HEADS = 2
SWA_GROUP = N_SWA_HEADS // N_SWA_KV_HEADS
N_FOX_HEADS = 8
MIX_WIDTH = (N_SWA_HEADS + N_FOX_HEADS) * HEAD_DIM
WINDOW = 128
BLOCK = 128
N_BUCKETS = 32
MAX_DISTANCE = 128
N_KEYS = 128
N_EXPERTS = N_KEYS * N_KEYS
PEER_HEADS = 8
PEER_TOPK = 16
D_KEY = 256
D_HALF = D_KEY // 2
PEER_CHUNK = 128
EPS = 1e-6
NEG = -1e30

Q_A = N_SWA_HEADS * HEAD_DIM
KV_A = N_SWA_KV_HEADS * HEAD_DIM
QKV_B = N_FOX_HEADS * HEAD_DIM
SPLITS = [Q_A, Q_A + KV_A, Q_A + 2 * KV_A, Q_A + 2 * KV_A + QKV_B,
          Q_A + 2 * KV_A + 2 * QKV_B, Q_A + 2 * KV_A + 3 * QKV_B]
IN_WIDTH = Q_A + 2 * KV_A + 3 * QKV_B + N_FOX_HEADS

kernel_name = "hybrid_swa_sink_fox_peer"


def rms_norm(x, g):
    xf = x.astype(jnp.float32)
    y = xf * lax.rsqrt(jnp.mean(xf * xf, axis=-1, keepdims=True) + EPS)
    return (y * g.astype(jnp.float32)).astype(x.dtype)


def t5_bucket(d):
    n = jnp.maximum(d, 0)
    max_exact = N_BUCKETS // 2
    nf = jnp.maximum(n, 1).astype(jnp.float32)
    large = max_exact + (jnp.log(nf / max_exact) / math.log(MAX_DISTANCE / max_exact)
                         * (N_BUCKETS - max_exact)).astype(jnp.int32)
    large = jnp.minimum(large, N_BUCKETS - 1)
    return jnp.where(n < max_exact, n, large)


def swa_attention(q, k, v, sinks, rel_bias):
    B, S = q.shape[0], q.shape[1]
    nb = S // BLOCK
    qb = q.reshape(B, nb, BLOCK, N_SWA_KV_HEADS, SWA_GROUP, HEAD_DIM)

    def band(t):
        tp = jnp.pad(t, ((0, 0), (BLOCK, 0), (0, 0), (0, 0)))
        tb = tp.reshape(B, nb + 1, BLOCK, N_SWA_KV_HEADS, HEAD_DIM)
        return jnp.concatenate([tb[:, :-1], tb[:, 1:]], axis=2)

    kb, vb = band(k), band(v)
    logits = jnp.einsum('bnqhgd,bnchd->bnhgqc', qb, kb).astype(jnp.float32) * (HEAD_DIM ** -0.5)
    qi = jnp.arange(BLOCK)[:, None]
    ci = jnp.arange(2 * BLOCK)[None, :]
    dist = qi + BLOCK - ci
    in_band = (dist >= 0) & (dist < WINDOW)
    bias = rel_bias.astype(jnp.float32)[t5_bucket(dist)]
    bias = bias.transpose(2, 0, 1).reshape(N_SWA_KV_HEADS, SWA_GROUP, BLOCK, 2 * BLOCK)
    key_pos = jnp.arange(nb)[:, None] * BLOCK - BLOCK + ci
    valid = in_band[None] & (key_pos >= 0)[:, None, :]
    logits = jnp.where(valid[None, :, None, None], logits + bias, NEG)
    sink = jnp.broadcast_to(sinks.astype(jnp.float32).reshape(N_SWA_KV_HEADS, SWA_GROUP, 1, 1),
                            logits.shape[:-1] + (1,))
    probs = jax.nn.softmax(jnp.concatenate([logits, sink], axis=-1), axis=-1)[..., :-1]
    out = jnp.einsum('bnhgqc,bnchd->bnqhgd', probs.astype(v.dtype), vb)
    return out.reshape(B, S, N_SWA_HEADS * HEAD_DIM)


def fox_attention(q, k, v, log_f):
    B, S = q.shape[0], q.shape[1]
    nb = S // BLOCK
    F = jnp.cumsum(log_f, axis=1).transpose(0, 2, 1)
    qb = q.reshape(B, nb, BLOCK, N_FOX_HEADS, HEAD_DIM).transpose(1, 0, 2, 3, 4)
    Fq = F.reshape(B, N_FOX_HEADS, nb, BLOCK).transpose(2, 0, 1, 3)
    kpos = jnp.arange(S)

    def one_block(args):
        i, qi, fq = args
        logits = jnp.einsum('bqhd,bkhd->bhqk', qi, k).astype(jnp.float32) * (HEAD_DIM ** -0.5)
        logits = logits + fq[..., None] - F[:, :, None, :]
        qpos = i * BLOCK + jnp.arange(BLOCK)
        causal = kpos[None, :] <= qpos[:, None]
        probs = jax.nn.softmax(jnp.where(causal, logits, NEG), axis=-1)
        return jnp.einsum('bhqk,bkhd->bqhd', probs.astype(v.dtype), v)

    out = lax.map(one_block, (jnp.arange(nb), qb, Fq))
    return out.transpose(1, 0, 2, 3, 4).reshape(B, S, N_FOX_HEADS * HEAD_DIM)


def peer_ffn(x, w_query, sub_keys1, sub_keys2, expert_u, expert_v):
    B, S, D = x.shape
    xt = x.reshape((B * S) // PEER_CHUNK, PEER_CHUNK, D)

    def chunk(xc):
        qv = (xc @ w_query).reshape(PEER_CHUNK, PEER_HEADS, 2, D_HALF)
        s1 = jnp.einsum('thd,nd->thn', qv[:, :, 0], sub_keys1).astype(jnp.float32)
        s2 = jnp.einsum('thd,nd->thn', qv[:, :, 1], sub_keys2).astype(jnp.float32)
        v1, i1 = lax.top_k(s1, PEER_TOPK)
        v2, i2 = lax.top_k(s2, PEER_TOPK)
        cand_s = (v1[..., :, None] + v2[..., None, :]).reshape(PEER_CHUNK, PEER_HEADS, PEER_TOPK * PEER_TOPK)
        cand_i = (i1[..., :, None] * N_KEYS + i2[..., None, :]).reshape(PEER_CHUNK, PEER_HEADS, PEER_TOPK * PEER_TOPK)
        top_s, pos = lax.top_k(cand_s, PEER_TOPK)
        eidx = jnp.take_along_axis(cand_i, pos, axis=-1)
        g = jax.nn.softmax(top_s, axis=-1)
        pre = jnp.einsum('td,thkd->thk', xc, expert_u[eidx]).astype(jnp.float32)
        coef = (g * jax.nn.gelu(pre, approximate=False)).astype(xc.dtype)
        return jnp.einsum('thk,thkd->td', coef, expert_v[eidx])

    return lax.map(chunk, xt).reshape(B, S, D)


def setup_inputs(seed: int = 0) -> dict:
    key = jax.random.key(seed)
    ks = jax.random.split(key, 20)
    f32 = jnp.float32
    nrm = lambda k, shape, s: (jax.random.normal(k, shape, f32) * s).astype(f32)
    gain = lambda k, shape: (1.0 + 0.02 * jax.random.normal(k, shape, f32)).astype(f32)
    return {
        "x": nrm(ks[0], (BATCH, SEQ, D_MODEL), 1.0),
        "rel_bias": nrm(ks[1], (N_BUCKETS, N_SWA_HEADS), 0.5),
        "norm_mix": gain(ks[2], (DEPTH, D_MODEL)),
        "w_in": nrm(ks[3], (DEPTH, D_MODEL, IN_WIDTH), D_MODEL ** -0.5),
        "q_norm_a": gain(ks[4], (DEPTH, HEAD_DIM)),
        "k_norm_a": gain(ks[5], (DEPTH, HEAD_DIM)),
        "q_norm_b": gain(ks[6], (DEPTH, HEAD_DIM)),
        "k_norm_b": gain(ks[7], (DEPTH, HEAD_DIM)),
        "b_forget": jax.random.uniform(ks[8], (DEPTH, N_FOX_HEADS), f32, 1.0, 5.0),
        "sinks": nrm(ks[9], (DEPTH, N_SWA_HEADS), 0.5),
        "w_out": nrm(ks[10], (DEPTH, MIX_WIDTH, D_MODEL), MIX_WIDTH ** -0.5),
        "norm_ffn": gain(ks[11], (DEPTH, D_MODEL)),
        "w_query": nrm(ks[12], (DEPTH, D_MODEL, PEER_HEADS * D_KEY), D_MODEL ** -0.5),
        "sub_keys1": nrm(ks[13], (DEPTH, N_KEYS, D_HALF), D_HALF ** -0.5),
        "sub_keys2": nrm(ks[14], (DEPTH, N_KEYS, D_HALF), D_HALF ** -0.5),
        "expert_u": nrm(ks[15], (DEPTH, N_EXPERTS, D_MODEL), D_MODEL ** -0.5),
        "expert_v": nrm(ks[16], (DEPTH, N_EXPERTS, D_MODEL), (PEER_HEADS * PEER_TOPK) ** -0.5),
    }


def reference(x, rel_bias, norm_mix, w_in, q_norm_a, k_norm_a, q_norm_b, k_norm_b, b_forget,
              sinks, w_out, norm_ffn, w_query, sub_keys1, sub_keys2, expert_u, expert_v):
    B, S = x.shape[0], x.shape[1]
    for l in range(DEPTH):
        h = rms_norm(x, norm_mix[l])
        proj = h @ w_in[l]
        qa, ka, va, qb, kb, vb, fb = jnp.split(proj, SPLITS, axis=-1)
        qa = rms_norm(qa.reshape(B, S, N_SWA_HEADS, HEAD_DIM), q_norm_a[l])
        ka = rms_norm(ka.reshape(B, S, N_SWA_KV_HEADS, HEAD_DIM), k_norm_a[l])
        va = va.reshape(B, S, N_SWA_KV_HEADS, HEAD_DIM)
        qb = rms_norm(qb.reshape(B, S, N_FOX_HEADS, HEAD_DIM), q_norm_b[l])
        kb = rms_norm(kb.reshape(B, S, N_FOX_HEADS, HEAD_DIM), k_norm_b[l])
        vb = vb.reshape(B, S, N_FOX_HEADS, HEAD_DIM)
        log_f = jax.nn.log_sigmoid(fb.astype(jnp.float32) + b_forget[l].astype(jnp.float32))
        mix = jnp.concatenate([swa_attention(qa, ka, va, sinks[l], rel_bias),
                               fox_attention(qb, kb, vb, log_f)], axis=-1)
        x = x + mix @ w_out[l]
        x = x + peer_ffn(rms_norm(x, norm_ffn[l]), w_query[l], sub_keys1[l], sub_keys2[l],
                         expert_u[l], expert_v[l])
    return x
```

```python
import math
import numpy as np
import concourse.bass as bass
import concourse.mybir as mybir
from concourse.bass_utils import run_bass_kernel_spmd
from contextlib import ExitStack

F32 = mybir.dt.float32
BF16 = mybir.dt.bfloat16
U32 = mybir.dt.uint32
I32 = mybir.dt.int32
ALU = mybir.AluOpType
AF = mybir.ActivationFunctionType
AX = mybir.AxisListType

D = 1024
EPS = 1e-6
NEGM = -30000.0


class Sched:
    ENGS = ('pe', 'dve', 'act', 'pool', 'sp')

    def __init__(self, nc, es):
        self.nc, self.es = nc, es
        self.q = {e: [] for e in self.ENGS}
        self.cnt = {e: 0 for e in ('pe', 'dve', 'act', 'pool')}
        self.sems = {}
        for e in self.cnt:
            self.sems['c_' + e] = es.enter_context(nc.semaphore('c_' + e))
        self.waited = {e: {} for e in self.ENGS}
        self.lw = {}
        self.rd = {}
        self.dcnt = {}
        self.outdeps = {}
        self.arena = None
        self.aoff = 0
        self.AW = 0

    def sb(self, name, shape, dt=F32):
        return self.es.enter_context(self.nc.sbuf_tensor("s_" + name, list(shape), dt))

    def ps(self, name, shape=(128, 512), dt=F32):
        return self.es.enter_context(self.nc.psum_tensor("p_" + name, list(shape), dt))

    def make_arena(self, words):
        self.arena = self.sb("arena", [128, words])
        self.AW = words
        self.aoff = 0

    def av(self, shape, dt=F32):
        n = 1
        for v in shape[1:]:
            n *= v
        if dt == BF16:
            n = (n + 1) // 2
        off = self.aoff
        self.aoff += n
        assert self.aoff <= self.AW, ("arena overflow", self.aoff, self.AW)
        ap = self.arena[0:shape[0], off:off + n]
        if dt != F32:
            ap = ap.bitcast(dt)
        if len(shape) == 3:
            ap = ap.rearrange("p (a b) -> p a b", a=shape[1])
        elif len(shape) == 4:
            ap = ap.rearrange("p (a b c) -> p a b c", a=shape[1], b=shape[2])
        return ap

    def _deps(self, reads, writes):
        deps = {}

        def add(d):
            if d is not None and deps.get(d[0], 0) < d[1]:
                deps[d[0]] = d[1]
        for b in reads:
            add(self.lw.get(b))
        for b in writes:
            add(self.lw.get(b))
            for d in self.rd.get(b, ()):
                add(d)
        return deps

    def _emit_waits(self, eng, deps, skip_self=False):
        for s, v in deps.items():
            if skip_self and s == 'c_' + eng:
                continue
            if self.waited[eng].get(s, 0) < v:
                self.q[eng].append(('w', s, v))
                self.waited[eng][s] = v

    def _commit(self, tok, reads, writes):
        for b in writes:
            self.lw[b] = tok
            self.rd[b] = []
        for b in reads:
            if b not in writes:
                self.rd.setdefault(b, []).append(tok)

    def op(self, eng, fn, reads=(), writes=()):
        deps = self._deps(reads, writes)
        self._emit_waits(eng, deps, skip_self=(eng == 'pe'))
        self.cnt[eng] += 1
        tok = ('c_' + eng, self.cnt[eng])
        self.q[eng].append(('o', fn, 'c_' + eng, 1))
        self._commit(tok, reads, writes)

    def I(self, eng, meth, *args, R=(), W=(), **kw):
        self.op(eng, (meth, args, kw), R, W)

    def dma(self, eng, out_ap, in_ap, reads=(), writes=(), out=False, indirect=None, sem=None):
        if sem is None:
            sem = 'd_' + str(writes[0] if writes else reads[0]) + ('_o' if out else '')
        if sem not in self.sems:
            nm = ''.join(ch if ch.isalnum() else '_' for ch in sem)
            self.sems[sem] = self.es.enter_context(self.nc.semaphore(nm))
            self.dcnt[sem] = 0
        deps = self._deps(reads, writes)
        if self.dcnt[sem] > 0:
            v = 16 * self.dcnt[sem]
            if deps.get(sem, 0) < v:
                deps[sem] = v
        self._emit_waits(eng, deps)
        self.dcnt[sem] += 1
        tok = (sem, 16 * self.dcnt[sem])
        if indirect is not None:
            fn = lambda e: e.indirect_dma_start(out_ap, None, in_ap, indirect)
        else:
            fn = lambda e: e.dma_start(out=out_ap, in_=in_ap)
        self.q[eng].append(('o', fn, sem, 16))
        self._commit(tok, reads, writes)
        if out:
            self.outdeps[sem] = tok[1]

    def barrier(self):
        allv = {}
        for e, c in self.cnt.items():
            if c > 0:
                allv['c_' + e] = c
        for s, c in self.dcnt.items():
            if c > 0:
                allv[s] = 16 * c
        for eng in self.ENGS:
            self._emit_waits(eng, allv)
        self.lw.clear()
        self.rd.clear()

    def finish(self):
        self.barrier()
        nc = self.nc
        S = self

        def run(name, e):
            for it in S.q[name]:
                if it[0] == 'w':
                    e.wait_ge(S.sems[it[1]], it[2])
                else:
                    f = it[1]
                    ins = getattr(e, f[0])(*f[1], **f[2]) if isinstance(f, tuple) else f(e)
                    if isinstance(ins, (list, tuple)):
                        ins = ins[-1]
                    ins.then_inc(S.sems[it[2]], it[3])
        with nc.Block() as block:
            @block.tensor
            def _(e):
                run('pe', e)

            @block.vector
            def _(e):
                run('dve', e)

            @block.scalar
            def _(e):
                run('act', e)

            @block.gpsimd
            def _(e):
                run('pool', e)

            @block.sync
            def _(e):
                run('sp', e)


C_ID, C_J, C_TRI, C_ONE, C_IOTA, C_IOTA16 = 0, 128, 256, 384, 512, 528
CSTW = 544
P_GM, P_GC, P_BF, P_SK = 0, 8, 12, 20
WKV = 450
NGB = 12


def build_nc(NT=16, attn=True, peer=True, npass=4, nkb=32, dbg=False, stage=3):
    nc = bass.Bass("TRN2", target_bir_lowering=False)
    es = ExitStack()
    with es:
        def din(name, shape, dt=F32):
            return nc.dram_tensor(name, list(shape), dt, kind="ExternalInput").ap()
        xo_d = din("xo", [2048, D])
        xf_d = din("xf", [4096, D])
        cst_d = din("cst", [128, CSTW])
        ccst_d = din("ccst", [128, 512])
        prm_d = din("prm", [128, 32])
        wq_d = din("wq4", [4, D, 256])
        wkv_d = din("wkv4", [4, D, WKV])
        wo_d = din("wo4", [4, 256, D])
        relb_d = din("relb", [32, 8])
        oht_d = din("ohtab", [33, 768])
        wqy_d = din("wqy", [D, 2048])
        sk_d = din("sk", [2, 128, 128])
        eu_d = din("eu", [16384, D])
        ev_d = din("ev", [16384, D])
        gf_d = din("gf", [128, D])
        y_d = nc.dram_tensor("y", [2048, D], F32, kind="ExternalOutput").ap()
        scr = nc.dram_tensor("scr", [8, 768], F32)
        if dbg:
            dbg_d = nc.dram_tensor("dbg", [2048, D], F32, kind="ExternalOutput").ap()

        S = Sched(nc, es)
        I = S.I
        x2 = S.sb("x2", [128, 16, D])
        cst = S.sb("cst", [128, CSTW])
        ccst = S.sb("ccst", [128, 512])
        prm = S.sb("prm", [128, 32])
        esink = S.sb("esink", [128, 8])
        S.make_arena(33400)
        banks = [S.ps("bk%d" % i) for i in range(8)]
        T0, T1, P0, T2, SA, SB, S2, OB = banks
        BN = {id(b): "bk%d" % i for i, b in enumerate(banks)}

        ident = cst[:, C_ID:C_ID + 128]
        jmat = cst[:, C_J:C_J + 128]
        tri = cst[:, C_TRI:C_TRI + 128]
        ones = cst[:, C_ONE:C_ONE + 128]
        iota = cst[:, C_IOTA:C_IOTA + 16]
        iota16 = cst[:, C_IOTA16:C_IOTA16 + 16]
        foxmask = ccst[:, 0:256].rearrange("p (a b) -> p a b", a=2)
        mab = ccst[:, 256:512].rearrange("p (a b) -> p a b", a=2)

        S.dma('sp', cst[:], cst_d, writes=['cst'])
        S.dma('sp', ccst[:], ccst_d, writes=['ccst'])
        S.dma('sp', prm[:], prm_d, writes=['prm'])
        for m in range(16):
            S.dma('sp', x2[:, m, :], xo_d[m * 128:(m + 1) * 128, :], writes=[('x2', m)])

        def rms_rstd(src_ap, src_name, ssq, std, rstd, junk, tag, n):
            I('act', 'activation', out=junk, in_=src_ap, func=AF.Square, accum_out=ssq,
              R=[src_name], W=['junk', 'ssq' + tag])
            I('act', 'activation', out=std, in_=ssq, func=AF.Sqrt, bias=EPS, scale=1.0 / n,
              R=['ssq' + tag], W=['std' + tag])
            I('dve', 'reciprocal', rstd, std, R=['std' + tag], W=['rstd' + tag])

        def transpose8(src_ap, src_name, dstT, dst_name):
            for c in range(8):
                bk = T0 if c < 4 else T1
                I('pe', 'transpose', bk[:, (c % 4) * 128:(c % 4 + 1) * 128], src_ap[:, c * 128:(c + 1) * 128], ident,
                  R=[src_name, 'cst'], W=[BN[id(bk)]])
            I('act', 'copy', dstT[:, 0:4, :].rearrange("p a b -> p (a b)"), T0[:, :], R=['bk0'], W=[dst_name + 'a'])
            I('dve', 'tensor_copy', dstT[:, 4:8, :].rearrange("p a b -> p (a b)"), T1[:, :], R=['bk1'], W=[dst_name + 'b'])

        if attn:
            I('act', 'activation', out=esink[:], in_=prm[:, P_SK:P_SK + 8], func=AF.Exp, R=['prm'], W=['esink'])
            mark0 = S.aoff
            relaug = S.av([33, 8])
            oht = S.av([33, 768])
            tt = S.av([8, 768])
            S.dma('sp', relaug[0:32, :], relb_d, writes=['relaug'])
            I('dve', 'memset', relaug[32:33, :], 1.0, W=['relaug1'])
            S.dma('sp', oht, oht_d, writes=['oht'])
            I('pe', 'matmul', P0[0:8, 0:384], relaug, oht[:, 0:384], start=True, stop=True,
              R=['relaug', 'relaug1', 'oht'], W=['bk2'])
            I('pe', 'matmul', T2[0:8, 0:384], relaug, oht[:, 384:768], start=True, stop=True,
              R=['relaug', 'relaug1', 'oht'], W=['bk3'])
            I('act', 'activation', out=tt[:, 0:384], in_=P0[0:8, 0:384], func=AF.Copy, scale=8.0, R=['bk2'], W=['tt0'])
            I('act', 'activation', out=tt[:, 384:768], in_=T2[0:8, 0:384], func=AF.Copy, scale=8.0, R=['bk3'], W=['tt1'])
            S.dma('sp', scr.ap(), tt, reads=['tt0', 'tt1'], writes=['scr'])
            S.barrier()

            for p in range(npass if stage >= 1 else 0):
                S.aoff = mark0
                KTa = S.av([128, 4096])
                KTb = S.av([128, 4096])
                QTa = S.av([128, 2048])
                QTb = S.av([128, 2048])
                Va = S.av([128, 32, 65])
                Vb = S.av([128, 32, 2, 65])
                biasT = S.av([128, 16, 32, 2])
                R_ = S.av([128, 6, 128])
                fz = S.av([128, 32, 2])
                markp = S.aoff
                for e_ in range(2):
                    for c in range(3):
                        hk = bass.AP(tensor=scr.ap().tensor, offset=(2 * p + e_) * 768 + c * 256,
                                     ap=[[1, 128], [1, 128]])
                        S.dma('sp', R_[:, e_ * 3 + c, :], hk, reads=['scr'], writes=[('R', e_, c)])
                I('pool', 'memset', Va[:, :, 64:65], 1.0, W=['Va1'])
                I('pool', 'memset', Vb[:, :, :, 64:65], 1.0, W=['Vb1'])

                def load_w(w_dram, ncols, name):
                    w_sb = S.av([128, 8, ncols])
                    S.dma('sp', w_sb, w_dram.rearrange("(c p) n -> p c n", p=128), writes=[name])
                    for c in range(8):
                        I('pool', 'tensor_scalar', out=w_sb[:, c, :], in0=w_sb[:, c, :],
                          scalar1=prm[:, P_GM + c:P_GM + c + 1], scalar2=None, op0=ALU.mult,
                          R=[name, 'prm'], W=[name])
                    return w_sb

                def tile_phase(src_dram, ntiles, w_sb, wname, ncols, gcol0, dstTa, dstTb, dna, dnb, kv, p=p,
                               Va=Va, Vb=Vb, fz=fz):
                    xt = [S.av([128, D]) for _ in range(2)]
                    xT = [S.av([128, 8, 128]) for _ in range(2)]
                    junk = S.av([128, D], BF16)
                    kq = S.av([128, 256])
                    sq = S.av([128, 256])
                    kn = S.av([128, 256])
                    ssq = [S.av([128, 1]) for _ in range(2)]
                    std = [S.av([128, 1]) for _ in range(2)]
                    rstd = [S.av([128, 1]) for _ in range(2)]
                    ssq4 = S.av([128, 4]); std4 = S.av([128, 4]); r4 = S.av([128, 4])
                    for j in range(ntiles):
                        sl = j % 2
                        xtn = ('xt', sl)
                        S.dma('sp', xt[sl], src_dram[j * 128:(j + 1) * 128, :], writes=[xtn])
                        rms_rstd(xt[sl], xtn, ssq[sl], std[sl], rstd[sl], junk, 'A%d' % sl, D)
                        transpose8(xt[sl], xtn, xT[sl], 'xT%d' % sl)
                        for c in range(8):
                            I('pe', 'matmul', P0[:, 0:ncols], xT[sl][:, c, :], w_sb[:, c, :], start=(c == 0), stop=(c == 7),
                              R=['xT%da' % sl, 'xT%db' % sl, wname], W=['bk2'])
                        rs = rstd[sl][:, 0:1]
                        rsn = 'rstdA%d' % sl
                        I('act', 'activation', out=kq, in_=P0[:, 0:256], func=AF.Copy, scale=rs, R=['bk2', rsn], W=['kq'])
                        if kv:
                            I('act', 'activation', out=Va[:, j, 0:64], in_=P0[:, 256:320], func=AF.Copy, scale=rs,
                              R=['bk2', rsn, 'Va1'], W=[('Va', j)])
                            I('act', 'activation', out=Vb[:, j, :, 0:64],
                              in_=P0[:, 320:448].rearrange("p (a b) -> p a b", a=2), func=AF.Copy, scale=rs,
                              R=['bk2', rsn, 'Vb1'], W=[('Vb', j)])
                            I('dve', 'scalar_tensor_tensor', out=fz[:, j, :], in0=P0[:, 448:450], scalar=rs,
                              in1=prm[:, P_BF + 2 * p:P_BF + 2 * p + 2], op0=ALU.mult, op1=ALU.add,
                              R=['bk2', rsn, 'prm'], W=[('fz', j)])
                        I('dve', 'tensor_tensor', out=sq, in0=kq, in1=kq, op=ALU.mult, R=['kq'], W=['sq'])
                        I('dve', 'tensor_reduce', out=ssq4, in_=sq.rearrange("p (a b) -> p a b", a=4), axis=AX.X, op=ALU.add,
                          R=['sq'], W=['ssq4'])
                        I('act', 'activation', out=std4, in_=ssq4, func=AF.Sqrt, bias=EPS, scale=1.0 / 64, R=['ssq4'], W=['std4'])
                        I('dve', 'reciprocal', r4, std4, R=['std4'], W=['r4'])
                        I('dve', 'tensor_tensor', out=kn.rearrange("p (a b) -> p a b", a=4),
                          in0=kq.rearrange("p (a b) -> p a b", a=4), in1=r4.unsqueeze(2).to_broadcast([128, 4, 64]),
                          op=ALU.mult, R=['kq', 'r4'], W=['kn'])
                        I('pe', 'transpose', T2[:, 0:128], kn[:, 0:128], ident, R=['kn', 'cst'], W=['bk3'])
                        I('pe', 'transpose', T2[:, 128:256], kn[:, 128:256], ident, R=['kn', 'cst'], W=['bk3'])
                        I('act', 'activation', out=dstTa[:, j * 128:(j + 1) * 128], in_=T2[:, 0:128], func=AF.Copy,
                          scale=prm[:, gcol0:gcol0 + 1], R=['bk3', 'prm'], W=[(dna, j)])
                        I('act', 'activation', out=dstTb[:, j * 128:(j + 1) * 128], in_=T2[:, 128:256], func=AF.Copy,
                          scale=prm[:, gcol0 + 1:gcol0 + 2], R=['bk3', 'prm'], W=[(dnb, j)])

                w_sb = load_w(wkv_d[p], WKV, 'wkv')
                tile_phase(xf_d, nkb, w_sb, 'wkv', WKV, P_GC, KTa, KTb, 'KTa', 'KTb', True)
                nlf = S.av([128, 32, 2]); ex = S.av([128, 32, 2])
                fw = S.av([128, 160]); cpref = S.av([128, 32, 2]); gall = S.av([128, 32, 2]); gc = S.av([128, 16, 2])
                fzn = [('fz', j) for j in range(nkb)]
                if nkb < 32:
                    I('dve', 'memset', fz[:, nkb:32, :], 1.0, W=['fzpad'])
                    fzn.append('fzpad')
                I('act', 'activation', out=ex, in_=fz, func=AF.Exp, scale=-1.0, R=fzn, W=['ex'])
                I('act', 'activation', out=nlf, in_=ex, func=AF.Ln, bias=1.0, scale=1.0, R=['ex'], W=['nlf'])
                for j in range(32):
                    I('pe', 'matmul', P0[:, 2 * j:2 * j + 2], tri, nlf[:, j, :], start=True, stop=True,
                      R=['nlf', 'cst'], W=['bk2'])
                    I('pe', 'matmul', P0[:, 64 + 2 * j:64 + 2 * j + 2], ones, nlf[:, j, :], start=True, stop=True,
                      R=['nlf', 'cst'], W=['bk2'])
                for m in range(16):
                    I('pe', 'matmul', P0[:, 128 + 2 * m:128 + 2 * m + 2], mab[:, 0, :], nlf[:, 2 * m, :],
                      start=True, stop=False, R=['nlf', 'ccst'], W=['bk2'])
                    I('pe', 'matmul', P0[:, 128 + 2 * m:128 + 2 * m + 2], mab[:, 1, :], nlf[:, 2 * m + 1, :],
                      start=False, stop=True, R=['nlf', 'ccst'], W=['bk2'])
                I('dve', 'tensor_copy', fw, P0[:, 0:160], R=['bk2'], W=['fw'])
                Wv = fw[:, 0:64].rearrange("p (a b) -> p a b", a=32)
                BS = fw[:, 64:128].rearrange("p (a b) -> p a b", a=32)
                PM = fw[:, 128:160].rearrange("p (a b) -> p a b", a=16)
                I('dve', 'memset', cpref[:, 0, :], 0.0, W=['cpref'])
                for j in range(1, 32):
                    I('dve', 'tensor_tensor', out=cpref[:, j, :], in0=cpref[:, j - 1, :], in1=BS[:, j - 1, :], op=ALU.add,
                      R=['cpref', 'fw'], W=['cpref'])
                I('dve', 'tensor_tensor', out=gall, in0=Wv, in1=cpref, op=ALU.add, R=['fw', 'cpref'], W=['gall'])
                I('dve', 'tensor_tensor', out=gc, in0=cpref[:, 0:32:2, :], in1=PM, op=ALU.add, R=['fw', 'cpref'], W=['gc'])
                for m in range(16):
                    nj = 2 * m + 2
                    I('dve', 'tensor_tensor', out=biasT[:, m, 0:nj, :], in0=gall[:, 0:nj, :],
                      in1=gc[:, m, :].unsqueeze(1).to_broadcast([128, nj, 2]), op=ALU.subtract,
                      R=['gall', 'gc'], W=['biasT'])
                S.barrier()
                if stage < 2:
                    continue
                S.aoff = markp
                w_sb = load_w(wq_d[p], 256, 'wq')
                tile_phase(xo_d, 16, w_sb, 'wq', 256, P_GC + 2, QTa, QTb, 'QTa', 'QTb', False)
                S.barrier()
                if stage < 3:
                    continue
                S.aoff = markp
                wo_sb = S.av([128, 2, D])
                S.dma('sp', wo_sb, wo_d[p].rearrange("(r p) n -> p r n", p=128), writes=['wo'])
                PT = [S.av([128, 4, 128]) for _ in range(2)]
                PT2 = S.av([128, 3, 128])
                mixp = S.av([128, 256])
                mixT = S.av([128, 2, 128])
                rinv = S.av([128, 4])
                den = S.av([128, 2])
                O4 = OB[:, 0:260].rearrange("p (a b) -> p a b", a=4)
                sbanks = [SA, SB]
                gi = 0
                for m in range(16):
                    qs = slice(m * 128, (m + 1) * 128)
                    for e_ in range(2):
                        pr = slice(64 * e_, 64 * e_ + 64)
                        nj = 2 * m + 2
                        groups = [list(range(g0, min(g0 + 4, nj))) for g0 in range(0, nj, 4)]

                        def scores(grp, sl):
                            bk = sbanks[sl]
                            for jj, j in enumerate(grp):
                                diag = j >= 2 * m
                                I('pe', 'matmul', bk[:, jj * 128:(jj + 1) * 128], KTb[pr, j * 128:(j + 1) * 128], QTb[pr, qs],
                                  start=True, stop=(not diag), R=[('KTb', j), ('QTb', m)], W=[BN[id(bk)]])
                                if diag:
                                    I('pe', 'matmul', bk[:, jj * 128:(jj + 1) * 128], ident, foxmask[:, j - 2 * m, :],
                                      start=False, stop=True, R=['cst', 'ccst'], W=[BN[id(bk)]])

                        def exps(grp, sl):
                            bk = sbanks[sl]
                            for jj, j in enumerate(grp):
                                I('act', 'activation', out=PT[sl][:, jj, :], in_=bk[:, jj * 128:(jj + 1) * 128], func=AF.Exp,
                                  bias=biasT[:, m, j, e_:e_ + 1], scale=0.125,
                                  R=[BN[id(bk)], 'biasT'], W=[('PT', sl, jj)])

                        def pvs(grp, sl):
                            for jj, j in enumerate(grp):
                                I('pe', 'matmul', O4[:, e_, :], PT[sl][:, jj, :], Vb[:, j, e_, :],
                                  start=(j == 0), stop=(j == nj - 1),
                                  R=[('PT', sl, jj), ('Vb', j), 'Vb1'], W=['bk7'])
                        prev = None
                        for grp in groups:
                            sl = gi % 2
                            gi += 1
                            scores(grp, sl)
                            exps(grp, sl)
                            if prev is not None:
                                pvs(*prev)
                            prev = (grp, sl)
                        pvs(*prev)
                    for e_ in range(2):
                        pr = slice(64 * e_, 64 * e_ + 64)
                        cs = [c for c in range(3) if 0 <= 2 * m - 1 + c < 32]
                        for c in cs:
                            j = 2 * m - 1 + c
                            I('pe', 'matmul', S2[:, c * 128:(c + 1) * 128], KTa[pr, j * 128:(j + 1) * 128], QTa[pr, qs],
                              start=True, stop=False, R=[('KTa', j), ('QTa', m)], W=['bk6'])
                            I('pe', 'matmul', S2[:, c * 128:(c + 1) * 128], jmat, R_[:, e_ * 3 + c, :],
                              start=False, stop=True, R=['cst', ('R', e_, c)], W=['bk6'])
                        c0 = cs[0]
                        I('act', 'activation', out=PT2[:, c0:3, :].rearrange("p a b -> p (a b)"), in_=S2[:, c0 * 128:384],
                          func=AF.Exp, scale=0.125, R=['bk6'], W=['PT2'])
                        for c in cs:
                            j = 2 * m - 1 + c
                            I('pe', 'matmul', O4[:, 2 + e_, :], PT2[:, c, :], Va[:, j, :], start=(c == c0), stop=(c == 2),
                              R=['PT2', ('Va', j), 'Va1'], W=['bk7'])
                    for e_ in range(2):
                        I('dve', 'reciprocal', rinv[:, e_:e_ + 1], O4[:, e_, 64:65], R=['bk7'], W=[('rinv', e_)])
                        I('dve', 'tensor_scalar', out=mixp[:, 128 + 64 * e_:128 + 64 * e_ + 64], in0=O4[:, e_, 0:64],
                          scalar1=rinv[:, e_:e_ + 1], scalar2=None, op0=ALU.mult,
                          R=['bk7', ('rinv', e_)], W=[('mixp', 2 + e_)])
                        I('dve', 'tensor_tensor', out=den[:, e_:e_ + 1], in0=O4[:, 2 + e_, 64:65],
                          in1=esink[:, 2 * p + e_:2 * p + e_ + 1], op=ALU.add, R=['bk7', 'esink'], W=[('den', e_)])
                        I('dve', 'reciprocal', rinv[:, 2 + e_:3 + e_], den[:, e_:e_ + 1], R=[('den', e_)], W=[('rinv', 2 + e_)])
                        I('dve', 'tensor_scalar', out=mixp[:, 64 * e_:64 * e_ + 64], in0=O4[:, 2 + e_, 0:64],
                          scalar1=rinv[:, 2 + e_:3 + e_], scalar2=None, op0=ALU.mult,
                          R=['bk7', ('rinv', 2 + e_)], W=[('mixp', e_)])
                    mixn = [('mixp', k) for k in range(4)]
                    I('pe', 'transpose', T2[:, 0:128], mixp[:, 0:128], ident, R=mixn + ['cst'], W=['bk3'])
                    I('pe', 'transpose', T2[:, 128:256], mixp[:, 128:256], ident, R=mixn + ['cst'], W=['bk3'])
                    I('act', 'copy', mixT.rearrange("p a b -> p (a b)"), T2[:, 0:256], R=['bk3'], W=['mixT'])
                    for hf, bk in ((0, T0), (1, T1)):
                        for r in range(2):
                            I('pe', 'matmul', bk[:, :], mixT[:, r, :], wo_sb[:, r, hf * 512:(hf + 1) * 512],
                              start=(r == 0), stop=(r == 1), R=['mixT', 'wo'], W=[BN[id(bk)]])
                        I('dve', 'tensor_tensor', out=x2[:, m, hf * 512:(hf + 1) * 512], in0=x2[:, m, hf * 512:(hf + 1) * 512],
                          in1=bk[:, :], op=ALU.add, R=[BN[id(bk)], ('x2', m)], W=[('x2', m)])
                S.barrier()
            S.aoff = mark0

        if dbg:
            for m in range(16):
                S.dma('sp', dbg_d[m * 128:(m + 1) * 128, :], x2[:, m, :], reads=[('x2', m)], out=True, sem='d_dbg%d' % m)
            S.barrier()

        if peer:
            S.aoff = 0
            eidx = S.av([128, 16, 128], I32)
            gate = S.av([128, 16, 128])
            gf = S.av([128, D])
            xn = S.av([128, D])
            junk = S.av([128, D], BF16)
            ssq = S.av([128, 1]); std = S.av([128, 1]); rstd = S.av([128, 1])
            mark1 = S.aoff
            S.dma('sp', gf, gf_d, writes=['gf'])

            def norm_tile(t):
                rms_rstd(x2[:, t, :], ('x2', t), ssq, std, rstd, junk, 'P', D)
                I('dve', 'scalar_tensor_tensor', out=xn, in0=x2[:, t, :], scalar=rstd[:, 0:1], in1=gf,
                  op0=ALU.mult, op1=ALU.mult, R=[('x2', t), 'rstdP', 'gf'], W=['xn'])

            skn = S.av([128, 2, 128]); skT = S.av([128, 2, 128])
            S.dma('sp', skn, sk_d.rearrange("a n d -> n a d"), writes=['skn'])
            for a in range(2):
                I('pe', 'transpose', T2[:, a * 128:(a + 1) * 128], skn[:, a, :], ident, R=['skn', 'cst'], W=['bk3'])
            I('act', 'copy', skT.rearrange("p a b -> p (a b)"), T2[:, 0:256], R=['bk3'], W=['skT'])
            wq_sb = S.av([128, 8, 1024])
            xnT = S.av([128, 8, 128])
            qvT = S.av([128, 8, 128])
            s = S.av([128, 8, 128]); s2 = S.av([128, 8, 128])
            vals = S.av([128, 8, 16]); idxu = S.av([128, 8, 16], U32); idxf = S.av([128, 8, 16])
            cand = S.av([128, 4, 256]); cand2 = S.av([128, 4, 256])
            ts = S.av([128, 4, 16]); pos = S.av([128, 4, 16], U32); posf = S.av([128, 4, 16])
            ge4 = S.av([128, 4, 16, 16]); oh4 = S.av([128, 4, 16, 16])
            af = S.av([128, 4, 16]); bfv = S.av([128, 4, 16]); sel1 = S.av([128, 4, 16]); sel2 = S.av([128, 4, 16])
            eidf = S.av([128, 4, 16]); tsm = S.av([128, 4, 16]); gex = S.av([128, 4, 16])
            gs = S.av([128, 4]); rg = S.av([128, 4])
            v4 = vals.rearrange("p (h a) k -> p h a k", a=2)
            i4 = idxf.rearrange("p (h a) k -> p h a k", a=2)
            qbanks = [P0, SA]
            B4 = [128, 4, 16, 16]
            for hh in range(2):
                for c in range(8):
                    S.dma('sp', wq_sb[:, c, :], wqy_d[c * 128:(c + 1) * 128, hh * 1024:(hh + 1) * 1024], writes=[('wqy', c)])
                for t in range(NT):
                    norm_tile(t)
                    transpose8(xn, 'xn', xnT, 'xnT')
                    for g in range(2):
                        bk = qbanks[g]
                        for cq in range(4):
                            cc = 4 * g + cq
                            for c in range(8):
                                I('pe', 'matmul', bk[:, cq * 128:(cq + 1) * 128], wq_sb[:, c, cc * 128:(cc + 1) * 128],
                                  xnT[:, c, :], start=(c == 0), stop=(c == 7),
                                  R=[('wqy', c), 'xnTa', 'xnTb'], W=[BN[id(bk)]])
                        I('act', 'copy', qvT[:, 4 * g:4 * g + 4, :].rearrange("p a b -> p (a b)"), bk[:, :],
                          R=[BN[id(bk)]], W=[('qvT', g)])
                    for g in range(2):
                        bk = [S2, OB][g]
                        for cq in range(4):
                            cc = 4 * g + cq
                            I('pe', 'matmul', bk[:, cq * 128:(cq + 1) * 128], qvT[:, cc, :], skT[:, cc % 2, :],
                              start=True, stop=True, R=[('qvT', g), 'skT'], W=[BN[id(bk)]])
                        I('act', 'copy', s[:, 4 * g:4 * g + 4, :].rearrange("p a b -> p (a b)"), bk[:, :],
                          R=[BN[id(bk)]], W=[('s', g)])
                    for cc in range(8):
                        I('dve', 'max', vals[:, cc, 0:8], s[:, cc, :], R=[('s', cc // 4)], W=[('v', cc, 0)])
                    for cc in range(8):
                        I('dve', 'max_index', idxu[:, cc, 0:8], vals[:, cc, 0:8], s[:, cc, :],
                          R=[('s', cc // 4), ('v', cc, 0)], W=[('i', cc, 0)])
                    for cc in range(8):
                        I('dve', 'match_replace', s2[:, cc, :], vals[:, cc, 0:8], s[:, cc, :], -1e30,
                          R=[('s', cc // 4), ('v', cc, 0)], W=[('s2', cc)])
                    for cc in range(8):
                        I('dve', 'max', vals[:, cc, 8:16], s2[:, cc, :], R=[('s2', cc)], W=[('v', cc, 1)])
                    for cc in range(8):
                        I('dve', 'max_index', idxu[:, cc, 8:16], vals[:, cc, 8:16], s2[:, cc, :],
                          R=[('s2', cc), ('v', cc, 1)], W=[('i', cc, 1)])
                    alli = [('i', cc, k) for cc in range(8) for k in range(2)]
                    allv = [('v', cc, k) for cc in range(8) for k in range(2)]
                    I('dve', 'tensor_copy', idxf, idxu, R=alli, W=['idxf'])
                    I('dve', 'tensor_tensor', out=cand.rearrange("p h (a b) -> p h a b", a=16),
                      in0=v4[:, :, 0, :].unsqueeze(3).to_broadcast(B4), in1=v4[:, :, 1, :].unsqueeze(2).to_broadcast(B4),
                      op=ALU.add, R=allv, W=['cand'])
                    for h in range(4):
                        I('dve', 'max', ts[:, h, 0:8], cand[:, h, :], R=['cand'], W=[('ts', h, 0)])
                    for h in range(4):
                        I('dve', 'max_index', pos[:, h, 0:8], ts[:, h, 0:8], cand[:, h, :],
                          R=['cand', ('ts', h, 0)], W=[('pos', h, 0)])
                    for h in range(4):
                        I('dve', 'match_replace', cand2[:, h, :], ts[:, h, 0:8], cand[:, h, :], -1e30,
                          R=['cand', ('ts', h, 0)], W=[('cand2', h)])
                    for h in range(4):
                        I('dve', 'max', ts[:, h, 8:16], cand2[:, h, :], R=[('cand2', h)], W=[('ts', h, 1)])
                    for h in range(4):
                        I('dve', 'max_index', pos[:, h, 8:16], ts[:, h, 8:16], cand2[:, h, :],
                          R=[('cand2', h), ('ts', h, 1)], W=[('pos', h, 1)])
                    allp = [('pos', h, k) for h in range(4) for k in range(2)]
                    allt = [('ts', h, k) for h in range(4) for k in range(2)]
                    I('dve', 'tensor_copy', posf, pos, R=allp, W=['posf'])
                    I('dve', 'tensor_tensor', out=ge4, in0=posf.unsqueeze(3).to_broadcast(B4),
                      in1=iota16.unsqueeze(1).unsqueeze(1).to_broadcast(B4), op=ALU.is_ge, R=['posf', 'cst'], W=['ge4'])
                    I('dve', 'tensor_reduce', out=af, in_=ge4, axis=AX.X, op=ALU.add, R=['ge4'], W=['af'])
                    I('dve', 'tensor_scalar', out=af, in0=af, scalar1=-1.0, scalar2=None, op0=ALU.add, R=['af'], W=['af'])
                    I('dve', 'scalar_tensor_tensor', out=bfv, in0=af, scalar=-16.0, in1=posf, op0=ALU.mult, op1=ALU.add,
                      R=['af', 'posf'], W=['bfv'])
                    for (src, ii, dst, nm) in ((af, 0, sel1, 'sel1'), (bfv, 1, sel2, 'sel2')):
                        I('dve', 'tensor_tensor', out=oh4, in0=iota.unsqueeze(1).unsqueeze(1).to_broadcast(B4),
                          in1=src.unsqueeze(3).to_broadcast(B4), op=ALU.is_equal, R=['af', 'bfv', 'cst'], W=['oh4'])
                        I('dve', 'tensor_tensor', out=oh4, in0=oh4, in1=i4[:, :, ii, :].unsqueeze(2).to_broadcast(B4),
                          op=ALU.mult, R=['oh4', 'idxf'], W=['oh4'])
                        I('dve', 'tensor_reduce', out=dst, in_=oh4, axis=AX.X, op=ALU.add, R=['oh4'], W=[nm])
                    I('dve', 'scalar_tensor_tensor', out=eidf, in0=sel1, scalar=128.0, in1=sel2, op0=ALU.mult, op1=ALU.add,
                      R=['sel1', 'sel2'], W=['eidf'])
                    I('dve', 'tensor_copy', eidx[:, t, hh * 64:(hh + 1) * 64], eidf.rearrange("p a b -> p (a b)"),
                      R=['eidf'], W=[('eidx', t, hh)])
                    I('dve', 'tensor_tensor', out=tsm, in0=ts, in1=ts[:, :, 0:1].to_broadcast([128, 4, 16]), op=ALU.subtract,
                      R=allt, W=['tsm'])
                    I('act', 'activation', out=gex, in_=tsm, func=AF.Exp, R=['tsm'], W=['gex'])
                    I('dve', 'tensor_reduce', out=gs, in_=gex, axis=AX.X, op=ALU.add, R=['gex'], W=['gs'])
                    I('dve', 'reciprocal', rg, gs, R=['gs'], W=['rg'])
                    I('dve', 'tensor_tensor', out=gate[:, t, hh * 64:(hh + 1) * 64].rearrange("p (a b) -> p a b", a=4),
                      in0=gex, in1=rg.unsqueeze(2).to_broadcast([128, 4, 16]), op=ALU.mult,
                      R=['gex', 'rg'], W=[('gate', t, hh)])
            S.barrier()
            S.aoff = mark1
            ring = [S.av([128, D]) for _ in range(NGB)]
            junk2 = S.av([128, D])
            acc1 = S.av([128, D])
            pre = S.av([128, 128]); gl = S.av([128, 128]); coef = S.av([128, 128])
            gcount = 0
            for t in range(NT):
                norm_tile(t)
                en = [('eidx', t, 0), ('eidx', t, 1)]
                for sl_ in range(128):
                    r = gcount % NGB
                    gcount += 1
                    S.dma('pool', ring[r], eu_d, reads=en, writes=[('ring', r)],
                          indirect=bass.IndirectOffsetOnAxis(ap=eidx[:, t, sl_:sl_ + 1], axis=0))
                    I('dve', 'scalar_tensor_tensor', out=junk2, in0=ring[r], scalar=1.0, in1=xn, op0=ALU.mult, op1=ALU.mult,
                      accum_out=pre[:, sl_:sl_ + 1], R=[('ring', r), 'xn'], W=[('pre', sl_)])
                I('act', 'activation', out=gl, in_=pre, func=AF.Gelu, R=[('pre', k) for k in range(128)], W=['gl'])
                I('dve', 'tensor_tensor', out=coef, in0=gl, in1=gate[:, t, :], op=ALU.mult,
                  R=['gl', ('gate', t, 0), ('gate', t, 1)], W=['coef'])
                for sl_ in range(128):
                    r = gcount % NGB
                    gcount += 1
                    S.dma('pool', ring[r], ev_d, reads=en, writes=[('ring', r)],
                          indirect=bass.IndirectOffsetOnAxis(ap=eidx[:, t, sl_:sl_ + 1], axis=0))
                    if sl_ == 1:
                        I('dve', 'tensor_scalar', out=acc1, in0=ring[r], scalar1=coef[:, sl_:sl_ + 1], scalar2=None,
                          op0=ALU.mult, R=[('ring', r), 'coef'], W=['acc1'])
                    elif sl_ % 2 == 1:
                        I('dve', 'scalar_tensor_tensor', out=acc1, in0=ring[r], scalar=coef[:, sl_:sl_ + 1], in1=acc1,
                          op0=ALU.mult, op1=ALU.add, R=[('ring', r), 'coef', 'acc1'], W=['acc1'])
                    else:
                        I('dve', 'scalar_tensor_tensor', out=x2[:, t, :], in0=ring[r], scalar=coef[:, sl_:sl_ + 1],
                          in1=x2[:, t, :], op0=ALU.mult, op1=ALU.add, R=[('ring', r), 'coef', ('x2', t)], W=[('x2', t)])
                I('dve', 'tensor_tensor', out=x2[:, t, :], in0=x2[:, t, :], in1=acc1, op=ALU.add,
                  R=['acc1', ('x2', t)], W=[('x2', t)])
                S.dma('sp', y_d[t * 128:(t + 1) * 128, :], x2[:, t, :], reads=[('x2', t)], out=True, sem='d_y%d' % t)
        else:
            for t in range(NT):
                S.dma('sp', y_d[t * 128:(t + 1) * 128, :], x2[:, t, :], reads=[('x2', t)], out=True, sem='d_y%d' % t)
        S.finish()
    return nc


def _t5_bucket_table():
    d = np.arange(256)
    nf = np.maximum(d, 1).astype(np.float32)
    large = 16 + (np.log(nf / np.float32(16)) / np.float32(math.log(128 / 16)) * np.float32(16)).astype(np.int32)
    large = np.minimum(large, 31)
    return np.where(d < 16, d, large)


def _consts():
    cst = np.zeros((128, CSTW), np.float32)
    cst[:, C_ID:C_ID + 128] = np.eye(128)
    cst[:, C_J:C_J + 128] = np.eye(128)[::-1]
    cst[:, C_TRI:C_TRI + 128] = np.triu(np.ones((128, 128)))
    cst[:, C_ONE:C_ONE + 128] = 1.0
    cst[:, C_IOTA:C_IOTA + 16] = np.arange(16)
    cst[:, C_IOTA16:C_IOTA16 + 16] = 16 * np.arange(16)
    return cst


def _core_consts(h):
    cc = np.zeros((128, 512), np.float32)
    k = np.arange(128)[:, None]
    q = np.arange(128)[None, :]
    trimask = np.where(k > q, 8 * NEGM, 0.0).astype(np.float32)
    full = np.full((128, 128), 8 * NEGM, np.float32)
    upto = np.zeros((128, 128), np.float32)
    upto[:64, :] = 1.0
    if h == 0:
        cc[:, 0:128] = trimask
        cc[:, 128:256] = full
        cc[:, 256:384] = upto
        cc[:, 384:512] = 0.0
    else:
        cc[:, 0:128] = 0.0
        cc[:, 128:256] = trimask
        cc[:, 256:384] = 1.0
        cc[:, 384:512] = upto
    bt = _t5_bucket_table()
    oh = np.zeros((33, 768), np.float32)
    types = ['prev', 'cur', 'mask'] if h == 0 else ['mask', 'prev', 'cur']
    for c, ty in enumerate(types):
        for mp in range(256):
            col = c * 256 + mp
            if ty == 'prev':
                dist = mp + 1
            elif ty == 'cur':
                dist = mp - 127
            else:
                dist = -1
            if 0 <= dist < 128 and mp <= 254:
                oh[bt[dist], col] = 1.0
            else:
                oh[32, col] = NEGM
    return cc, oh


_NC_CACHE = {}


def prepare(inputs):
    x = np.asarray(inputs["x"], np.float32)
    w_in = np.asarray(inputs["w_in"], np.float32)[0]
    w_out = np.asarray(inputs["w_out"], np.float32)[0]
    rep = lambda v, n=128: np.ascontiguousarray(np.broadcast_to(np.asarray(v, np.float32).reshape(1, -1), (n, np.size(v))))
    wq4 = np.zeros((4, D, 256), np.float32)
    wkv4 = np.zeros((4, D, WKV), np.float32)
    wo4 = np.zeros((4, 256, D), np.float32)
    for p in range(4):
        kvh = p // 2
        qa = w_in[:, 128 * p:128 * p + 128]
        qb = w_in[:, 768 + 128 * p:768 + 128 * p + 128]
        ka = w_in[:, 512 + 64 * kvh:512 + 64 * kvh + 64]
        va = w_in[:, 640 + 64 * kvh:640 + 64 * kvh + 64]
        kb = w_in[:, 1280 + 128 * p:1280 + 128 * p + 128]
        vb = w_in[:, 1792 + 128 * p:1792 + 128 * p + 128]
        fb = w_in[:, 2304 + 2 * p:2304 + 2 * p + 2]
        wq4[p] = np.concatenate([qa, qb], axis=1)
        wkv4[p] = np.concatenate([ka, ka, kb, va, vb, fb], axis=1)
        wo4[p] = np.concatenate([w_out[128 * p:128 * p + 128], w_out[512 + 128 * p:512 + 128 * p + 128]], axis=0)
    prm = np.zeros((128, 32), np.float32)
    prm[:, P_GM:P_GM + 8] = np.asarray(inputs["norm_mix"], np.float32)[0].reshape(8, 128).T
    dup = lambda v: np.concatenate([np.asarray(v, np.float32)[0], np.asarray(v, np.float32)[0]])
    prm[:, P_GC + 0] = dup(inputs["k_norm_a"])
    prm[:, P_GC + 1] = dup(inputs["k_norm_b"])
    prm[:, P_GC + 2] = dup(inputs["q_norm_a"])
    prm[:, P_GC + 3] = dup(inputs["q_norm_b"])
    prm[:, P_BF:P_BF + 8] = rep(inputs["b_forget"])
    prm[:, P_SK:P_SK + 8] = rep(inputs["sinks"])
    cst = _consts()
    shared = {
        "cst": cst, "prm": prm, "wq4": wq4, "wkv4": wkv4, "wo4": wo4,
        "relb": np.asarray(inputs["rel_bias"], np.float32),
        "wqy": np.asarray(inputs["w_query"], np.float32)[0],
        "sk": np.stack([np.asarray(inputs["sub_keys1"], np.float32)[0], np.asarray(inputs["sub_keys2"], np.float32)[0]]),
        "eu": np.asarray(inputs["expert_u"], np.float32)[0],
        "ev": np.asarray(inputs["expert_v"], np.float32)[0],
        "gf": rep(inputs["norm_ffn"]),
    }
    in_maps = []
    for c in range(8):
        b, h = c // 2, c % 2
        xb = x[b]
        xo = np.ascontiguousarray(xb.reshape(16, 2, 128, D)[:, h].reshape(2048, D))
        cc, oh = _core_consts(h)
        m = dict(shared)
        m.update({"xo": xo, "xf": xb, "ccst": cc, "ohtab": oh})
        in_maps.append(m)
    return in_maps


def kernel(**inputs):
    in_maps = prepare(inputs)
    if "nc" not in _NC_CACHE:
        _NC_CACHE["nc"] = build_nc()
    res = run_bass_kernel_spmd(_NC_CACHE["nc"], in_maps, core_ids=list(range(8)))
    out = np.zeros((4, 4096, D), np.float32)
    for c in range(8):
        b, h = c // 2, c % 2
        out[b].reshape(16, 2, 128, D)[:, h] = res.results[c]["y"].reshape(16, 128, D)
    return out
```

```python
import math
import numpy as np
import concourse.bass as bass
import concourse.mybir as mybir
from concourse.bass_utils import run_bass_kernel_spmd
from contextlib import ExitStack

F32 = mybir.dt.float32
BF16 = mybir.dt.bfloat16
U32 = mybir.dt.uint32
I32 = mybir.dt.int32
ALU = mybir.AluOpType
AF = mybir.ActivationFunctionType
AX = mybir.AxisListType

D = 1024
EPS = 1e-6
NEGM = -30000.0


class Sched:
    ENGS = ('pe', 'dve', 'act', 'pool', 'sp')

    def __init__(self, nc, es):
        self.nc, self.es = nc, es
        self.q = {e: [] for e in self.ENGS}
        self.cnt = {e: 0 for e in ('pe', 'dve', 'act', 'pool')}
        self.sems = {}
        for e in self.cnt:
            self.sems['c_' + e] = es.enter_context(nc.semaphore('c_' + e))
        self.waited = {e: {} for e in self.ENGS}
        self.lw = {}
        self.rd = {}
        self.dcnt = {}
        self.outdeps = {}
        self.arena = None
        self.aoff = 0
        self.AW = 0

    def sb(self, name, shape, dt=F32):
        return self.es.enter_context(self.nc.sbuf_tensor("s_" + name, list(shape), dt))

    def ps(self, name, shape=(128, 512), dt=F32):
        return self.es.enter_context(self.nc.psum_tensor("p_" + name, list(shape), dt))

    def make_arena(self, words):
        self.arena = self.sb("arena", [128, words])
        self.AW = words
        self.aoff = 0

    def av(self, shape, dt=F32):
        n = 1
        for v in shape[1:]:
            n *= v
        if dt == BF16:
            n = (n + 1) // 2
        off = self.aoff
        self.aoff += n
        assert self.aoff <= self.AW, ("arena overflow", self.aoff, self.AW)
        ap = self.arena[0:shape[0], off:off + n]
        if dt != F32:
            ap = ap.bitcast(dt)
        if len(shape) == 3:
            ap = ap.rearrange("p (a b) -> p a b", a=shape[1])
        elif len(shape) == 4:
            ap = ap.rearrange("p (a b c) -> p a b c", a=shape[1], b=shape[2])
        return ap

    def _deps(self, reads, writes):
        deps = {}

        def add(d):
            if d is not None and deps.get(d[0], 0) < d[1]:
                deps[d[0]] = d[1]
        for b in reads:
            add(self.lw.get(b))
        for b in writes:
            add(self.lw.get(b))
            for d in self.rd.get(b, ()):
                add(d)
        return deps

    def _emit_waits(self, eng, deps, skip_self=False):
        for s, v in deps.items():
            if skip_self and s == 'c_' + eng:
                continue
            if self.waited[eng].get(s, 0) < v:
                self.q[eng].append(('w', s, v))
                self.waited[eng][s] = v

    def _commit(self, tok, reads, writes):
        for b in writes:
            self.lw[b] = tok
            self.rd[b] = []
        for b in reads:
            if b not in writes:
                self.rd.setdefault(b, []).append(tok)

    def op(self, eng, fn, reads=(), writes=()):
        deps = self._deps(reads, writes)
        self._emit_waits(eng, deps, skip_self=(eng == 'pe'))
        self.cnt[eng] += 1
        tok = ('c_' + eng, self.cnt[eng])
        self.q[eng].append(('o', fn, 'c_' + eng, 1))
        self._commit(tok, reads, writes)

    def I(self, eng, meth, *args, R=(), W=(), **kw):
        self.op(eng, (meth, args, kw), R, W)

    def dma(self, eng, out_ap, in_ap, reads=(), writes=(), out=False, indirect=None, sem=None):
        if sem is None:
            sem = 'd_' + str(writes[0] if writes else reads[0]) + ('_o' if out else '')
        if sem not in self.sems:
            nm = ''.join(ch if ch.isalnum() else '_' for ch in sem)
            self.sems[sem] = self.es.enter_context(self.nc.semaphore(nm))
            self.dcnt[sem] = 0
        deps = self._deps(reads, writes)
        if self.dcnt[sem] > 0:
            v = 16 * self.dcnt[sem]
            if deps.get(sem, 0) < v:
                deps[sem] = v
        self._emit_waits(eng, deps)
        self.dcnt[sem] += 1
        tok = (sem, 16 * self.dcnt[sem])
        if indirect is not None:
            fn = lambda e: e.indirect_dma_start(out_ap, None, in_ap, indirect)
        else:
            fn = lambda e: e.dma_start(out=out_ap, in_=in_ap)
        self.q[eng].append(('o', fn, sem, 16))
        self._commit(tok, reads, writes)
        if out:
            self.outdeps[sem] = tok[1]

    def barrier(self):
        allv = {}
        for e, c in self.cnt.items():
            if c > 0:
                allv['c_' + e] = c
        for s, c in self.dcnt.items():
            if c > 0:
                allv[s] = 16 * c
        for eng in self.ENGS:
            self._emit_waits(eng, allv)
        self.lw.clear()
        self.rd.clear()

    def finish(self):
        self.barrier()
        nc = self.nc
        S = self

        def run(name, e):
            for it in S.q[name]:
                if it[0] == 'w':
                    e.wait_ge(S.sems[it[1]], it[2])
                else:
                    f = it[1]
                    ins = getattr(e, f[0])(*f[1], **f[2]) if isinstance(f, tuple) else f(e)
                    if isinstance(ins, (list, tuple)):
                        ins = ins[-1]
                    ins.then_inc(S.sems[it[2]], it[3])
        with nc.Block() as block:
            @block.tensor
            def _(e):
                run('pe', e)

            @block.vector
            def _(e):
                run('dve', e)

            @block.scalar
            def _(e):
                run('act', e)

            @block.gpsimd
            def _(e):
                run('pool', e)

            @block.sync
            def _(e):
                run('sp', e)


C_ID, C_J, C_TRI, C_ONE, C_IOTA, C_IOTA16 = 0, 128, 256, 384, 512, 528
CSTW = 544
P_GM, P_GC, P_BF, P_SK = 0, 8, 12, 20
WKV = 450
NGB = 12


def build_nc(NT=16, attn=True, peer=True, npass=4, nkb=32, dbg=False, stage=3, p2=True, nodve=False, ngb=NGB):
    nc = bass.Bass("TRN2", target_bir_lowering=False)
    es = ExitStack()
    with es:
        def din(name, shape, dt=F32):
            return nc.dram_tensor(name, list(shape), dt, kind="ExternalInput").ap()
        xo_d = din("xo", [2048, D])
        xf_d = din("xf", [4096, D])
        cst_d = din("cst", [128, CSTW])
        ccst_d = din("ccst", [128, 512])
        prm_d = din("prm", [128, 32])
        wq_d = din("wq4", [4, D, 256])
        wkv_d = din("wkv4", [4, D, WKV])
        wo_d = din("wo4", [4, 256, D])
        relb_d = din("relb", [32, 8])
        oht_d = din("ohtab", [33, 768])
        wqy_d = din("wqy", [D, 2048])
        sk_d = din("sk", [2, 128, 128])
        eu_d = din("eu", [16384, D])
        ev_d = din("ev", [16384, D])
        gf_d = din("gf", [128, D])
        y_d = nc.dram_tensor("y", [2048, D], F32, kind="ExternalOutput").ap()
        scr = nc.dram_tensor("scr", [8, 768], F32)
        if dbg:
            dbg_d = nc.dram_tensor("dbg", [2048, D], F32, kind="ExternalOutput").ap()

        S = Sched(nc, es)
        I = S.I
        x2 = S.sb("x2", [128, 16, D])
        cst = S.sb("cst", [128, CSTW])
        ccst = S.sb("ccst", [128, 512])
        prm = S.sb("prm", [128, 32])
        esink = S.sb("esink", [128, 8])
        S.make_arena(33400)
        banks = [S.ps("bk%d" % i) for i in range(8)]
        T0, T1, P0, T2, SA, SB, S2, OB = banks
        BN = {id(b): "bk%d" % i for i, b in enumerate(banks)}

        ident = cst[:, C_ID:C_ID + 128]
        jmat = cst[:, C_J:C_J + 128]
        tri = cst[:, C_TRI:C_TRI + 128]
        ones = cst[:, C_ONE:C_ONE + 128]
        iota = cst[:, C_IOTA:C_IOTA + 16]
        iota16 = cst[:, C_IOTA16:C_IOTA16 + 16]
        foxmask = ccst[:, 0:256].rearrange("p (a b) -> p a b", a=2)
        mab = ccst[:, 256:512].rearrange("p (a b) -> p a b", a=2)

        S.dma('sp', cst[:], cst_d, writes=['cst'])
        S.dma('sp', ccst[:], ccst_d, writes=['ccst'])
        S.dma('sp', prm[:], prm_d, writes=['prm'])
        for m in range(16):
            S.dma('sp', x2[:, m, :], xo_d[m * 128:(m + 1) * 128, :], writes=[('x2', m)])

        def rms_rstd(src_ap, src_name, ssq, std, rstd, junk, tag, n):
            I('act', 'activation', out=junk, in_=src_ap, func=AF.Square, accum_out=ssq,
              R=[src_name], W=['junk', 'ssq' + tag])
            I('act', 'activation', out=std, in_=ssq, func=AF.Sqrt, bias=EPS, scale=1.0 / n,
              R=['ssq' + tag], W=['std' + tag])
            I('dve', 'reciprocal', rstd, std, R=['std' + tag], W=['rstd' + tag])

        def transpose8(src_ap, src_name, dstT, dst_name):
            for c in range(8):
                bk = T0 if c < 4 else T1
                I('pe', 'transpose', bk[:, (c % 4) * 128:(c % 4 + 1) * 128], src_ap[:, c * 128:(c + 1) * 128], ident,
                  R=[src_name, 'cst'], W=[BN[id(bk)]])
            I('act', 'copy', dstT[:, 0:4, :].rearrange("p a b -> p (a b)"), T0[:, :], R=['bk0'], W=[dst_name + 'a'])
            I('dve', 'tensor_copy', dstT[:, 4:8, :].rearrange("p a b -> p (a b)"), T1[:, :], R=['bk1'], W=[dst_name + 'b'])

        if attn:
            I('act', 'activation', out=esink[:], in_=prm[:, P_SK:P_SK + 8], func=AF.Exp, R=['prm'], W=['esink'])
            mark0 = S.aoff
            relaug = S.av([33, 8])
            oht = S.av([33, 768])
            tt = S.av([8, 768])
            S.dma('sp', relaug[0:32, :], relb_d, writes=['relaug'])
            I('dve', 'memset', relaug[32:33, :], 1.0, W=['relaug1'])
            S.dma('sp', oht, oht_d, writes=['oht'])
            I('pe', 'matmul', P0[0:8, 0:384], relaug, oht[:, 0:384], start=True, stop=True,
              R=['relaug', 'relaug1', 'oht'], W=['bk2'])
            I('pe', 'matmul', T2[0:8, 0:384], relaug, oht[:, 384:768], start=True, stop=True,
              R=['relaug', 'relaug1', 'oht'], W=['bk3'])
            I('act', 'activation', out=tt[:, 0:384], in_=P0[0:8, 0:384], func=AF.Copy, scale=8.0, R=['bk2'], W=['tt0'])
            I('act', 'activation', out=tt[:, 384:768], in_=T2[0:8, 0:384], func=AF.Copy, scale=8.0, R=['bk3'], W=['tt1'])
            S.dma('sp', scr.ap(), tt, reads=['tt0', 'tt1'], writes=['scr'])
            S.barrier()

            for p in range(npass if stage >= 1 else 0):
                S.aoff = mark0
                KTa = S.av([128, 4096])
                KTb = S.av([128, 4096])
                QTa = S.av([128, 2048])
                QTb = S.av([128, 2048])
                Va = S.av([128, 32, 65])
                Vb = S.av([128, 32, 2, 65])
                biasT = S.av([128, 16, 32, 2])
                R_ = S.av([128, 6, 128])
                fz = S.av([128, 32, 2])
                markp = S.aoff
                for e_ in range(2):
                    for c in range(3):
                        hk = bass.AP(tensor=scr.ap().tensor, offset=(2 * p + e_) * 768 + c * 256,
                                     ap=[[1, 128], [1, 128]])
                        S.dma('sp', R_[:, e_ * 3 + c, :], hk, reads=['scr'], writes=[('R', e_, c)])
                I('pool', 'memset', Va[:, :, 64:65], 1.0, W=['Va1'])
                I('pool', 'memset', Vb[:, :, :, 64:65], 1.0, W=['Vb1'])

                def load_w(w_dram, ncols, name):
                    w_sb = S.av([128, 8, ncols])
                    S.dma('sp', w_sb, w_dram.rearrange("(c p) n -> p c n", p=128), writes=[name])
                    for c in range(8):
                        I('pool', 'tensor_scalar', out=w_sb[:, c, :], in0=w_sb[:, c, :],
                          scalar1=prm[:, P_GM + c:P_GM + c + 1], scalar2=None, op0=ALU.mult,
                          R=[name, 'prm'], W=[name])
                    return w_sb

                def tile_phase(src_dram, ntiles, w_sb, wname, ncols, gcol0, dstTa, dstTb, dna, dnb, kv, p=p,
                               Va=Va, Vb=Vb, fz=fz):
                    xt = [S.av([128, D]) for _ in range(3)]
                    xT = [S.av([128, 8, 128]) for _ in range(2)]
                    junk = S.av([128, D], BF16)
                    kq = [S.av([128, 256]) for _ in range(2)]
                    sq = S.av([128, 256])
                    kn = [S.av([128, 256]) for _ in range(2)]
                    ssq = [S.av([128, 1]) for _ in range(3)]
                    std = [S.av([128, 1]) for _ in range(3)]
                    rstd = [S.av([128, 1]) for _ in range(3)]
                    ssq4 = S.av([128, 4]); std4 = S.av([128, 4]); r4 = S.av([128, 4])
                    pbk = [P0, SA]

                    def s1(j):
                        S.dma('sp', xt[j % 3], src_dram[j * 128:(j + 1) * 128, :], writes=[('xt', j % 3)])

                    def s2(j):
                        a = j % 3
                        rms_rstd(xt[a], ('xt', a), ssq[a], std[a], rstd[a], junk, 'A%d' % a, D)
                        transpose8(xt[a], ('xt', a), xT[j % 2], 'xT%d' % (j % 2))

                    def s3(j):
                        sl = j % 2
                        bk = pbk[sl]
                        bn = BN[id(bk)]
                        for c in range(8):
                            I('pe', 'matmul', bk[:, 0:ncols], xT[sl][:, c, :], w_sb[:, c, :], start=(c == 0), stop=(c == 7),
                              R=['xT%da' % sl, 'xT%db' % sl, wname], W=[bn])
                        rs = rstd[j % 3][:, 0:1]
                        rsn = 'rstdA%d' % (j % 3)
                        I('act', 'activation', out=kq[sl], in_=bk[:, 0:256], func=AF.Copy, scale=rs, R=[bn, rsn], W=[('kq', sl)])
                        if kv:
                            I('act', 'activation', out=Va[:, j, 0:64], in_=bk[:, 256:320], func=AF.Copy, scale=rs,
                              R=[bn, rsn, 'Va1'], W=[('Va', j)])
                            I('act', 'activation', out=Vb[:, j, :, 0:64],
                              in_=bk[:, 320:448].rearrange("p (a b) -> p a b", a=2), func=AF.Copy, scale=rs,
                              R=[bn, rsn, 'Vb1'], W=[('Vb', j)])
                            I('dve', 'scalar_tensor_tensor', out=fz[:, j, :], in0=bk[:, 448:450], scalar=rs,
                              in1=prm[:, P_BF + 2 * p:P_BF + 2 * p + 2], op0=ALU.mult, op1=ALU.add,
                              R=[bn, rsn, 'prm'], W=[('fz', j)])

                    def s4(j):
                        sl = j % 2
                        I('dve', 'tensor_tensor', out=sq, in0=kq[sl], in1=kq[sl], op=ALU.mult, R=[('kq', sl)], W=['sq'])
                        I('dve', 'tensor_reduce', out=ssq4, in_=sq.rearrange("p (a b) -> p a b", a=4), axis=AX.X, op=ALU.add,
                          R=['sq'], W=['ssq4'])
                        I('act', 'activation', out=std4, in_=ssq4, func=AF.Sqrt, bias=EPS, scale=1.0 / 64, R=['ssq4'], W=['std4'])
                        I('dve', 'reciprocal', r4, std4, R=['std4'], W=['r4'])
                        I('dve', 'tensor_tensor', out=kn[sl].rearrange("p (a b) -> p a b", a=4),
                          in0=kq[sl].rearrange("p (a b) -> p a b", a=4), in1=r4.unsqueeze(2).to_broadcast([128, 4, 64]),
                          op=ALU.mult, R=[('kq', sl), 'r4'], W=[('kn', sl)])

                    def s5(j):
                        sl = j % 2
                        I('pe', 'transpose', T2[:, 0:128], kn[sl][:, 0:128], ident, R=[('kn', sl), 'cst'], W=['bk3'])
                        I('pe', 'transpose', T2[:, 128:256], kn[sl][:, 128:256], ident, R=[('kn', sl), 'cst'], W=['bk3'])
                        I('act', 'activation', out=dstTa[:, j * 128:(j + 1) * 128], in_=T2[:, 0:128], func=AF.Copy,
                          scale=prm[:, gcol0:gcol0 + 1], R=['bk3', 'prm'], W=[(dna, j)])
                        I('act', 'activation', out=dstTb[:, j * 128:(j + 1) * 128], in_=T2[:, 128:256], func=AF.Copy,
                          scale=prm[:, gcol0 + 1:gcol0 + 2], R=['bk3', 'prm'], W=[(dnb, j)])

                    s1(0)
                    if ntiles > 1:
                        s1(1)
                    s2(0)
                    for i in range(ntiles + 1):
                        if i + 2 < ntiles:
                            s1(i + 2)
                        if i + 1 < ntiles:
                            s2(i + 1)
                        if i < ntiles:
                            s3(i)
                            s4(i)
                        if i >= 1:
                            s5(i - 1)

                w_sb = load_w(wkv_d[p], WKV, 'wkv')
                tile_phase(xf_d, nkb, w_sb, 'wkv', WKV, P_GC, KTa, KTb, 'KTa', 'KTb', True)
                nlf = S.av([128, 32, 2]); ex = S.av([128, 32, 2])
                fw = S.av([128, 160]); cpref = S.av([128, 32, 2]); gall = S.av([128, 32, 2]); gc = S.av([128, 16, 2])
                fzn = [('fz', j) for j in range(nkb)]
                if nkb < 32:
                    I('dve', 'memset', fz[:, nkb:32, :], 1.0, W=['fzpad'])
                    fzn.append('fzpad')
                I('act', 'activation', out=ex, in_=fz, func=AF.Exp, scale=-1.0, R=fzn, W=['ex'])
                I('act', 'activation', out=nlf, in_=ex, func=AF.Ln, bias=1.0, scale=1.0, R=['ex'], W=['nlf'])
                for j in range(32):
                    I('pe', 'matmul', P0[:, 2 * j:2 * j + 2], tri, nlf[:, j, :], start=True, stop=True,
                      R=['nlf', 'cst'], W=['bk2'])
                    I('pe', 'matmul', P0[:, 64 + 2 * j:64 + 2 * j + 2], ones, nlf[:, j, :], start=True, stop=True,
                      R=['nlf', 'cst'], W=['bk2'])
                for m in range(16):
                    I('pe', 'matmul', P0[:, 128 + 2 * m:128 + 2 * m + 2], mab[:, 0, :], nlf[:, 2 * m, :],
                      start=True, stop=False, R=['nlf', 'ccst'], W=['bk2'])
                    I('pe', 'matmul', P0[:, 128 + 2 * m:128 + 2 * m + 2], mab[:, 1, :], nlf[:, 2 * m + 1, :],
                      start=False, stop=True, R=['nlf', 'ccst'], W=['bk2'])
                I('dve', 'tensor_copy', fw, P0[:, 0:160], R=['bk2'], W=['fw'])
                Wv = fw[:, 0:64].rearrange("p (a b) -> p a b", a=32)
                BS = fw[:, 64:128].rearrange("p (a b) -> p a b", a=32)
                PM = fw[:, 128:160].rearrange("p (a b) -> p a b", a=16)
                I('dve', 'memset', cpref[:, 0, :], 0.0, W=['cpref'])
                for j in range(1, 32):
                    I('dve', 'tensor_tensor', out=cpref[:, j, :], in0=cpref[:, j - 1, :], in1=BS[:, j - 1, :], op=ALU.add,
                      R=['cpref', 'fw'], W=['cpref'])
                I('dve', 'tensor_tensor', out=gall, in0=Wv, in1=cpref, op=ALU.add, R=['fw', 'cpref'], W=['gall'])
                I('dve', 'tensor_tensor', out=gc, in0=cpref[:, 0:32:2, :], in1=PM, op=ALU.add, R=['fw', 'cpref'], W=['gc'])
                for m in range(16):
                    nj = 2 * m + 2
                    I('dve', 'tensor_tensor', out=biasT[:, m, 0:nj, :], in0=gall[:, 0:nj, :],
                      in1=gc[:, m, :].unsqueeze(1).to_broadcast([128, nj, 2]), op=ALU.subtract,
                      R=['gall', 'gc'], W=['biasT'])
                S.barrier()
                if stage < 2:
                    continue
                S.aoff = markp
                w_sb = load_w(wq_d[p], 256, 'wq')
                tile_phase(xo_d, 16, w_sb, 'wq', 256, P_GC + 2, QTa, QTb, 'QTa', 'QTb', False)
                S.barrier()
                if stage < 3:
                    continue
                S.aoff = markp
                wo_sb = S.av([128, 2, D])
                S.dma('sp', wo_sb, wo_d[p].rearrange("(r p) n -> p r n", p=128), writes=['wo'])
                NPT = 3
                PT = [S.av([128, 4, 128]) for _ in range(NPT)]
                PT2 = [S.av([128, 3, 128]) for _ in range(2)]
                mixp = [S.av([128, 256]) for _ in range(2)]
                mixT = S.av([128, 2, 128])
                rinv = [S.av([128, 4]) for _ in range(2)]
                den = [S.av([128, 2]) for _ in range(2)]
                O4 = OB[:, 0:260].rearrange("p (a b) -> p a b", a=4)
                sbanks = [P0, SA, SB]
                items = []
                for m in range(16):
                    nj = 2 * m + 2
                    for e_ in range(2):
                        for g0 in range(0, nj, 4):
                            items.append(('fox', m, e_, list(range(g0, min(g0 + 4, nj)))))
                    for e_ in range(2):
                        items.append(('swa', m, e_, [c for c in range(3) if 0 <= 2 * m - 1 + c < 32]))
                    items.append(('end', m, 0, []))
                fcount = [0]
                scount = [0]
                slot_of = {}

                def scores(idx):
                    kind, m, e_, grp = items[idx]
                    qs = slice(m * 128, (m + 1) * 128)
                    pr = slice(64 * e_, 64 * e_ + 64)
                    if kind == 'fox':
                        sl = fcount[0] % NPT
                        fcount[0] += 1
                        slot_of[idx] = sl
                        bk = sbanks[sl]
                        bn = BN[id(bk)]
                        for jj, j in enumerate(grp):
                            diag = j >= 2 * m
                            I('pe', 'matmul', bk[:, jj * 128:(jj + 1) * 128], KTb[pr, j * 128:(j + 1) * 128], QTb[pr, qs],
                              start=True, stop=(not diag), R=[('KTb', j), ('QTb', m)], W=[bn])
                            if diag:
                                I('pe', 'matmul', bk[:, jj * 128:(jj + 1) * 128], ident, foxmask[:, j - 2 * m, :],
                                  start=False, stop=True, R=['cst', 'ccst'], W=[bn])
                        for jj, j in enumerate(grp):
                            I('act', 'activation', out=PT[sl][:, jj, :], in_=bk[:, jj * 128:(jj + 1) * 128], func=AF.Exp,
                              bias=biasT[:, m, j, e_:e_ + 1], scale=0.125, R=[bn, 'biasT'], W=[('PT', sl, jj)])
                    elif kind == 'swa':
                        sl = scount[0] % 2
                        scount[0] += 1
                        slot_of[idx] = sl
                        for c in grp:
                            j = 2 * m - 1 + c
                            I('pe', 'matmul', S2[:, c * 128:(c + 1) * 128], KTa[pr, j * 128:(j + 1) * 128], QTa[pr, qs],
                              start=True, stop=False, R=[('KTa', j), ('QTa', m)], W=['bk6'])
                            I('pe', 'matmul', S2[:, c * 128:(c + 1) * 128], jmat, R_[:, e_ * 3 + c, :],
                              start=False, stop=True, R=['cst', ('R', e_, c)], W=['bk6'])
                        c0 = grp[0]
                        I('act', 'activation', out=PT2[sl][:, c0:3, :].rearrange("p a b -> p (a b)"), in_=S2[:, c0 * 128:384],
                          func=AF.Exp, scale=0.125, R=['bk6'], W=[('PT2', sl)])

                def pvs(idx):
                    kind, m, e_, grp = items[idx]
                    par = m % 2
                    if kind == 'fox':
                        sl = slot_of[idx]
                        nj = 2 * m + 2
                        for jj, j in enumerate(grp):
                            I('pe', 'matmul', O4[:, e_, :], PT[sl][:, jj, :], Vb[:, j, e_, :],
                              start=(j == 0), stop=(j == nj - 1), R=[('PT', sl, jj), ('Vb', j), 'Vb1'], W=['bk7'])
                    elif kind == 'swa':
                        sl = slot_of[idx]
                        for c in grp:
                            j = 2 * m - 1 + c
                            I('pe', 'matmul', O4[:, 2 + e_, :], PT2[sl][:, c, :], Va[:, j, :], start=(c == grp[0]), stop=(c == 2),
                              R=[('PT2', sl), ('Va', j), 'Va1'], W=['bk7'])
                    else:
                        for e_ in range(2):
                            I('dve', 'reciprocal', rinv[par][:, e_:e_ + 1], O4[:, e_, 64:65], R=['bk7'], W=[('rinv', par, e_)])
                            I('dve', 'tensor_scalar', out=mixp[par][:, 128 + 64 * e_:128 + 64 * e_ + 64], in0=O4[:, e_, 0:64],
                              scalar1=rinv[par][:, e_:e_ + 1], scalar2=None, op0=ALU.mult,
                              R=['bk7', ('rinv', par, e_)], W=[('mixp', par)])
                            I('dve', 'tensor_tensor', out=den[par][:, e_:e_ + 1], in0=O4[:, 2 + e_, 64:65],
                              in1=esink[:, 2 * p + e_:2 * p + e_ + 1], op=ALU.add, R=['bk7', 'esink'], W=[('den', par, e_)])
                            I('dve', 'reciprocal', rinv[par][:, 2 + e_:3 + e_], den[par][:, e_:e_ + 1],
                              R=[('den', par, e_)], W=[('rinv', par, 2 + e_)])
                            I('dve', 'tensor_scalar', out=mixp[par][:, 64 * e_:64 * e_ + 64], in0=O4[:, 2 + e_, 0:64],
                              scalar1=rinv[par][:, 2 + e_:3 + e_], scalar2=None, op0=ALU.mult,
                              R=['bk7', ('rinv', par, 2 + e_)], W=[('mixp', par)])

                def outproj_t(m):
                    par = m % 2
                    I('pe', 'transpose', T2[:, 0:128], mixp[par][:, 0:128], ident, R=[('mixp', par), 'cst'], W=['bk3'])
                    I('pe', 'transpose', T2[:, 128:256], mixp[par][:, 128:256], ident, R=[('mixp', par), 'cst'], W=['bk3'])
                    I('act', 'copy', mixT.rearrange("p a b -> p (a b)"), T2[:, 0:256], R=['bk3'], W=['mixT'])

                def outproj_w(m):
                    for hf, bk in ((0, T0), (1, T1)):
                        for r in range(2):
                            I('pe', 'matmul', bk[:, :], mixT[:, r, :], wo_sb[:, r, hf * 512:(hf + 1) * 512],
                              start=(r == 0), stop=(r == 1), R=['mixT', 'wo'], W=[BN[id(bk)]])
                        I('dve', 'tensor_tensor', out=x2[:, m, hf * 512:(hf + 1) * 512], in0=x2[:, m, hf * 512:(hf + 1) * 512],
                          in1=bk[:, :], op=ALU.add, R=[BN[id(bk)], ('x2', m)], W=[('x2', m)])

                LAG = 2
                pending = []
                n_items = len(items)
                for idx in range(n_items + LAG):
                    if idx < n_items:
                        scores(idx)
                    k = idx - LAG
                    if 0 <= k < n_items:
                        pvs(k)
                        if items[k][0] == 'end':
                            pending.append((idx + 2, outproj_t, items[k][1]))
                            pending.append((idx + 4, outproj_w, items[k][1]))
                    for it in [q for q in pending if q[0] <= idx]:
                        it[1](it[2])
                        pending.remove(it)
                for it in sorted(pending, key=lambda q: q[0]):
                    it[1](it[2])
                S.barrier()
            S.aoff = mark0

        if dbg:
            for m in range(16):
                S.dma('sp', dbg_d[m * 128:(m + 1) * 128, :], x2[:, m, :], reads=[('x2', m)], out=True, sem='d_dbg%d' % m)
            S.barrier()

        if peer:
            S.aoff = 0
            eidx = S.av([128, 16, 128], I32)
            gate = S.av([128, 16, 128])
            gf = S.av([128, D])
            xn = S.av([128, D])
            junk = S.av([128, D], BF16)
            ssq = S.av([128, 1]); std = S.av([128, 1]); rstd = S.av([128, 1])
            mark1 = S.aoff
            S.dma('sp', gf, gf_d, writes=['gf'])

            def norm_tile(t):
                rms_rstd(x2[:, t, :], ('x2', t), ssq, std, rstd, junk, 'P', D)
                I('dve', 'scalar_tensor_tensor', out=xn, in0=x2[:, t, :], scalar=rstd[:, 0:1], in1=gf,
                  op0=ALU.mult, op1=ALU.mult, R=[('x2', t), 'rstdP', 'gf'], W=['xn'])

            skn = S.av([128, 2, 128]); skT = S.av([128, 2, 128])
            S.dma('sp', skn, sk_d.rearrange("a n d -> n a d"), writes=['skn'])
            for a in range(2):
                I('pe', 'transpose', T2[:, a * 128:(a + 1) * 128], skn[:, a, :], ident, R=['skn', 'cst'], W=['bk3'])
            I('act', 'copy', skT.rearrange("p a b -> p (a b)"), T2[:, 0:256], R=['bk3'], W=['skT'])
            wq_sb = S.av([128, 8, 1024])
            xnT = S.av([128, 8, 128])
            qvT = S.av([128, 8, 128])
            s = S.av([128, 8, 128]); s2 = S.av([128, 8, 128])
            vals = S.av([128, 8, 16]); idxu = S.av([128, 8, 16], U32); idxf = S.av([128, 8, 16])
            cand = S.av([128, 4, 256]); cand2 = S.av([128, 4, 256])
            ts = S.av([128, 4, 16]); pos = S.av([128, 4, 16], U32); posf = S.av([128, 4, 16])
            ge4 = S.av([128, 4, 16, 16]); oh4 = S.av([128, 4, 16, 16])
            af = S.av([128, 4, 16]); bfv = S.av([128, 4, 16]); sel1 = S.av([128, 4, 16]); sel2 = S.av([128, 4, 16])
            eidf = S.av([128, 4, 16]); tsm = S.av([128, 4, 16]); gex = S.av([128, 4, 16])
            gs = S.av([128, 4]); rg = S.av([128, 4])
            v4 = vals.rearrange("p (h a) k -> p h a k", a=2)
            i4 = idxf.rearrange("p (h a) k -> p h a k", a=2)
            qbanks = [P0, SA]
            B4 = [128, 4, 16, 16]
            for hh in range(2):
                for c in range(8):
                    S.dma('sp', wq_sb[:, c, :], wqy_d[c * 128:(c + 1) * 128, hh * 1024:(hh + 1) * 1024], writes=[('wqy', c)])
                for t in range(NT):
                    norm_tile(t)
                    transpose8(xn, 'xn', xnT, 'xnT')
                    for g in range(2):
                        bk = qbanks[g]
                        for cq in range(4):
                            cc = 4 * g + cq
                            for c in range(8):
                                I('pe', 'matmul', bk[:, cq * 128:(cq + 1) * 128], wq_sb[:, c, cc * 128:(cc + 1) * 128],
                                  xnT[:, c, :], start=(c == 0), stop=(c == 7),
                                  R=[('wqy', c), 'xnTa', 'xnTb'], W=[BN[id(bk)]])
                        I('act', 'copy', qvT[:, 4 * g:4 * g + 4, :].rearrange("p a b -> p (a b)"), bk[:, :],
                          R=[BN[id(bk)]], W=[('qvT', g)])
                    for g in range(2):
                        bk = [S2, OB][g]
                        for cq in range(4):
                            cc = 4 * g + cq
                            I('pe', 'matmul', bk[:, cq * 128:(cq + 1) * 128], qvT[:, cc, :], skT[:, cc % 2, :],
                              start=True, stop=True, R=[('qvT', g), 'skT'], W=[BN[id(bk)]])
                        I('act', 'copy', s[:, 4 * g:4 * g + 4, :].rearrange("p a b -> p (a b)"), bk[:, :],
                          R=[BN[id(bk)]], W=[('s', g)])
                    for cc in range(8):
                        I('dve', 'max', vals[:, cc, 0:8], s[:, cc, :], R=[('s', cc // 4)], W=[('v', cc, 0)])
                    for cc in range(8):
                        I('dve', 'max_index', idxu[:, cc, 0:8], vals[:, cc, 0:8], s[:, cc, :],
                          R=[('s', cc // 4), ('v', cc, 0)], W=[('i', cc, 0)])
                    for cc in range(8):
                        I('dve', 'match_replace', s2[:, cc, :], vals[:, cc, 0:8], s[:, cc, :], -1e30,
                          R=[('s', cc // 4), ('v', cc, 0)], W=[('s2', cc)])
                    for cc in range(8):
                        I('dve', 'max', vals[:, cc, 8:16], s2[:, cc, :], R=[('s2', cc)], W=[('v', cc, 1)])
                    for cc in range(8):
                        I('dve', 'max_index', idxu[:, cc, 8:16], vals[:, cc, 8:16], s2[:, cc, :],
                          R=[('s2', cc), ('v', cc, 1)], W=[('i', cc, 1)])
                    alli = [('i', cc, k) for cc in range(8) for k in range(2)]
                    allv = [('v', cc, k) for cc in range(8) for k in range(2)]
                    I('dve', 'tensor_copy', idxf, idxu, R=alli, W=['idxf'])
                    I('dve', 'tensor_tensor', out=cand.rearrange("p h (a b) -> p h a b", a=16),
                      in0=v4[:, :, 0, :].unsqueeze(3).to_broadcast(B4), in1=v4[:, :, 1, :].unsqueeze(2).to_broadcast(B4),
                      op=ALU.add, R=allv, W=['cand'])
                    for h in range(4):
                        I('dve', 'max', ts[:, h, 0:8], cand[:, h, :], R=['cand'], W=[('ts', h, 0)])
                    for h in range(4):
                        I('dve', 'max_index', pos[:, h, 0:8], ts[:, h, 0:8], cand[:, h, :],
                          R=['cand', ('ts', h, 0)], W=[('pos', h, 0)])
                    for h in range(4):
                        I('dve', 'match_replace', cand2[:, h, :], ts[:, h, 0:8], cand[:, h, :], -1e30,
                          R=['cand', ('ts', h, 0)], W=[('cand2', h)])
                    for h in range(4):
                        I('dve', 'max', ts[:, h, 8:16], cand2[:, h, :], R=[('cand2', h)], W=[('ts', h, 1)])
                    for h in range(4):
                        I('dve', 'max_index', pos[:, h, 8:16], ts[:, h, 8:16], cand2[:, h, :],
                          R=[('cand2', h), ('ts', h, 1)], W=[('pos', h, 1)])
                    allp = [('pos', h, k) for h in range(4) for k in range(2)]
                    allt = [('ts', h, k) for h in range(4) for k in range(2)]
                    I('dve', 'tensor_copy', posf, pos, R=allp, W=['posf'])
                    I('dve', 'tensor_tensor', out=ge4, in0=posf.unsqueeze(3).to_broadcast(B4),
                      in1=iota16.unsqueeze(1).unsqueeze(1).to_broadcast(B4), op=ALU.is_ge, R=['posf', 'cst'], W=['ge4'])
                    I('dve', 'tensor_reduce', out=af, in_=ge4, axis=AX.X, op=ALU.add, R=['ge4'], W=['af'])
                    I('dve', 'tensor_scalar', out=af, in0=af, scalar1=-1.0, scalar2=None, op0=ALU.add, R=['af'], W=['af'])
                    I('dve', 'scalar_tensor_tensor', out=bfv, in0=af, scalar=-16.0, in1=posf, op0=ALU.mult, op1=ALU.add,
                      R=['af', 'posf'], W=['bfv'])
                    for (src, ii, dst, nm) in ((af, 0, sel1, 'sel1'), (bfv, 1, sel2, 'sel2')):
                        I('dve', 'tensor_tensor', out=oh4, in0=iota.unsqueeze(1).unsqueeze(1).to_broadcast(B4),
                          in1=src.unsqueeze(3).to_broadcast(B4), op=ALU.is_equal, R=['af', 'bfv', 'cst'], W=['oh4'])
                        I('dve', 'tensor_tensor', out=oh4, in0=oh4, in1=i4[:, :, ii, :].unsqueeze(2).to_broadcast(B4),
                          op=ALU.mult, R=['oh4', 'idxf'], W=['oh4'])
                        I('dve', 'tensor_reduce', out=dst, in_=oh4, axis=AX.X, op=ALU.add, R=['oh4'], W=[nm])
                    I('dve', 'scalar_tensor_tensor', out=eidf, in0=sel1, scalar=128.0, in1=sel2, op0=ALU.mult, op1=ALU.add,
                      R=['sel1', 'sel2'], W=['eidf'])
                    I('dve', 'tensor_copy', eidx[:, t, hh * 64:(hh + 1) * 64], eidf.rearrange("p a b -> p (a b)"),
                      R=['eidf'], W=[('eidx', t, hh)])
                    I('dve', 'tensor_tensor', out=tsm, in0=ts, in1=ts[:, :, 0:1].to_broadcast([128, 4, 16]), op=ALU.subtract,
                      R=allt, W=['tsm'])
                    I('act', 'activation', out=gex, in_=tsm, func=AF.Exp, R=['tsm'], W=['gex'])
                    I('dve', 'tensor_reduce', out=gs, in_=gex, axis=AX.X, op=ALU.add, R=['gex'], W=['gs'])
                    I('dve', 'reciprocal', rg, gs, R=['gs'], W=['rg'])
                    I('dve', 'tensor_tensor', out=gate[:, t, hh * 64:(hh + 1) * 64].rearrange("p (a b) -> p a b", a=4),
                      in0=gex, in1=rg.unsqueeze(2).to_broadcast([128, 4, 16]), op=ALU.mult,
                      R=['gex', 'rg'], W=[('gate', t, hh)])
            S.barrier()
            S.aoff = mark1
            NGBL = ngb
            ring = [S.av([128, D]) for _ in range(NGBL)]
            acc1 = S.av([128, D])
            pre = S.av([128, 128]); gl = S.av([128, 128]); coef = S.av([128, 128])
            gcount = 0
            for t in range(NT if p2 else 0):
                norm_tile(t)
                en = [('eidx', t, 0), ('eidx', t, 1)]
                for sl_ in range(128):
                    r = gcount % NGBL
                    gcount += 1
                    S.dma('pool', ring[r], eu_d, reads=en, writes=[('ring', r)],
                          indirect=bass.IndirectOffsetOnAxis(ap=eidx[:, t, sl_:sl_ + 1], axis=0))
                    if nodve:
                        I('dve', 'tensor_copy', pre[:, sl_:sl_ + 1], ring[r][:, 0:1], R=[('ring', r), 'xn'], W=[('pre', sl_)])
                    else:
                        I('dve', 'scalar_tensor_tensor', out=ring[r], in0=ring[r], scalar=1.0, in1=xn, op0=ALU.mult, op1=ALU.mult,
                          accum_out=pre[:, sl_:sl_ + 1], R=['xn'], W=[('ring', r), ('pre', sl_)])
                I('act', 'activation', out=gl, in_=pre, func=AF.Gelu, R=[('pre', k) for k in range(128)], W=['gl'])
                I('dve', 'tensor_tensor', out=coef, in0=gl, in1=gate[:, t, :], op=ALU.mult,
                  R=['gl', ('gate', t, 0), ('gate', t, 1)], W=['coef'])
                for sl_ in range(128):
                    r = gcount % NGBL
                    gcount += 1
                    S.dma('pool', ring[r], ev_d, reads=en, writes=[('ring', r)],
                          indirect=bass.IndirectOffsetOnAxis(ap=eidx[:, t, sl_:sl_ + 1], axis=0))
                    if nodve:
                        I('dve', 'tensor_copy', acc1[:, 0:1], ring[r][:, 0:1], R=[('ring', r), 'coef'], W=['acc1'])
                    elif sl_ == 1:
                        I('dve', 'tensor_scalar', out=acc1, in0=ring[r], scalar1=coef[:, sl_:sl_ + 1], scalar2=None,
                          op0=ALU.mult, R=[('ring', r), 'coef'], W=['acc1'])
                    elif sl_ % 2 == 1:
                        I('dve', 'scalar_tensor_tensor', out=acc1, in0=ring[r], scalar=coef[:, sl_:sl_ + 1], in1=acc1,
                          op0=ALU.mult, op1=ALU.add, R=[('ring', r), 'coef', 'acc1'], W=['acc1'])
                    else:
                        I('dve', 'scalar_tensor_tensor', out=x2[:, t, :], in0=ring[r], scalar=coef[:, sl_:sl_ + 1],
                          in1=x2[:, t, :], op0=ALU.mult, op1=ALU.add, R=[('ring', r), 'coef', ('x2', t)], W=[('x2', t)])
                I('dve', 'tensor_tensor', out=x2[:, t, :], in0=x2[:, t, :], in1=acc1, op=ALU.add,
                  R=['acc1', ('x2', t)], W=[('x2', t)])
                S.dma('sp', y_d[t * 128:(t + 1) * 128, :], x2[:, t, :], reads=[('x2', t)], out=True, sem='d_y%d' % t)
        else:
            for t in range(NT):
                S.dma('sp', y_d[t * 128:(t + 1) * 128, :], x2[:, t, :], reads=[('x2', t)], out=True, sem='d_y%d' % t)
        S.finish()
    return nc


def _t5_bucket_table():
    d = np.arange(256)
    nf = np.maximum(d, 1).astype(np.float32)
    large = 16 + (np.log(nf / np.float32(16)) / np.float32(math.log(128 / 16)) * np.float32(16)).astype(np.int32)
    large = np.minimum(large, 31)
    return np.where(d < 16, d, large)


def _consts():
    cst = np.zeros((128, CSTW), np.float32)
    cst[:, C_ID:C_ID + 128] = np.eye(128)
    cst[:, C_J:C_J + 128] = np.eye(128)[::-1]
    cst[:, C_TRI:C_TRI + 128] = np.triu(np.ones((128, 128)))
    cst[:, C_ONE:C_ONE + 128] = 1.0
    cst[:, C_IOTA:C_IOTA + 16] = np.arange(16)
    cst[:, C_IOTA16:C_IOTA16 + 16] = 16 * np.arange(16)
    return cst


def _core_consts(h):
    cc = np.zeros((128, 512), np.float32)
    k = np.arange(128)[:, None]
    q = np.arange(128)[None, :]
    trimask = np.where(k > q, 8 * NEGM, 0.0).astype(np.float32)
    full = np.full((128, 128), 8 * NEGM, np.float32)
    upto = np.zeros((128, 128), np.float32)
    upto[:64, :] = 1.0
    if h == 0:
        cc[:, 0:128] = trimask
        cc[:, 128:256] = full
        cc[:, 256:384] = upto
        cc[:, 384:512] = 0.0
    else:
        cc[:, 0:128] = 0.0
        cc[:, 128:256] = trimask
        cc[:, 256:384] = 1.0
        cc[:, 384:512] = upto
    bt = _t5_bucket_table()
    oh = np.zeros((33, 768), np.float32)
    types = ['prev', 'cur', 'mask'] if h == 0 else ['mask', 'prev', 'cur']
    for c, ty in enumerate(types):
        for mp in range(256):
            col = c * 256 + mp
            if ty == 'prev':
                dist = mp + 1
            elif ty == 'cur':
                dist = mp - 127
            else:
                dist = -1
            if 0 <= dist < 128 and mp <= 254:
                oh[bt[dist], col] = 1.0
            else:
                oh[32, col] = NEGM
    return cc, oh


_NC_CACHE = {}


def prepare(inputs):
    x = np.asarray(inputs["x"], np.float32)
    w_in = np.asarray(inputs["w_in"], np.float32)[0]
    w_out = np.asarray(inputs["w_out"], np.float32)[0]
    rep = lambda v, n=128: np.ascontiguousarray(np.broadcast_to(np.asarray(v, np.float32).reshape(1, -1), (n, np.size(v))))
    wq4 = np.zeros((4, D, 256), np.float32)
    wkv4 = np.zeros((4, D, WKV), np.float32)
    wo4 = np.zeros((4, 256, D), np.float32)
    for p in range(4):
        kvh = p // 2
        qa = w_in[:, 128 * p:128 * p + 128]
        qb = w_in[:, 768 + 128 * p:768 + 128 * p + 128]
        ka = w_in[:, 512 + 64 * kvh:512 + 64 * kvh + 64]
        va = w_in[:, 640 + 64 * kvh:640 + 64 * kvh + 64]
        kb = w_in[:, 1280 + 128 * p:1280 + 128 * p + 128]
        vb = w_in[:, 1792 + 128 * p:1792 + 128 * p + 128]
        fb = w_in[:, 2304 + 2 * p:2304 + 2 * p + 2]
        wq4[p] = np.concatenate([qa, qb], axis=1)
        wkv4[p] = np.concatenate([ka, ka, kb, va, vb, fb], axis=1)
        wo4[p] = np.concatenate([w_out[128 * p:128 * p + 128], w_out[512 + 128 * p:512 + 128 * p + 128]], axis=0)
    prm = np.zeros((128, 32), np.float32)
    prm[:, P_GM:P_GM + 8] = np.asarray(inputs["norm_mix"], np.float32)[0].reshape(8, 128).T
    dup = lambda v: np.concatenate([np.asarray(v, np.float32)[0], np.asarray(v, np.float32)[0]])
    prm[:, P_GC + 0] = dup(inputs["k_norm_a"])
    prm[:, P_GC + 1] = dup(inputs["k_norm_b"])
    prm[:, P_GC + 2] = dup(inputs["q_norm_a"])
    prm[:, P_GC + 3] = dup(inputs["q_norm_b"])
    prm[:, P_BF:P_BF + 8] = rep(inputs["b_forget"])
    prm[:, P_SK:P_SK + 8] = rep(inputs["sinks"])
    cst = _consts()
    shared = {
        "cst": cst, "prm": prm, "wq4": wq4, "wkv4": wkv4, "wo4": wo4,
        "relb": np.asarray(inputs["rel_bias"], np.float32),
        "wqy": np.asarray(inputs["w_query"], np.float32)[0],
        "sk": np.stack([np.asarray(inputs["sub_keys1"], np.float32)[0], np.asarray(inputs["sub_keys2"], np.float32)[0]]),
        "eu": np.asarray(inputs["expert_u"], np.float32)[0],
        "ev": np.asarray(inputs["expert_v"], np.float32)[0],
        "gf": rep(inputs["norm_ffn"]),
    }
    in_maps = []
    for c in range(8):
        b, h = c // 2, c % 2
        xb = x[b]
        xo = np.ascontiguousarray(xb.reshape(16, 2, 128, D)[:, h].reshape(2048, D))
        cc, oh = _core_consts(h)
        m = dict(shared)
        m.update({"xo": xo, "xf": xb, "ccst": cc, "ohtab": oh})
        in_maps.append(m)
    return in_maps


def kernel(**inputs):
    in_maps = prepare(inputs)
    if "nc" not in _NC_CACHE:
        _NC_CACHE["nc"] = build_nc()
    res = run_bass_kernel_spmd(_NC_CACHE["nc"], in_maps, core_ids=list(range(8)))
    out = np.zeros((4, 4096, D), np.float32)
    for c in range(8):
        b, h = c // 2, c % 2
        out[b].reshape(16, 2, 128, D)[:, h] = res.results[c]["y"].reshape(16, 128, D)
    return out
```

```python
import math
import numpy as np
import concourse.bass as bass
import concourse.mybir as mybir
from concourse.bass_utils import run_bass_kernel_spmd
from contextlib import ExitStack

F32 = mybir.dt.float32
BF16 = mybir.dt.bfloat16
U32 = mybir.dt.uint32
I32 = mybir.dt.int32
ALU = mybir.AluOpType
AF = mybir.ActivationFunctionType
AX = mybir.AxisListType

D = 1024
EPS = 1e-6
NEGM = -30000.0


class Sched:
    ENGS = ('pe', 'dve', 'act', 'pool', 'sp')

    def __init__(self, nc, es):
        self.nc, self.es = nc, es
        self.q = {e: [] for e in self.ENGS}
        self.cnt = {e: 0 for e in ('pe', 'dve', 'act', 'pool')}
        self.sems = {}
        for e in self.cnt:
            self.sems['c_' + e] = es.enter_context(nc.semaphore('c_' + e))
        self.waited = {e: {} for e in self.ENGS}
        self.lw = {}
        self.rd = {}
        self.dcnt = {}
        self.outdeps = {}
        self.arena = None
        self.aoff = 0
        self.AW = 0

    def sb(self, name, shape, dt=F32):
        return self.es.enter_context(self.nc.sbuf_tensor("s_" + name, list(shape), dt))

    def ps(self, name, shape=(128, 512), dt=F32):
        return self.es.enter_context(self.nc.psum_tensor("p_" + name, list(shape), dt))

    def make_arena(self, words):
        self.arena = self.sb("arena", [128, words])
        self.AW = words
        self.aoff = 0

    def av(self, shape, dt=F32):
        n = 1
        for v in shape[1:]:
            n *= v
        if dt == BF16:
            n = (n + 1) // 2
        off = self.aoff
        self.aoff += n
        assert self.aoff <= self.AW, ("arena overflow", self.aoff, self.AW)
        ap = self.arena[0:shape[0], off:off + n]
        if dt != F32:
            ap = ap.bitcast(dt)
        if len(shape) == 3:
            ap = ap.rearrange("p (a b) -> p a b", a=shape[1])
        elif len(shape) == 4:
            ap = ap.rearrange("p (a b c) -> p a b c", a=shape[1], b=shape[2])
        return ap

    def _deps(self, reads, writes):
        deps = {}

        def add(d):
            if d is not None and deps.get(d[0], 0) < d[1]:
                deps[d[0]] = d[1]
        for b in reads:
            add(self.lw.get(b))
        for b in writes:
            add(self.lw.get(b))
            for d in self.rd.get(b, ()):
                add(d)
        return deps

    def _emit_waits(self, eng, deps, skip_self=False):
        for s, v in deps.items():
            if skip_self and s == 'c_' + eng:
                continue
            if self.waited[eng].get(s, 0) < v:
                self.q[eng].append(('w', s, v))
                self.waited[eng][s] = v

    def _commit(self, tok, reads, writes):
        for b in writes:
            self.lw[b] = tok
            self.rd[b] = []
        for b in reads:
            if b not in writes:
                self.rd.setdefault(b, []).append(tok)

    def op(self, eng, fn, reads=(), writes=()):
        deps = self._deps(reads, writes)
        self._emit_waits(eng, deps, skip_self=(eng == 'pe'))
        self.cnt[eng] += 1
        tok = ('c_' + eng, self.cnt[eng])
        self.q[eng].append(('o', fn, 'c_' + eng, 1))
        self._commit(tok, reads, writes)

    def I(self, eng, meth, *args, R=(), W=(), **kw):
        self.op(eng, (meth, args, kw), R, W)

    def dma(self, eng, out_ap, in_ap, reads=(), writes=(), out=False, indirect=None, sem=None):
        if sem is None:
            sem = 'd_' + str(writes[0] if writes else reads[0]) + ('_o' if out else '')
        if sem not in self.sems:
            nm = ''.join(ch if ch.isalnum() else '_' for ch in sem)
            self.sems[sem] = self.es.enter_context(self.nc.semaphore(nm))
            self.dcnt[sem] = 0
        deps = self._deps(reads, writes)
        if self.dcnt[sem] > 0:
            v = 16 * self.dcnt[sem]
            if deps.get(sem, 0) < v:
                deps[sem] = v
        self._emit_waits(eng, deps)
        self.dcnt[sem] += 1
        tok = (sem, 16 * self.dcnt[sem])
        if indirect is not None:
            fn = lambda e: e.indirect_dma_start(out_ap, None, in_ap, indirect)
        else:
            fn = lambda e: e.dma_start(out=out_ap, in_=in_ap)
        self.q[eng].append(('o', fn, sem, 16))
        self._commit(tok, reads, writes)
        if out:
            self.outdeps[sem] = tok[1]

    def barrier(self):
        allv = {}
        for e, c in self.cnt.items():
            if c > 0:
                allv['c_' + e] = c
        for s, c in self.dcnt.items():
            if c > 0:
                allv[s] = 16 * c
        for eng in self.ENGS:
            self._emit_waits(eng, allv)
        self.lw.clear()
        self.rd.clear()

    def finish(self):
        self.barrier()
        nc = self.nc
        S = self

        def run(name, e):
            for it in S.q[name]:
                if it[0] == 'w':
                    e.wait_ge(S.sems[it[1]], it[2])
                else:
                    f = it[1]
                    ins = getattr(e, f[0])(*f[1], **f[2]) if isinstance(f, tuple) else f(e)
                    if isinstance(ins, (list, tuple)):
                        ins = ins[-1]
                    ins.then_inc(S.sems[it[2]], it[3])
        with nc.Block() as block:
            @block.tensor
            def _(e):
                run('pe', e)

            @block.vector
            def _(e):
                run('dve', e)

            @block.scalar
            def _(e):
                run('act', e)

            @block.gpsimd
            def _(e):
                run('pool', e)

            @block.sync
            def _(e):
                run('sp', e)


C_ID, C_J, C_TRI, C_ONE, C_IOTA, C_IOTA16 = 0, 128, 256, 384, 512, 528
CSTW = 544
P_GM, P_GC, P_BF, P_SK = 0, 8, 12, 20
WKV = 450
NGB = 12


def build_nc(NT=16, attn=True, peer=True, npass=4, nkb=32, dbg=False, stage=3, p2=True, nodve=False, ngb=NGB):
    nc = bass.Bass("TRN2", target_bir_lowering=False)
    es = ExitStack()
    with es:
        def din(name, shape, dt=F32):
            return nc.dram_tensor(name, list(shape), dt, kind="ExternalInput").ap()
        xo_d = din("xo", [2048, D])
        xf_d = din("xf", [4096, D])
        xoT_d = din("xoT", [D, 2048])
        xfT_d = din("xfT", [D, 4096])
        cst_d = din("cst", [128, CSTW])
        ccst_d = din("ccst", [128, 512])
        prm_d = din("prm", [128, 32])
        wq_d = din("wq4", [4, D, 256])
        wkv_d = din("wkv4", [4, D, WKV])
        wo_d = din("wo4", [4, 256, D])
        relb_d = din("relb", [32, 8])
        oht_d = din("ohtab", [33, 768])
        wqy_d = din("wqy", [D, 2048])
        sk_d = din("sk", [2, 128, 128])
        eu_d = din("eu", [16384, D])
        ev_d = din("ev", [16384, D])
        gf_d = din("gf", [128, D])
        y_d = nc.dram_tensor("y", [2048, D], F32, kind="ExternalOutput").ap()
        scr = nc.dram_tensor("scr", [8, 768], F32)
        if dbg:
            dbg_d = nc.dram_tensor("dbg", [2048, D], F32, kind="ExternalOutput").ap()

        S = Sched(nc, es)
        I = S.I
        x2 = S.sb("x2", [128, 16, D])
        cst = S.sb("cst", [128, CSTW])
        ccst = S.sb("ccst", [128, 512])
        prm = S.sb("prm", [128, 32])
        esink = S.sb("esink", [128, 8])
        S.make_arena(35200)
        banks = [S.ps("bk%d" % i) for i in range(8)]
        T0, T1, P0, T2, SA, SB, S2, OB = banks
        BN = {id(b): "bk%d" % i for i, b in enumerate(banks)}

        ident = cst[:, C_ID:C_ID + 128]
        jmat = cst[:, C_J:C_J + 128]
        tri = cst[:, C_TRI:C_TRI + 128]
        ones = cst[:, C_ONE:C_ONE + 128]
        iota = cst[:, C_IOTA:C_IOTA + 16]
        iota16 = cst[:, C_IOTA16:C_IOTA16 + 16]
        foxmask = ccst[:, 0:256].rearrange("p (a b) -> p a b", a=2)
        mab = ccst[:, 256:512].rearrange("p (a b) -> p a b", a=2)

        S.dma('sp', cst[:], cst_d, writes=['cst'])
        S.dma('sp', ccst[:], ccst_d, writes=['ccst'])
        S.dma('sp', prm[:], prm_d, writes=['prm'])
        for m in range(16):
            S.dma('sp', x2[:, m, :], xo_d[m * 128:(m + 1) * 128, :], writes=[('x2', m)])

        def rms_rstd(src_ap, src_name, ssq, std, rstd, junk, tag, n):
            I('act', 'activation', out=junk, in_=src_ap, func=AF.Square, accum_out=ssq,
              R=[src_name], W=['junk', 'ssq' + tag])
            I('act', 'activation', out=std, in_=ssq, func=AF.Sqrt, bias=EPS, scale=1.0 / n,
              R=['ssq' + tag], W=['std' + tag])
            I('dve', 'reciprocal', rstd, std, R=['std' + tag], W=['rstd' + tag])

        def transpose8(src_ap, src_name, dstT, dst_name):
            for c in range(8):
                bk = T0 if c < 4 else T1
                I('pe', 'transpose', bk[:, (c % 4) * 128:(c % 4 + 1) * 128], src_ap[:, c * 128:(c + 1) * 128], ident,
                  R=[src_name, 'cst'], W=[BN[id(bk)]])
            I('act', 'copy', dstT[:, 0:4, :].rearrange("p a b -> p (a b)"), T0[:, :], R=['bk0'], W=[dst_name + 'a'])
            I('dve', 'tensor_copy', dstT[:, 4:8, :].rearrange("p a b -> p (a b)"), T1[:, :], R=['bk1'], W=[dst_name + 'b'])

        if attn:
            I('act', 'activation', out=esink[:], in_=prm[:, P_SK:P_SK + 8], func=AF.Exp, R=['prm'], W=['esink'])
            mark0 = S.aoff
            relaug = S.av([33, 8])
            oht = S.av([33, 768])
            tt = S.av([8, 768])
            S.dma('sp', relaug[0:32, :], relb_d, writes=['relaug'])
            I('dve', 'memset', relaug[32:33, :], 1.0, W=['relaug1'])
            S.dma('sp', oht, oht_d, writes=['oht'])
            I('pe', 'matmul', P0[0:8, 0:384], relaug, oht[:, 0:384], start=True, stop=True,
              R=['relaug', 'relaug1', 'oht'], W=['bk2'])
            I('pe', 'matmul', T2[0:8, 0:384], relaug, oht[:, 384:768], start=True, stop=True,
              R=['relaug', 'relaug1', 'oht'], W=['bk3'])
            I('act', 'activation', out=tt[:, 0:384], in_=P0[0:8, 0:384], func=AF.Copy, scale=8.0, R=['bk2'], W=['tt0'])
            I('act', 'activation', out=tt[:, 384:768], in_=T2[0:8, 0:384], func=AF.Copy, scale=8.0, R=['bk3'], W=['tt1'])
            S.dma('sp', scr.ap(), tt, reads=['tt0', 'tt1'], writes=['scr'])
            S.barrier()

            for p in range(npass if stage >= 1 else 0):
                S.aoff = mark0
                KTa = S.av([128, 4096])
                KTb = S.av([128, 4096])
                VaF = S.av([128, 32 * 65 + 31])
                Va = VaF[:, 0:32 * 65].rearrange("p (a b) -> p a b", a=32)
                vpad_a = VaF[:, 32 * 65:32 * 65 + 31]
                VbF = S.av([128, 64 * 65 + 31])
                Vb = VbF[:, 0:64 * 65].rearrange("p (a b c) -> p a b c", a=32, b=2)
                vpad_b = VbF[:, 64 * 65:64 * 65 + 31]
                biasT = S.av([128, 16, 32, 2])
                R_ = S.av([128, 6, 128])
                fz = S.av([128, 32, 2])
                markp = S.aoff
                for e_ in range(2):
                    for c in range(3):
                        hk = bass.AP(tensor=scr.ap().tensor, offset=(2 * p + e_) * 768 + c * 256,
                                     ap=[[1, 128], [1, 128]])
                        S.dma('sp', R_[:, e_ * 3 + c, :], hk, reads=['scr'], writes=[('R', e_, c)])
                I('pool', 'memset', Va[:, :, 64:65], 1.0, W=['Va1'])
                I('pool', 'memset', Vb[:, :, :, 64:65], 1.0, W=['Vb1'])
                I('pool', 'memset', vpad_a, 0.0, W=['vpa'])
                I('pool', 'memset', vpad_b, 0.0, W=['vpb'])

                def load_w(w_dram, ncols, name):
                    w_sb = S.av([128, 8, ncols])
                    S.dma('sp', w_sb, w_dram.rearrange("(c p) n -> p c n", p=128), writes=[name])
                    for c in range(8):
                        if c % 2 == 0:
                            I('dve', 'tensor_scalar', out=w_sb[:, c, :], in0=w_sb[:, c, :],
                              scalar1=prm[:, P_GM + c:P_GM + c + 1], scalar2=None, op0=ALU.mult,
                              R=[name, 'prm'], W=[(name, c)])
                        else:
                            I('act', 'activation', out=w_sb[:, c, :], in_=w_sb[:, c, :], func=AF.Copy,
                              scale=prm[:, P_GM + c:P_GM + c + 1], R=[name, 'prm'], W=[(name, c)])
                    return w_sb

                def tile_phase(src_dram, srcT_dram, ntiles, w_sb, wname, ncols, gcol0, dstTa, dstTb, dna, dnb, kv, p=p,
                               Va=Va, Vb=Vb, fz=fz):
                    xt = [S.av([128, D]) for _ in range(3)]
                    xT = [S.av([128, 8, 128]) for _ in range(3)]
                    junk = S.av([128, D], BF16)
                    kq = [S.av([128, 256]) for _ in range(2)]
                    sq = S.av([128, 256])
                    kn = [S.av([128, 256]) for _ in range(2)]
                    ssq = [S.av([128, 1]) for _ in range(3)]
                    std = [S.av([128, 1]) for _ in range(3)]
                    rstd = [S.av([128, 1]) for _ in range(3)]
                    ssq4 = S.av([128, 4]); std4 = S.av([128, 4]); r4 = S.av([128, 4])
                    pbk = [P0, SA]

                    def s1(j):
                        S.dma('sp', xt[j % 3], src_dram[j * 128:(j + 1) * 128, :], writes=[('xt', j % 3)])
                        S.dma('sp', xT[j % 3], srcT_dram[:, j * 128:(j + 1) * 128].rearrange("(c p) t -> p c t", p=128),
                              writes=[('xT', j % 3)])

                    def s2(j):
                        a = j % 3
                        rms_rstd(xt[a], ('xt', a), ssq[a], std[a], rstd[a], junk, 'A%d' % a, D)

                    def s3(j):
                        sl = j % 2
                        bk = pbk[sl]
                        bn = BN[id(bk)]
                        for c in range(8):
                            I('pe', 'matmul', bk[:, 0:ncols], xT[j % 3][:, c, :], w_sb[:, c, :], start=(c == 0), stop=(c == 7),
                              R=[('xT', j % 3), (wname, c)], W=[bn])
                        rs = rstd[j % 3][:, 0:1]
                        rsn = 'rstdA%d' % (j % 3)
                        I('act', 'activation', out=kq[sl], in_=bk[:, 0:256], func=AF.Copy, scale=rs, R=[bn, rsn], W=[('kq', sl)])
                        if kv:
                            I('act', 'activation', out=Va[:, j, 0:64], in_=bk[:, 256:320], func=AF.Copy, scale=rs,
                              R=[bn, rsn, 'Va1'], W=[('Va', j)])
                            I('act', 'activation', out=Vb[:, j, :, 0:64],
                              in_=bk[:, 320:448].rearrange("p (a b) -> p a b", a=2), func=AF.Copy, scale=rs,
                              R=[bn, rsn, 'Vb1'], W=[('Vb', j)])
                            I('dve', 'scalar_tensor_tensor', out=fz[:, j, :], in0=bk[:, 448:450], scalar=rs,
                              in1=prm[:, P_BF + 2 * p:P_BF + 2 * p + 2], op0=ALU.mult, op1=ALU.add,
                              R=[bn, rsn, 'prm'], W=[('fz', j)])

                    def s4(j):
                        sl = j % 2
                        I('dve', 'tensor_tensor', out=sq, in0=kq[sl], in1=kq[sl], op=ALU.mult, R=[('kq', sl)], W=['sq'])
                        I('dve', 'tensor_reduce', out=ssq4, in_=sq.rearrange("p (a b) -> p a b", a=4), axis=AX.X, op=ALU.add,
                          R=['sq'], W=['ssq4'])
                        I('act', 'activation', out=std4, in_=ssq4, func=AF.Sqrt, bias=EPS, scale=1.0 / 64, R=['ssq4'], W=['std4'])
                        I('dve', 'reciprocal', r4, std4, R=['std4'], W=['r4'])
                        I('dve', 'tensor_tensor', out=kn[sl].rearrange("p (a b) -> p a b", a=4),
                          in0=kq[sl].rearrange("p (a b) -> p a b", a=4), in1=r4.unsqueeze(2).to_broadcast([128, 4, 64]),
                          op=ALU.mult, R=[('kq', sl), 'r4'], W=[('kn', sl)])

                    def s5(j):
                        sl = j % 2
                        I('pe', 'transpose', T2[:, 0:128], kn[sl][:, 0:128], ident, R=[('kn', sl), 'cst'], W=['bk3'])
                        I('pe', 'transpose', T2[:, 128:256], kn[sl][:, 128:256], ident, R=[('kn', sl), 'cst'], W=['bk3'])
                        if kv:
                            I('act', 'activation', out=dstTa[:, j * 128:(j + 1) * 128], in_=T2[:, 0:128], func=AF.Copy,
                              scale=prm[:, gcol0:gcol0 + 1], R=['bk3', 'prm'], W=[(dna, j)])
                            I('act', 'activation', out=dstTb[:, j * 128:(j + 1) * 128], in_=T2[:, 128:256], func=AF.Copy,
                              scale=prm[:, gcol0 + 1:gcol0 + 2], R=['bk3', 'prm'], W=[(dnb, j)])
                        else:
                            for (dst, dn, c0_, gc_) in ((dstTa, dna, 0, gcol0), (dstTb, dnb, 128, gcol0 + 1)):
                                for e_ in range(2):
                                    pr = slice(64 * e_, 64 * e_ + 64)
                                    I('act', 'activation', out=dst[e_][pr, j * 128:(j + 1) * 128], in_=T2[pr, c0_:c0_ + 128],
                                      func=AF.Copy, scale=prm[pr, gc_:gc_ + 1], R=['bk3', 'prm', 'qzero'], W=[(dn, j)])

                    s1(0)
                    if ntiles > 1:
                        s1(1)
                    s2(0)
                    for i in range(ntiles + 1):
                        if i + 2 < ntiles:
                            s1(i + 2)
                        if i + 1 < ntiles:
                            s2(i + 1)
                        if i < ntiles:
                            s3(i)
                            s4(i)
                        if i >= 1:
                            s5(i - 1)

                w_sb = load_w(wkv_d[p], WKV, 'wkv')
                tile_phase(xf_d, xfT_d, nkb, w_sb, 'wkv', WKV, P_GC, KTa, KTb, 'KTa', 'KTb', True)
                nlf = S.av([128, 32, 2]); ex = S.av([128, 32, 2])
                fw = S.av([128, 160]); cpref = S.av([128, 32, 2]); gall = S.av([128, 32, 2]); gc = S.av([128, 16, 2])
                fzn = [('fz', j) for j in range(nkb)]
                if nkb < 32:
                    I('dve', 'memset', fz[:, nkb:32, :], 1.0, W=['fzpad'])
                    fzn.append('fzpad')
                I('act', 'activation', out=ex, in_=fz, func=AF.Exp, scale=-1.0, R=fzn, W=['ex'])
                I('act', 'activation', out=nlf, in_=ex, func=AF.Ln, bias=1.0, scale=1.0, R=['ex'], W=['nlf'])
                nlf2 = nlf.rearrange("p a b -> p (a b)")
                I('pe', 'matmul', P0[:, 0:64], tri, nlf2, start=True, stop=True, R=['nlf', 'cst'], W=['bk2'])
                I('pe', 'matmul', P0[:, 64:128], ones, nlf2, start=True, stop=True, R=['nlf', 'cst'], W=['bk2'])
                I('pe', 'matmul', P0[:, 128:160].rearrange("p (a b) -> p a b", a=16), mab[:, 0, :], nlf[:, 0:32:2, :],
                  start=True, stop=False, R=['nlf', 'ccst'], W=['bk2'])
                I('pe', 'matmul', P0[:, 128:160].rearrange("p (a b) -> p a b", a=16), mab[:, 1, :], nlf[:, 1:32:2, :],
                  start=False, stop=True, R=['nlf', 'ccst'], W=['bk2'])
                I('dve', 'tensor_copy', fw, P0[:, 0:160], R=['bk2'], W=['fw'])
                Wv = fw[:, 0:64].rearrange("p (a b) -> p a b", a=32)
                BS = fw[:, 64:128].rearrange("p (a b) -> p a b", a=32)
                PM = fw[:, 128:160].rearrange("p (a b) -> p a b", a=16)
                I('dve', 'memset', cpref[:, 0, :], 0.0, W=['cpref'])
                for j in range(1, 32):
                    I('dve', 'tensor_tensor', out=cpref[:, j, :], in0=cpref[:, j - 1, :], in1=BS[:, j - 1, :], op=ALU.add,
                      R=['cpref', 'fw'], W=['cpref'])
                I('dve', 'tensor_tensor', out=gall, in0=Wv, in1=cpref, op=ALU.add, R=['fw', 'cpref'], W=['gall'])
                I('dve', 'tensor_tensor', out=gc, in0=cpref[:, 0:32:2, :], in1=PM, op=ALU.add, R=['fw', 'cpref'], W=['gc'])
                for m in range(16):
                    nj = 2 * m + 2
                    I('dve', 'tensor_tensor', out=biasT[:, m, 0:nj, :], in0=gall[:, 0:nj, :],
                      in1=gc[:, m, :].unsqueeze(1).to_broadcast([128, nj, 2]), op=ALU.subtract,
                      R=['gall', 'gc'], W=['biasT'])
                S.barrier()
                if stage < 2:
                    continue
                S.aoff = markp
                QTa = [S.av([128, 2048]) for _ in range(2)]
                QTb = [S.av([128, 2048]) for _ in range(2)]
                markq = S.aoff
                for qq in (QTa, QTb):
                    I('pool', 'memset', qq[0][64:128, :], 0.0, W=['qzero'])
                    I('pool', 'memset', qq[1][0:64, :], 0.0, W=['qzero'])
                w_sb = load_w(wq_d[p], 256, 'wq')
                tile_phase(xo_d, xoT_d, 16, w_sb, 'wq', 256, P_GC + 2, QTa, QTb, 'QTa', 'QTb', False)
                S.barrier()
                if stage < 3:
                    continue
                S.aoff = markq
                wo_sb = S.av([128, 2, D])
                S.dma('sp', wo_sb, wo_d[p].rearrange("(r p) n -> p r n", p=128), writes=['wo'])
                NPT = 3
                PT = [S.av([128, 4, 128]) for _ in range(NPT)]
                PT2 = [S.av([128, 3, 128]) for _ in range(2)]
                mixp = [S.av([128, 256]) for _ in range(2)]
                mixT = S.av([128, 2, 128])
                rinv = [S.av([128, 4]) for _ in range(2)]
                den = [S.av([128, 2]) for _ in range(2)]
                O4 = OB[:, 0:384].rearrange("p (a b) -> p a b", a=4)
                sbanks = [P0, SA, SB]
                items = []
                for m in range(16):
                    nj = 2 * m + 2
                    for e_ in range(2):
                        for g0 in range(0, nj, 4):
                            items.append(('fox', m, e_, list(range(g0, min(g0 + 4, nj)))))
                    for e_ in range(2):
                        items.append(('swa', m, e_, [c for c in range(3) if 0 <= 2 * m - 1 + c < 32]))
                    items.append(('end', m, 0, []))
                fcount = [0]
                scount = [0]
                slot_of = {}

                def scores(idx):
                    kind, m, e_, grp = items[idx]
                    qs = slice(m * 128, (m + 1) * 128)
                    pr = slice(64 * e_, 64 * e_ + 64)
                    if kind == 'fox':
                        sl = fcount[0] % NPT
                        fcount[0] += 1
                        slot_of[idx] = sl
                        bk = sbanks[sl]
                        bn = BN[id(bk)]
                        for jj, j in enumerate(grp):
                            diag = j >= 2 * m
                            I('pe', 'matmul', bk[:, jj * 128:(jj + 1) * 128], KTb[:, j * 128:(j + 1) * 128], QTb[e_][:, qs],
                              start=True, stop=(not diag), R=[('KTb', j), ('QTb', m)], W=[bn])
                            if diag:
                                I('pe', 'matmul', bk[:, jj * 128:(jj + 1) * 128], ident, foxmask[:, j - 2 * m, :],
                                  start=False, stop=True, R=['cst', 'ccst'], W=[bn])
                        for jj, j in enumerate(grp):
                            I('act', 'activation', out=PT[sl][:, jj, :], in_=bk[:, jj * 128:(jj + 1) * 128], func=AF.Exp,
                              bias=biasT[:, m, j, e_:e_ + 1], scale=0.125, R=[bn, 'biasT'], W=[('PT', sl, jj)])
                    elif kind == 'swa':
                        sl = scount[0] % 2
                        scount[0] += 1
                        slot_of[idx] = sl
                        for c in grp:
                            j = 2 * m - 1 + c
                            I('pe', 'matmul', S2[:, c * 128:(c + 1) * 128], KTa[:, j * 128:(j + 1) * 128], QTa[e_][:, qs],
                              start=True, stop=False, R=[('KTa', j), ('QTa', m)], W=['bk6'])
                            I('pe', 'matmul', S2[:, c * 128:(c + 1) * 128], jmat, R_[:, e_ * 3 + c, :],
                              start=False, stop=True, R=['cst', ('R', e_, c)], W=['bk6'])
                        c0 = grp[0]
                        I('act', 'activation', out=PT2[sl][:, c0:3, :].rearrange("p a b -> p (a b)"), in_=S2[:, c0 * 128:384],
                          func=AF.Exp, scale=0.125, R=['bk6'], W=[('PT2', sl)])

                def pvs(idx):
                    kind, m, e_, grp = items[idx]
                    par = m % 2
                    if kind == 'fox':
                        sl = slot_of[idx]
                        nj = 2 * m + 2
                        for jj, j in enumerate(grp):
                            I('pe', 'matmul', O4[:, e_, :], PT[sl][:, jj, :], VbF[:, (2 * j + e_) * 65:(2 * j + e_) * 65 + 96],
                              start=(j == 0), stop=(j == nj - 1),
                              R=[('PT', sl, jj), ('Vb', j), ('Vb', min(j + 1, 31)), 'Vb1', 'vpb'], W=['bk7'])
                    elif kind == 'swa':
                        sl = slot_of[idx]
                        for c in grp:
                            j = 2 * m - 1 + c
                            I('pe', 'matmul', O4[:, 2 + e_, :], PT2[sl][:, c, :], VaF[:, j * 65:j * 65 + 96], start=(c == grp[0]), stop=(c == 2),
                              R=[('PT2', sl), ('Va', j), ('Va', min(j + 1, 31)), 'Va1', 'vpa'], W=['bk7'])
                    else:
                        for e_ in range(2):
                            I('dve', 'reciprocal', rinv[par][:, e_:e_ + 1], O4[:, e_, 64:65], R=['bk7'], W=[('rinv', par, e_)])
                            I('dve', 'tensor_scalar', out=mixp[par][:, 128 + 64 * e_:128 + 64 * e_ + 64], in0=O4[:, e_, 0:64],
                              scalar1=rinv[par][:, e_:e_ + 1], scalar2=None, op0=ALU.mult,
                              R=['bk7', ('rinv', par, e_)], W=[('mixp', par)])
                            I('dve', 'tensor_tensor', out=den[par][:, e_:e_ + 1], in0=O4[:, 2 + e_, 64:65],
                              in1=esink[:, 2 * p + e_:2 * p + e_ + 1], op=ALU.add, R=['bk7', 'esink'], W=[('den', par, e_)])
                            I('dve', 'reciprocal', rinv[par][:, 2 + e_:3 + e_], den[par][:, e_:e_ + 1],
                              R=[('den', par, e_)], W=[('rinv', par, 2 + e_)])
                            I('dve', 'tensor_scalar', out=mixp[par][:, 64 * e_:64 * e_ + 64], in0=O4[:, 2 + e_, 0:64],
                              scalar1=rinv[par][:, 2 + e_:3 + e_], scalar2=None, op0=ALU.mult,
                              R=['bk7', ('rinv', par, 2 + e_)], W=[('mixp', par)])

                def outproj_t(m):
                    par = m % 2
                    I('pe', 'transpose', T2[:, 0:128], mixp[par][:, 0:128], ident, R=[('mixp', par), 'cst'], W=['bk3'])
                    I('pe', 'transpose', T2[:, 128:256], mixp[par][:, 128:256], ident, R=[('mixp', par), 'cst'], W=['bk3'])
                    I('act', 'copy', mixT.rearrange("p a b -> p (a b)"), T2[:, 0:256], R=['bk3'], W=['mixT'])

                def outproj_w(m):
                    for hf, bk in ((0, T0), (1, T1)):
                        for r in range(2):
                            I('pe', 'matmul', bk[:, :], mixT[:, r, :], wo_sb[:, r, hf * 512:(hf + 1) * 512],
                              start=(r == 0), stop=(r == 1), R=['mixT', 'wo'], W=[BN[id(bk)]])
                        I('dve', 'tensor_tensor', out=x2[:, m, hf * 512:(hf + 1) * 512], in0=x2[:, m, hf * 512:(hf + 1) * 512],
                          in1=bk[:, :], op=ALU.add, R=[BN[id(bk)], ('x2', m)], W=[('x2', m)])

                LAG = 2
                pending = []
                n_items = len(items)
                for idx in range(n_items + LAG):
                    if idx < n_items:
                        scores(idx)
                    k = idx - LAG
                    if 0 <= k < n_items:
                        pvs(k)
                        if items[k][0] == 'end':
                            pending.append((idx + 2, outproj_t, items[k][1]))
                            pending.append((idx + 4, outproj_w, items[k][1]))
                    for it in [q for q in pending if q[0] <= idx]:
                        it[1](it[2])
                        pending.remove(it)
                for it in sorted(pending, key=lambda q: q[0]):
                    it[1](it[2])
                S.barrier()
            S.aoff = mark0

        if dbg:
            for m in range(16):
                S.dma('sp', dbg_d[m * 128:(m + 1) * 128, :], x2[:, m, :], reads=[('x2', m)], out=True, sem='d_dbg%d' % m)
            S.barrier()

        if peer:
            S.aoff = 0
            eidx = S.av([128, 16, 128], I32)
            gate = S.av([128, 16, 128])
            gf = S.av([128, D])
            xn = S.av([128, D])
            junk = S.av([128, D], BF16)
            ssq = S.av([128, 1]); std = S.av([128, 1]); rstd = S.av([128, 1])
            mark1 = S.aoff
            S.dma('sp', gf, gf_d, writes=['gf'])

            def norm_tile(t):
                rms_rstd(x2[:, t, :], ('x2', t), ssq, std, rstd, junk, 'P', D)
                I('dve', 'scalar_tensor_tensor', out=xn, in0=x2[:, t, :], scalar=rstd[:, 0:1], in1=gf,
                  op0=ALU.mult, op1=ALU.mult, R=[('x2', t), 'rstdP', 'gf'], W=['xn'])

            skn = S.av([128, 2, 128]); skT = S.av([128, 2, 128])
            S.dma('sp', skn, sk_d.rearrange("a n d -> n a d"), writes=['skn'])
            for a in range(2):
                I('pe', 'transpose', T2[:, a * 128:(a + 1) * 128], skn[:, a, :], ident, R=['skn', 'cst'], W=['bk3'])
            I('act', 'copy', skT.rearrange("p a b -> p (a b)"), T2[:, 0:256], R=['bk3'], W=['skT'])
            wq_sb = S.av([128, 8, 1024])
            xnT = S.av([128, 8, 128])
            qvT = S.av([128, 8, 128])
            sD = [S.av([128, 8, 128]) for _ in range(2)]; s2 = S.av([128, 8, 128])
            vals = S.av([128, 8, 16]); idxu = S.av([128, 8, 16], U32); idxf = S.av([128, 8, 16])
            cand = S.av([128, 4, 256]); cand2 = S.av([128, 4, 256])
            ts = S.av([128, 4, 16]); pos = S.av([128, 4, 16], U32); posf = S.av([128, 4, 16])
            ge4 = S.av([128, 4, 16, 16]); oh4 = S.av([128, 4, 16, 16])
            af = S.av([128, 4, 16]); bfv = S.av([128, 4, 16]); sel1 = S.av([128, 4, 16]); sel2 = S.av([128, 4, 16])
            eidf = S.av([128, 4, 16]); tsm = S.av([128, 4, 16]); gex = S.av([128, 4, 16])
            gs = S.av([128, 4]); rg = S.av([128, 4])
            v4 = vals.rearrange("p (h a) k -> p h a k", a=2)
            i4 = idxf.rearrange("p (h a) k -> p h a k", a=2)
            qbanks = [P0, SA]
            B4 = [128, 4, 16, 16]
            for hh in range(2):
                for c in range(8):
                    S.dma('sp', wq_sb[:, c, :], wqy_d[c * 128:(c + 1) * 128, hh * 1024:(hh + 1) * 1024], writes=[('wqy', c)])
                def p1_front(t, hh=hh):
                        norm_tile(t)
                        transpose8(xn, 'xn', xnT, 'xnT')
                        for g in range(2):
                            bk = qbanks[g]
                            for cq in range(4):
                                cc = 4 * g + cq
                                for c in range(8):
                                    I('pe', 'matmul', bk[:, cq * 128:(cq + 1) * 128], wq_sb[:, c, cc * 128:(cc + 1) * 128],
                                      xnT[:, c, :], start=(c == 0), stop=(c == 7),
                                      R=[('wqy', c), 'xnTa', 'xnTb'], W=[BN[id(bk)]])
                            I('act', 'copy', qvT[:, 4 * g:4 * g + 4, :].rearrange("p a b -> p (a b)"), bk[:, :],
                              R=[BN[id(bk)]], W=[('qvT', g)])
                        for g in range(2):
                            bk = [S2, OB][g]
                            for cq in range(4):
                                cc = 4 * g + cq
                                I('pe', 'matmul', bk[:, cq * 128:(cq + 1) * 128], qvT[:, cc, :], skT[:, cc % 2, :],
                                  start=True, stop=True, R=[('qvT', g), 'skT'], W=[BN[id(bk)]])
                            I('act', 'copy', sD[t % 2][:, 4 * g:4 * g + 4, :].rearrange("p a b -> p (a b)"), bk[:, :],
                              R=[BN[id(bk)]], W=[('s', t % 2, g)])

                def p1_chain(t, hh=hh):
                        for cc in range(8):
                            I('dve', 'max', vals[:, cc, 0:8], sD[t % 2][:, cc, :], R=[('s', t % 2, cc // 4)], W=[('v', cc, 0)])
                        for cc in range(8):
                            I('dve', 'max_index', idxu[:, cc, 0:8], vals[:, cc, 0:8], sD[t % 2][:, cc, :],
                              R=[('s', t % 2, cc // 4), ('v', cc, 0)], W=[('i', cc, 0)])
                        for cc in range(8):
                            I('dve', 'match_replace', s2[:, cc, :], vals[:, cc, 0:8], sD[t % 2][:, cc, :], -1e30,
                              R=[('s', t % 2, cc // 4), ('v', cc, 0)], W=[('s2', cc)])
                        for cc in range(8):
                            I('dve', 'max', vals[:, cc, 8:16], s2[:, cc, :], R=[('s2', cc)], W=[('v', cc, 1)])
                        for cc in range(8):
                            I('dve', 'max_index', idxu[:, cc, 8:16], vals[:, cc, 8:16], s2[:, cc, :],
                              R=[('s2', cc), ('v', cc, 1)], W=[('i', cc, 1)])
                        alli = [('i', cc, k) for cc in range(8) for k in range(2)]
                        allv = [('v', cc, k) for cc in range(8) for k in range(2)]
                        I('dve', 'tensor_copy', idxf, idxu, R=alli, W=['idxf'])
                        I('dve', 'tensor_tensor', out=cand.rearrange("p h (a b) -> p h a b", a=16),
                          in0=v4[:, :, 0, :].unsqueeze(3).to_broadcast(B4), in1=v4[:, :, 1, :].unsqueeze(2).to_broadcast(B4),
                          op=ALU.add, R=allv, W=['cand'])
                        for h in range(4):
                            I('dve', 'max', ts[:, h, 0:8], cand[:, h, :], R=['cand'], W=[('ts', h, 0)])
                        for h in range(4):
                            I('dve', 'max_index', pos[:, h, 0:8], ts[:, h, 0:8], cand[:, h, :],
                              R=['cand', ('ts', h, 0)], W=[('pos', h, 0)])
                        for h in range(4):
                            I('dve', 'match_replace', cand2[:, h, :], ts[:, h, 0:8], cand[:, h, :], -1e30,
                              R=['cand', ('ts', h, 0)], W=[('cand2', h)])
                        for h in range(4):
                            I('dve', 'max', ts[:, h, 8:16], cand2[:, h, :], R=[('cand2', h)], W=[('ts', h, 1)])
                        for h in range(4):
                            I('dve', 'max_index', pos[:, h, 8:16], ts[:, h, 8:16], cand2[:, h, :],
                              R=[('cand2', h), ('ts', h, 1)], W=[('pos', h, 1)])
                        allp = [('pos', h, k) for h in range(4) for k in range(2)]
                        allt = [('ts', h, k) for h in range(4) for k in range(2)]
                        I('dve', 'tensor_copy', posf, pos, R=allp, W=['posf'])
                        I('dve', 'tensor_tensor', out=ge4, in0=posf.unsqueeze(3).to_broadcast(B4),
                          in1=iota16.unsqueeze(1).unsqueeze(1).to_broadcast(B4), op=ALU.is_ge, R=['posf', 'cst'], W=['ge4'])
                        I('dve', 'tensor_reduce', out=af, in_=ge4, axis=AX.X, op=ALU.add, R=['ge4'], W=['af'])
                        I('dve', 'tensor_scalar', out=af, in0=af, scalar1=-1.0, scalar2=None, op0=ALU.add, R=['af'], W=['af'])
                        I('dve', 'scalar_tensor_tensor', out=bfv, in0=af, scalar=-16.0, in1=posf, op0=ALU.mult, op1=ALU.add,
                          R=['af', 'posf'], W=['bfv'])
                        for (src, ii, dst, nm) in ((af, 0, sel1, 'sel1'), (bfv, 1, sel2, 'sel2')):
                            I('dve', 'tensor_tensor', out=oh4, in0=iota.unsqueeze(1).unsqueeze(1).to_broadcast(B4),
                              in1=src.unsqueeze(3).to_broadcast(B4), op=ALU.is_equal, R=['af', 'bfv', 'cst'], W=['oh4'])
                            I('dve', 'tensor_tensor', out=oh4, in0=oh4, in1=i4[:, :, ii, :].unsqueeze(2).to_broadcast(B4),
                              op=ALU.mult, R=['oh4', 'idxf'], W=['oh4'])
                            I('dve', 'tensor_reduce', out=dst, in_=oh4, axis=AX.X, op=ALU.add, R=['oh4'], W=[nm])
                        I('dve', 'scalar_tensor_tensor', out=eidf, in0=sel1, scalar=128.0, in1=sel2, op0=ALU.mult, op1=ALU.add,
                          R=['sel1', 'sel2'], W=['eidf'])
                        I('dve', 'tensor_copy', eidx[:, t, hh * 64:(hh + 1) * 64], eidf.rearrange("p a b -> p (a b)"),
                          R=['eidf'], W=[('eidx', t, hh)])
                        I('dve', 'tensor_tensor', out=tsm, in0=ts, in1=ts[:, :, 0:1].to_broadcast([128, 4, 16]), op=ALU.subtract,
                          R=allt, W=['tsm'])
                        I('act', 'activation', out=gex, in_=tsm, func=AF.Exp, R=['tsm'], W=['gex'])
                        I('dve', 'tensor_reduce', out=gs, in_=gex, axis=AX.X, op=ALU.add, R=['gex'], W=['gs'])
                        I('dve', 'reciprocal', rg, gs, R=['gs'], W=['rg'])
                        I('dve', 'tensor_tensor', out=gate[:, t, hh * 64:(hh + 1) * 64].rearrange("p (a b) -> p a b", a=4),
                          in0=gex, in1=rg.unsqueeze(2).to_broadcast([128, 4, 16]), op=ALU.mult,
                          R=['gex', 'rg'], W=[('gate', t, hh)])

                p1_front(0)
                for t in range(NT):
                    if t + 1 < NT:
                        p1_front(t + 1)
                    p1_chain(t)
            S.barrier()
            S.aoff = mark1
            NGBL = ngb
            ring = [S.av([128, D]) for _ in range(NGBL)]
            acc1 = S.av([128, D])
            pre = S.av([128, 128]); gl = S.av([128, 128]); coef = S.av([128, 128])
            gcount = 0
            for t in range(NT if p2 else 0):
                norm_tile(t)
                en = [('eidx', t, 0), ('eidx', t, 1)]
                for sl_ in range(128):
                    r = gcount % NGBL
                    gcount += 1
                    S.dma('pool', ring[r], eu_d, reads=en, writes=[('ring', r)],
                          indirect=bass.IndirectOffsetOnAxis(ap=eidx[:, t, sl_:sl_ + 1], axis=0))
                    if nodve:
                        I('dve', 'tensor_copy', pre[:, sl_:sl_ + 1], ring[r][:, 0:1], R=[('ring', r), 'xn'], W=[('pre', sl_)])
                    else:
                        I('dve', 'scalar_tensor_tensor', out=ring[r], in0=ring[r], scalar=1.0, in1=xn, op0=ALU.mult, op1=ALU.mult,
                          accum_out=pre[:, sl_:sl_ + 1], R=['xn'], W=[('ring', r), ('pre', sl_)])
                I('act', 'activation', out=gl, in_=pre, func=AF.Gelu, R=[('pre', k) for k in range(128)], W=['gl'])
                I('dve', 'tensor_tensor', out=coef, in0=gl, in1=gate[:, t, :], op=ALU.mult,
                  R=['gl', ('gate', t, 0), ('gate', t, 1)], W=['coef'])
                for sl_ in range(128):
                    r = gcount % NGBL
                    gcount += 1
                    S.dma('pool', ring[r], ev_d, reads=en, writes=[('ring', r)],
                          indirect=bass.IndirectOffsetOnAxis(ap=eidx[:, t, sl_:sl_ + 1], axis=0))
                    if nodve:
                        I('dve', 'tensor_copy', acc1[:, 0:1], ring[r][:, 0:1], R=[('ring', r), 'coef'], W=['acc1'])
                    elif sl_ == 1:
                        I('dve', 'tensor_scalar', out=acc1, in0=ring[r], scalar1=coef[:, sl_:sl_ + 1], scalar2=None,
                          op0=ALU.mult, R=[('ring', r), 'coef'], W=['acc1'])
                    elif sl_ % 2 == 1:
                        I('dve', 'scalar_tensor_tensor', out=acc1, in0=ring[r], scalar=coef[:, sl_:sl_ + 1], in1=acc1,
                          op0=ALU.mult, op1=ALU.add, R=[('ring', r), 'coef', 'acc1'], W=['acc1'])
                    else:
                        I('dve', 'scalar_tensor_tensor', out=x2[:, t, :], in0=ring[r], scalar=coef[:, sl_:sl_ + 1],
                          in1=x2[:, t, :], op0=ALU.mult, op1=ALU.add, R=[('ring', r), 'coef', ('x2', t)], W=[('x2', t)])
                I('dve', 'tensor_tensor', out=x2[:, t, :], in0=x2[:, t, :], in1=acc1, op=ALU.add,
                  R=['acc1', ('x2', t)], W=[('x2', t)])
                S.dma('sp', y_d[t * 128:(t + 1) * 128, :], x2[:, t, :], reads=[('x2', t)], out=True, sem='d_y%d' % t)
        else:
            for t in range(NT):
                S.dma('sp', y_d[t * 128:(t + 1) * 128, :], x2[:, t, :], reads=[('x2', t)], out=True, sem='d_y%d' % t)
        S.finish()
    return nc


def _t5_bucket_table():
    d = np.arange(256)
    nf = np.maximum(d, 1).astype(np.float32)
    large = 16 + (np.log(nf / np.float32(16)) / np.float32(math.log(128 / 16)) * np.float32(16)).astype(np.int32)
    large = np.minimum(large, 31)
    return np.where(d < 16, d, large)


def _consts():
    cst = np.zeros((128, CSTW), np.float32)
    cst[:, C_ID:C_ID + 128] = np.eye(128)
    cst[:, C_J:C_J + 128] = np.eye(128)[::-1]
    cst[:, C_TRI:C_TRI + 128] = np.triu(np.ones((128, 128)))
    cst[:, C_ONE:C_ONE + 128] = 1.0
    cst[:, C_IOTA:C_IOTA + 16] = np.arange(16)
    cst[:, C_IOTA16:C_IOTA16 + 16] = 16 * np.arange(16)
    return cst


def _core_consts(h):
    cc = np.zeros((128, 512), np.float32)
    k = np.arange(128)[:, None]
    q = np.arange(128)[None, :]
    trimask = np.where(k > q, 8 * NEGM, 0.0).astype(np.float32)
    full = np.full((128, 128), 8 * NEGM, np.float32)
    upto = np.zeros((128, 128), np.float32)
    upto[:64, :] = 1.0
    if h == 0:
        cc[:, 0:128] = trimask
        cc[:, 128:256] = full
        cc[:, 256:384] = upto
        cc[:, 384:512] = 0.0
    else:
        cc[:, 0:128] = 0.0
        cc[:, 128:256] = trimask
        cc[:, 256:384] = 1.0
        cc[:, 384:512] = upto
    bt = _t5_bucket_table()
    oh = np.zeros((33, 768), np.float32)
    types = ['prev', 'cur', 'mask'] if h == 0 else ['mask', 'prev', 'cur']
    for c, ty in enumerate(types):
        for mp in range(256):
            col = c * 256 + mp
            if ty == 'prev':
                dist = mp + 1
            elif ty == 'cur':
                dist = mp - 127
            else:
                dist = -1
            if 0 <= dist < 128 and mp <= 254:
                oh[bt[dist], col] = 1.0
            else:
                oh[32, col] = NEGM
    return cc, oh


_NC_CACHE = {}


def prepare(inputs):
    x = np.asarray(inputs["x"], np.float32)
    w_in = np.asarray(inputs["w_in"], np.float32)[0]
    w_out = np.asarray(inputs["w_out"], np.float32)[0]
    rep = lambda v, n=128: np.ascontiguousarray(np.broadcast_to(np.asarray(v, np.float32).reshape(1, -1), (n, np.size(v))))
    wq4 = np.zeros((4, D, 256), np.float32)
    wkv4 = np.zeros((4, D, WKV), np.float32)
    wo4 = np.zeros((4, 256, D), np.float32)
    for p in range(4):
        kvh = p // 2
        qa = w_in[:, 128 * p:128 * p + 128]
        qb = w_in[:, 768 + 128 * p:768 + 128 * p + 128]
        ka = w_in[:, 512 + 64 * kvh:512 + 64 * kvh + 64]
        va = w_in[:, 640 + 64 * kvh:640 + 64 * kvh + 64]
        kb = w_in[:, 1280 + 128 * p:1280 + 128 * p + 128]
        vb = w_in[:, 1792 + 128 * p:1792 + 128 * p + 128]
        fb = w_in[:, 2304 + 2 * p:2304 + 2 * p + 2]
        wq4[p] = np.concatenate([qa, qb], axis=1)
        wkv4[p] = np.concatenate([ka, ka, kb, va, vb, fb], axis=1)
        wo4[p] = np.concatenate([w_out[128 * p:128 * p + 128], w_out[512 + 128 * p:512 + 128 * p + 128]], axis=0)
    prm = np.zeros((128, 32), np.float32)
    prm[:, P_GM:P_GM + 8] = np.asarray(inputs["norm_mix"], np.float32)[0].reshape(8, 128).T
    dup = lambda v: np.concatenate([np.asarray(v, np.float32)[0], np.asarray(v, np.float32)[0]])
    prm[:, P_GC + 0] = dup(inputs["k_norm_a"])
    prm[:, P_GC + 1] = dup(inputs["k_norm_b"])
    prm[:, P_GC + 2] = dup(inputs["q_norm_a"])
    prm[:, P_GC + 3] = dup(inputs["q_norm_b"])
    prm[:, P_BF:P_BF + 8] = rep(inputs["b_forget"])
    prm[:, P_SK:P_SK + 8] = rep(inputs["sinks"])
    cst = _consts()
    shared = {
        "cst": cst, "prm": prm, "wq4": wq4, "wkv4": wkv4, "wo4": wo4,
        "relb": np.asarray(inputs["rel_bias"], np.float32),
        "wqy": np.asarray(inputs["w_query"], np.float32)[0],
        "sk": np.stack([np.asarray(inputs["sub_keys1"], np.float32)[0], np.asarray(inputs["sub_keys2"], np.float32)[0]]),
        "eu": np.asarray(inputs["expert_u"], np.float32)[0],
        "ev": np.asarray(inputs["expert_v"], np.float32)[0],
        "gf": rep(inputs["norm_ffn"]),
    }
    in_maps = []
    for c in range(8):
        b, h = c // 2, c % 2
        xb = x[b]
        xo = np.ascontiguousarray(xb.reshape(16, 2, 128, D)[:, h].reshape(2048, D))
        cc, oh = _core_consts(h)
        m = dict(shared)
        m.update({"xo": xo, "xf": xb, "xoT": np.ascontiguousarray(xo.T), "xfT": np.ascontiguousarray(xb.T),
                  "ccst": cc, "ohtab": oh})
        in_maps.append(m)
    return in_maps


def kernel(**inputs):
    in_maps = prepare(inputs)
    if "nc" not in _NC_CACHE:
        _NC_CACHE["nc"] = build_nc()
    res = run_bass_kernel_spmd(_NC_CACHE["nc"], in_maps, core_ids=list(range(8)))
    out = np.zeros((4, 4096, D), np.float32)
    for c in range(8):
        b, h = c // 2, c % 2
        out[b].reshape(16, 2, 128, D)[:, h] = res.results[c]["y"].reshape(16, 128, D)
    return out
```

```python
import math
import numpy as np
import concourse.bass as bass
import concourse.mybir as mybir
from concourse.bass_utils import run_bass_kernel_spmd
from contextlib import ExitStack

F32 = mybir.dt.float32
BF16 = mybir.dt.bfloat16
U32 = mybir.dt.uint32
I32 = mybir.dt.int32
ALU = mybir.AluOpType
AF = mybir.ActivationFunctionType
AX = mybir.AxisListType

D = 1024
EPS = 1e-6
NEGM = -30000.0


class Sched:
    ENGS = ('pe', 'dve', 'act', 'pool', 'sp')

    def __init__(self, nc, es):
        self.nc, self.es = nc, es
        self.q = {e: [] for e in self.ENGS}
        self.cnt = {e: 0 for e in ('pe', 'dve', 'act', 'pool')}
        self.sems = {}
        for e in self.cnt:
            self.sems['c_' + e] = es.enter_context(nc.semaphore('c_' + e))
        self.waited = {e: {} for e in self.ENGS}
        self.lw = {}
        self.rd = {}
        self.dcnt = {}
        self.outdeps = {}
        self.arena = None
        self.aoff = 0
        self.AW = 0

    def sb(self, name, shape, dt=F32):
        return self.es.enter_context(self.nc.sbuf_tensor("s_" + name, list(shape), dt))

    def ps(self, name, shape=(128, 512), dt=F32):
        return self.es.enter_context(self.nc.psum_tensor("p_" + name, list(shape), dt))

    def make_arena(self, words):
        self.arena = self.sb("arena", [128, words])
        self.AW = words
        self.aoff = 0

    def av(self, shape, dt=F32):
        n = 1
        for v in shape[1:]:
            n *= v
        if dt == BF16:
            n = (n + 1) // 2
        off = self.aoff
        self.aoff += n
        assert self.aoff <= self.AW, ("arena overflow", self.aoff, self.AW)
        ap = self.arena[0:shape[0], off:off + n]
        if dt != F32:
            ap = ap.bitcast(dt)
        if len(shape) == 3:
            ap = ap.rearrange("p (a b) -> p a b", a=shape[1])
        elif len(shape) == 4:
            ap = ap.rearrange("p (a b c) -> p a b c", a=shape[1], b=shape[2])
        return ap

    def _deps(self, reads, writes):
        deps = {}

        def add(d):
            if d is not None and deps.get(d[0], 0) < d[1]:
                deps[d[0]] = d[1]
        for b in reads:
            add(self.lw.get(b))
        for b in writes:
            add(self.lw.get(b))
            for d in self.rd.get(b, ()):
                add(d)
        return deps

    def _emit_waits(self, eng, deps, skip_self=False):
        for s, v in deps.items():
            if skip_self and s == 'c_' + eng:
                continue
            if self.waited[eng].get(s, 0) < v:
                self.q[eng].append(('w', s, v))
                self.waited[eng][s] = v

    def _commit(self, tok, reads, writes):
        for b in writes:
            self.lw[b] = tok
            self.rd[b] = []
        for b in reads:
            if b not in writes:
                self.rd.setdefault(b, []).append(tok)

    def op(self, eng, fn, reads=(), writes=()):
        deps = self._deps(reads, writes)
        self._emit_waits(eng, deps, skip_self=(eng == 'pe'))
        self.cnt[eng] += 1
        tok = ('c_' + eng, self.cnt[eng])
        self.q[eng].append(('o', fn, 'c_' + eng, 1))
        self._commit(tok, reads, writes)

    def I(self, eng, meth, *args, R=(), W=(), **kw):
        self.op(eng, (meth, args, kw), R, W)

    def dma(self, eng, out_ap, in_ap, reads=(), writes=(), out=False, indirect=None, sem=None):
        if sem is None:
            sem = 'd_' + str(writes[0] if writes else reads[0]) + ('_o' if out else '')
        if sem not in self.sems:
            nm = ''.join(ch if ch.isalnum() else '_' for ch in sem)
            self.sems[sem] = self.es.enter_context(self.nc.semaphore(nm))
            self.dcnt[sem] = 0
        deps = self._deps(reads, writes)
        if self.dcnt[sem] > 0:
            v = 16 * self.dcnt[sem]
            if deps.get(sem, 0) < v:
                deps[sem] = v
        self._emit_waits(eng, deps)
        self.dcnt[sem] += 1
        tok = (sem, 16 * self.dcnt[sem])
        if indirect is not None:
            fn = lambda e: e.indirect_dma_start(out_ap, None, in_ap, indirect)
        else:
            fn = lambda e: e.dma_start(out=out_ap, in_=in_ap)
        self.q[eng].append(('o', fn, sem, 16))
        self._commit(tok, reads, writes)
        if out:
            self.outdeps[sem] = tok[1]

    def barrier(self):
        allv = {}
        for e, c in self.cnt.items():
            if c > 0:
                allv['c_' + e] = c
        for s, c in self.dcnt.items():
            if c > 0:
                allv[s] = 16 * c
        for eng in self.ENGS:
            self._emit_waits(eng, allv)
        self.lw.clear()
        self.rd.clear()

    def finish(self):
        self.barrier()
        nc = self.nc
        S = self

        def run(name, e):
            for it in S.q[name]:
                if it[0] == 'w':
                    e.wait_ge(S.sems[it[1]], it[2])
                else:
                    f = it[1]
                    ins = getattr(e, f[0])(*f[1], **f[2]) if isinstance(f, tuple) else f(e)
                    if isinstance(ins, (list, tuple)):
                        ins = ins[-1]
                    ins.then_inc(S.sems[it[2]], it[3])
        with nc.Block() as block:
            @block.tensor
            def _(e):
                run('pe', e)

            @block.vector
            def _(e):
                run('dve', e)

            @block.scalar
            def _(e):
                run('act', e)

            @block.gpsimd
            def _(e):
                run('pool', e)

            @block.sync
            def _(e):
                run('sp', e)


C_ID, C_J, C_TRI, C_ONE, C_IOTA, C_IOTA16 = 0, 128, 256, 384, 512, 528
CSTW = 544
P_GM, P_GC, P_BF, P_SK = 0, 8, 12, 20
WKV = 450
NGB = 8


def build_nc(NT=16, attn=True, peer=True, npass=4, nkb=32, dbg=False, stage=3, p2=True, nodve=False, ngb=NGB):
    nc = bass.Bass("TRN2", target_bir_lowering=False)
    es = ExitStack()
    with es:
        def din(name, shape, dt=F32):
            return nc.dram_tensor(name, list(shape), dt, kind="ExternalInput").ap()
        xo_d = din("xo", [2048, D])
        xf_d = din("xf", [4096, D])
        xoT_d = din("xoT", [D, 2048])
        xfT_d = din("xfT", [D, 4096])
        cst_d = din("cst", [128, CSTW])
        ccst_d = din("ccst", [128, 512])
        prm_d = din("prm", [128, 32])
        wq_d = din("wq4", [4, D, 256])
        wkv_d = din("wkv4", [4, D, WKV])
        wo_d = din("wo4", [4, 256, D])
        relb_d = din("relb", [32, 8])
        oht_d = din("ohtab", [33, 768])
        wqy_d = din("wqy", [D, 2048])
        sk_d = din("sk", [2, 128, 128])
        eu_d = din("eu", [16384, D])
        ev_d = din("ev", [16384, D])
        gf_d = din("gf", [128, D])
        y_d = nc.dram_tensor("y", [2048, D], F32, kind="ExternalOutput").ap()
        scr = nc.dram_tensor("scr", [8, 768], F32)
        if dbg:
            dbg_d = nc.dram_tensor("dbg", [2048, D], F32, kind="ExternalOutput").ap()

        S = Sched(nc, es)
        I = S.I
        x2 = S.sb("x2", [128, 16, D])
        cst = S.sb("cst", [128, CSTW])
        ccst = S.sb("ccst", [128, 512])
        prm = S.sb("prm", [128, 32])
        esink = S.sb("esink", [128, 8])
        S.make_arena(35200)
        banks = [S.ps("bk%d" % i) for i in range(8)]
        T0, T1, P0, T2, SA, SB, S2, OB = banks
        BN = {id(b): "bk%d" % i for i, b in enumerate(banks)}

        ident = cst[:, C_ID:C_ID + 128]
        jmat = cst[:, C_J:C_J + 128]
        tri = cst[:, C_TRI:C_TRI + 128]
        ones = cst[:, C_ONE:C_ONE + 128]
        iota = cst[:, C_IOTA:C_IOTA + 16]
        iota16 = cst[:, C_IOTA16:C_IOTA16 + 16]
        foxmask = ccst[:, 0:256].rearrange("p (a b) -> p a b", a=2)
        mab = ccst[:, 256:512].rearrange("p (a b) -> p a b", a=2)

        S.dma('sp', cst[:], cst_d, writes=['cst'])
        S.dma('sp', ccst[:], ccst_d, writes=['ccst'])
        S.dma('sp', prm[:], prm_d, writes=['prm'])
        for m in range(16):
            S.dma('sp', x2[:, m, :], xo_d[m * 128:(m + 1) * 128, :], writes=[('x2', m)])

        def rms_rstd(src_ap, src_name, ssq, std, rstd, junk, tag, n):
            I('act', 'activation', out=junk, in_=src_ap, func=AF.Square, accum_out=ssq,
              R=[src_name], W=['junk', 'ssq' + tag])
            I('act', 'activation', out=std, in_=ssq, func=AF.Sqrt, bias=EPS, scale=1.0 / n,
              R=['ssq' + tag], W=['std' + tag])
            I('dve', 'reciprocal', rstd, std, R=['std' + tag], W=['rstd' + tag])

        def transpose8(src_ap, src_name, dstT, dst_name):
            for c in range(8):
                bk = T0 if c < 4 else T1
                I('pe', 'transpose', bk[:, (c % 4) * 128:(c % 4 + 1) * 128], src_ap[:, c * 128:(c + 1) * 128], ident,
                  R=[src_name, 'cst'], W=[BN[id(bk)]])
            I('act', 'copy', dstT[:, 0:4, :].rearrange("p a b -> p (a b)"), T0[:, :], R=['bk0'], W=[dst_name + 'a'])
            I('dve', 'tensor_copy', dstT[:, 4:8, :].rearrange("p a b -> p (a b)"), T1[:, :], R=['bk1'], W=[dst_name + 'b'])

        if attn:
            I('act', 'activation', out=esink[:], in_=prm[:, P_SK:P_SK + 8], func=AF.Exp, R=['prm'], W=['esink'])
            mark0 = S.aoff
            relaug = S.av([33, 8])
            oht = S.av([33, 768])
            tt = S.av([8, 768])
            S.dma('sp', relaug[0:32, :], relb_d, writes=['relaug'])
            I('dve', 'memset', relaug[32:33, :], 1.0, W=['relaug1'])
            S.dma('sp', oht, oht_d, writes=['oht'])
            I('pe', 'matmul', P0[0:8, 0:384], relaug, oht[:, 0:384], start=True, stop=True,
              R=['relaug', 'relaug1', 'oht'], W=['bk2'])
            I('pe', 'matmul', T2[0:8, 0:384], relaug, oht[:, 384:768], start=True, stop=True,
              R=['relaug', 'relaug1', 'oht'], W=['bk3'])
            I('act', 'activation', out=tt[:, 0:384], in_=P0[0:8, 0:384], func=AF.Copy, scale=8.0, R=['bk2'], W=['tt0'])
            I('act', 'activation', out=tt[:, 384:768], in_=T2[0:8, 0:384], func=AF.Copy, scale=8.0, R=['bk3'], W=['tt1'])
            S.dma('sp', scr.ap(), tt, reads=['tt0', 'tt1'], writes=['scr'])
            S.barrier()

            for p in range(npass if stage >= 1 else 0):
                S.aoff = mark0
                KTa = S.av([128, 4096])
                KTb = S.av([128, 4096])
                VaF = S.av([128, 32 * 65 + 31])
                Va = VaF[:, 0:32 * 65].rearrange("p (a b) -> p a b", a=32)
                vpad_a = VaF[:, 32 * 65:32 * 65 + 31]
                VbF = S.av([128, 64 * 65 + 31])
                Vb = VbF[:, 0:64 * 65].rearrange("p (a b c) -> p a b c", a=32, b=2)
                vpad_b = VbF[:, 64 * 65:64 * 65 + 31]
                biasT = S.av([128, 16, 32, 2])
                R_ = S.av([128, 6, 128])
                fz = S.av([128, 32, 2])
                markp = S.aoff
                for e_ in range(2):
                    for c in range(3):
                        hk = bass.AP(tensor=scr.ap().tensor, offset=(2 * p + e_) * 768 + c * 256,
                                     ap=[[1, 128], [1, 128]])
                        S.dma('sp', R_[:, e_ * 3 + c, :], hk, reads=['scr'], writes=[('R', e_, c)])
                I('pool', 'memset', Va[:, :, 64:65], 1.0, W=['Va1'])
                I('pool', 'memset', Vb[:, :, :, 64:65], 1.0, W=['Vb1'])
                I('pool', 'memset', vpad_a, 0.0, W=['vpa'])
                I('pool', 'memset', vpad_b, 0.0, W=['vpb'])

                def load_w(w_dram, ncols, name):
                    w_sb = S.av([128, 8, ncols])
                    S.dma('sp', w_sb, w_dram.rearrange("(c p) n -> p c n", p=128), writes=[name])
                    for c in range(8):
                        if c % 2 == 0:
                            I('dve', 'tensor_scalar', out=w_sb[:, c, :], in0=w_sb[:, c, :],
                              scalar1=prm[:, P_GM + c:P_GM + c + 1], scalar2=None, op0=ALU.mult,
                              R=[name, 'prm'], W=[(name, c)])
                        else:
                            I('act', 'activation', out=w_sb[:, c, :], in_=w_sb[:, c, :], func=AF.Copy,
                              scale=prm[:, P_GM + c:P_GM + c + 1], R=[name, 'prm'], W=[(name, c)])
                    return w_sb

                def tile_phase(src_dram, srcT_dram, ntiles, w_sb, wname, ncols, gcol0, dstTa, dstTb, dna, dnb, kv, p=p,
                               Va=Va, Vb=Vb, fz=fz):
                    xt = [S.av([128, D]) for _ in range(3)]
                    xT = [S.av([128, 8, 128]) for _ in range(3)]
                    junk = S.av([128, D], BF16)
                    kq = [S.av([128, 256]) for _ in range(2)]
                    sq = S.av([128, 256])
                    kn = [S.av([128, 256]) for _ in range(2)]
                    ssq = [S.av([128, 1]) for _ in range(3)]
                    std = [S.av([128, 1]) for _ in range(3)]
                    rstd = [S.av([128, 1]) for _ in range(3)]
                    ssq4 = S.av([128, 4]); std4 = S.av([128, 4]); r4 = S.av([128, 4])
                    pbk = [P0, SA]

                    def s1(j):
                        S.dma('sp', xt[j % 3], src_dram[j * 128:(j + 1) * 128, :], writes=[('xt', j % 3)])
                        S.dma('sp', xT[j % 3], srcT_dram[:, j * 128:(j + 1) * 128].rearrange("(c p) t -> p c t", p=128),
                              writes=[('xT', j % 3)])

                    def s2(j):
                        a = j % 3
                        rms_rstd(xt[a], ('xt', a), ssq[a], std[a], rstd[a], junk, 'A%d' % a, D)

                    def s3(j):
                        sl = j % 2
                        bk = pbk[sl]
                        bn = BN[id(bk)]
                        for c in range(8):
                            I('pe', 'matmul', bk[:, 0:ncols], xT[j % 3][:, c, :], w_sb[:, c, :], start=(c == 0), stop=(c == 7),
                              R=[('xT', j % 3), (wname, c)], W=[bn])
                        rs = rstd[j % 3][:, 0:1]
                        rsn = 'rstdA%d' % (j % 3)
                        I('act', 'activation', out=kq[sl], in_=bk[:, 0:256], func=AF.Copy, scale=rs, R=[bn, rsn], W=[('kq', sl)])
                        if kv:
                            I('act', 'activation', out=Va[:, j, 0:64], in_=bk[:, 256:320], func=AF.Copy, scale=rs,
                              R=[bn, rsn, 'Va1'], W=[('Va', j)])
                            I('act', 'activation', out=Vb[:, j, :, 0:64],
                              in_=bk[:, 320:448].rearrange("p (a b) -> p a b", a=2), func=AF.Copy, scale=rs,
                              R=[bn, rsn, 'Vb1'], W=[('Vb', j)])
                            I('dve', 'scalar_tensor_tensor', out=fz[:, j, :], in0=bk[:, 448:450], scalar=rs,
                              in1=prm[:, P_BF + 2 * p:P_BF + 2 * p + 2], op0=ALU.mult, op1=ALU.add,
                              R=[bn, rsn, 'prm'], W=[('fz', j)])

                    def s4(j):
                        sl = j % 2
                        I('dve', 'tensor_tensor', out=sq, in0=kq[sl], in1=kq[sl], op=ALU.mult, R=[('kq', sl)], W=['sq'])
                        I('dve', 'tensor_reduce', out=ssq4, in_=sq.rearrange("p (a b) -> p a b", a=4), axis=AX.X, op=ALU.add,
                          R=['sq'], W=['ssq4'])
                        I('act', 'activation', out=std4, in_=ssq4, func=AF.Sqrt, bias=EPS, scale=1.0 / 64, R=['ssq4'], W=['std4'])
                        I('dve', 'reciprocal', r4, std4, R=['std4'], W=['r4'])
                        I('dve', 'tensor_tensor', out=kn[sl].rearrange("p (a b) -> p a b", a=4),
                          in0=kq[sl].rearrange("p (a b) -> p a b", a=4), in1=r4.unsqueeze(2).to_broadcast([128, 4, 64]),
                          op=ALU.mult, R=[('kq', sl), 'r4'], W=[('kn', sl)])

                    def s5(j):
                        sl = j % 2
                        I('pe', 'transpose', T2[:, 0:128], kn[sl][:, 0:128], ident, R=[('kn', sl), 'cst'], W=['bk3'])
                        I('pe', 'transpose', T2[:, 128:256], kn[sl][:, 128:256], ident, R=[('kn', sl), 'cst'], W=['bk3'])
                        if kv:
                            I('act', 'activation', out=dstTa[:, j * 128:(j + 1) * 128], in_=T2[:, 0:128], func=AF.Copy,
                              scale=prm[:, gcol0:gcol0 + 1], R=['bk3', 'prm'], W=[(dna, j)])
                            I('act', 'activation', out=dstTb[:, j * 128:(j + 1) * 128], in_=T2[:, 128:256], func=AF.Copy,
                              scale=prm[:, gcol0 + 1:gcol0 + 2], R=['bk3', 'prm'], W=[(dnb, j)])
                        else:
                            for (dst, dn, c0_, gc_) in ((dstTa, dna, 0, gcol0), (dstTb, dnb, 128, gcol0 + 1)):
                                for e_ in range(2):
                                    pr = slice(64 * e_, 64 * e_ + 64)
                                    I('act', 'activation', out=dst[e_][pr, j * 128:(j + 1) * 128], in_=T2[pr, c0_:c0_ + 128],
                                      func=AF.Copy, scale=prm[pr, gc_:gc_ + 1], R=['bk3', 'prm', 'qzero'], W=[(dn, j)])

                    s1(0)
                    if ntiles > 1:
                        s1(1)
                    s2(0)
                    for i in range(ntiles + 1):
                        if i + 2 < ntiles:
                            s1(i + 2)
                        if i + 1 < ntiles:
                            s2(i + 1)
                        if i < ntiles:
                            s3(i)
                            s4(i)
                        if i >= 1:
                            s5(i - 1)

                w_sb = load_w(wkv_d[p], WKV, 'wkv')
                tile_phase(xf_d, xfT_d, nkb, w_sb, 'wkv', WKV, P_GC, KTa, KTb, 'KTa', 'KTb', True)
                nlf = S.av([128, 32, 2]); ex = S.av([128, 32, 2])
                fw = S.av([128, 160]); cpref = S.av([128, 32, 2]); gall = S.av([128, 32, 2]); gc = S.av([128, 16, 2])
                fzn = [('fz', j) for j in range(nkb)]
                if nkb < 32:
                    I('dve', 'memset', fz[:, nkb:32, :], 1.0, W=['fzpad'])
                    fzn.append('fzpad')
                I('act', 'activation', out=ex, in_=fz, func=AF.Exp, scale=-1.0, R=fzn, W=['ex'])
                I('act', 'activation', out=nlf, in_=ex, func=AF.Ln, bias=1.0, scale=1.0, R=['ex'], W=['nlf'])
                nlf2 = nlf.rearrange("p a b -> p (a b)")
                I('pe', 'matmul', P0[:, 0:64], tri, nlf2, start=True, stop=True, R=['nlf', 'cst'], W=['bk2'])
                I('pe', 'matmul', P0[:, 64:128], ones, nlf2, start=True, stop=True, R=['nlf', 'cst'], W=['bk2'])
                I('pe', 'matmul', P0[:, 128:160].rearrange("p (a b) -> p a b", a=16), mab[:, 0, :], nlf[:, 0:32:2, :],
                  start=True, stop=False, R=['nlf', 'ccst'], W=['bk2'])
                I('pe', 'matmul', P0[:, 128:160].rearrange("p (a b) -> p a b", a=16), mab[:, 1, :], nlf[:, 1:32:2, :],
                  start=False, stop=True, R=['nlf', 'ccst'], W=['bk2'])
                I('dve', 'tensor_copy', fw, P0[:, 0:160], R=['bk2'], W=['fw'])
                Wv = fw[:, 0:64].rearrange("p (a b) -> p a b", a=32)
                BS = fw[:, 64:128].rearrange("p (a b) -> p a b", a=32)
                PM = fw[:, 128:160].rearrange("p (a b) -> p a b", a=16)
                I('dve', 'memset', cpref[:, 0, :], 0.0, W=['cpref'])
                for j in range(1, 32):
                    I('dve', 'tensor_tensor', out=cpref[:, j, :], in0=cpref[:, j - 1, :], in1=BS[:, j - 1, :], op=ALU.add,
                      R=['cpref', 'fw'], W=['cpref'])
                I('dve', 'tensor_tensor', out=gall, in0=Wv, in1=cpref, op=ALU.add, R=['fw', 'cpref'], W=['gall'])
                I('dve', 'tensor_tensor', out=gc, in0=cpref[:, 0:32:2, :], in1=PM, op=ALU.add, R=['fw', 'cpref'], W=['gc'])
                for m in range(16):
                    nj = 2 * m + 2
                    I('dve', 'tensor_tensor', out=biasT[:, m, 0:nj, :], in0=gall[:, 0:nj, :],
                      in1=gc[:, m, :].unsqueeze(1).to_broadcast([128, nj, 2]), op=ALU.subtract,
                      R=['gall', 'gc'], W=['biasT'])
                S.barrier()
                if stage < 2:
                    continue
                S.aoff = markp
                QTa = [S.av([128, 2048]) for _ in range(2)]
                QTb = [S.av([128, 2048]) for _ in range(2)]
                markq = S.aoff
                for qq in (QTa, QTb):
                    I('pool', 'memset', qq[0][64:128, :], 0.0, W=['qzero'])
                    I('pool', 'memset', qq[1][0:64, :], 0.0, W=['qzero'])
                w_sb = load_w(wq_d[p], 256, 'wq')
                tile_phase(xo_d, xoT_d, 16, w_sb, 'wq', 256, P_GC + 2, QTa, QTb, 'QTa', 'QTb', False)
                S.barrier()
                if stage < 3:
                    continue
                S.aoff = markq
                wo_sb = S.av([128, 2, D])
                S.dma('sp', wo_sb, wo_d[p].rearrange("(r p) n -> p r n", p=128), writes=['wo'])
                NPT = 3
                PT = [S.av([128, 4, 128]) for _ in range(NPT)]
                PT2 = [S.av([128, 3, 128]) for _ in range(2)]
                mixp = [S.av([128, 256]) for _ in range(2)]
                mixT = S.av([128, 2, 128])
                rinv = [S.av([128, 4]) for _ in range(2)]
                den = [S.av([128, 2]) for _ in range(2)]
                O4 = OB[:, 0:384].rearrange("p (a b) -> p a b", a=4)
                sbanks = [P0, SA, SB]
                items = []
                for m in range(16):
                    nj = 2 * m + 2
                    for e_ in range(2):
                        for g0 in range(0, nj, 4):
                            items.append(('fox', m, e_, list(range(g0, min(g0 + 4, nj)))))
                    for e_ in range(2):
                        items.append(('swa', m, e_, [c for c in range(3) if 0 <= 2 * m - 1 + c < 32]))
                    items.append(('end', m, 0, []))
                fcount = [0]
                scount = [0]
                slot_of = {}

                def scores(idx):
                    kind, m, e_, grp = items[idx]
                    qs = slice(m * 128, (m + 1) * 128)
                    pr = slice(64 * e_, 64 * e_ + 64)
                    if kind == 'fox':
                        sl = fcount[0] % NPT
                        fcount[0] += 1
                        slot_of[idx] = sl
                        bk = sbanks[sl]
                        bn = BN[id(bk)]
                        for jj, j in enumerate(grp):
                            diag = j >= 2 * m
                            I('pe', 'matmul', bk[:, jj * 128:(jj + 1) * 128], KTb[:, j * 128:(j + 1) * 128], QTb[e_][:, qs],
                              start=True, stop=(not diag), R=[('KTb', j), ('QTb', m)], W=[bn])
                            if diag:
                                I('pe', 'matmul', bk[:, jj * 128:(jj + 1) * 128], ident, foxmask[:, j - 2 * m, :],
                                  start=False, stop=True, R=['cst', 'ccst'], W=[bn])
                        for jj, j in enumerate(grp):
                            I('act', 'activation', out=PT[sl][:, jj, :], in_=bk[:, jj * 128:(jj + 1) * 128], func=AF.Exp,
                              bias=biasT[:, m, j, e_:e_ + 1], scale=0.125, R=[bn, 'biasT'], W=[('PT', sl, jj)])
                    elif kind == 'swa':
                        sl = scount[0] % 2
                        scount[0] += 1
                        slot_of[idx] = sl
                        for c in grp:
                            j = 2 * m - 1 + c
                            I('pe', 'matmul', S2[:, c * 128:(c + 1) * 128], KTa[:, j * 128:(j + 1) * 128], QTa[e_][:, qs],
                              start=True, stop=False, R=[('KTa', j), ('QTa', m)], W=['bk6'])
                            I('pe', 'matmul', S2[:, c * 128:(c + 1) * 128], jmat, R_[:, e_ * 3 + c, :],
                              start=False, stop=True, R=['cst', ('R', e_, c)], W=['bk6'])
                        c0 = grp[0]
                        I('act', 'activation', out=PT2[sl][:, c0:3, :].rearrange("p a b -> p (a b)"), in_=S2[:, c0 * 128:384],
                          func=AF.Exp, scale=0.125, R=['bk6'], W=[('PT2', sl)])

                def pvs(idx):
                    kind, m, e_, grp = items[idx]
                    par = m % 2
                    if kind == 'fox':
                        sl = slot_of[idx]
                        nj = 2 * m + 2
                        for jj, j in enumerate(grp):
                            I('pe', 'matmul', O4[:, e_, :], PT[sl][:, jj, :], VbF[:, (2 * j + e_) * 65:(2 * j + e_) * 65 + 96],
                              start=(j == 0), stop=(j == nj - 1),
                              R=[('PT', sl, jj), ('Vb', j), ('Vb', min(j + 1, 31)), 'Vb1', 'vpb'], W=['bk7'])
                    elif kind == 'swa':
                        sl = slot_of[idx]
                        for c in grp:
                            j = 2 * m - 1 + c
                            I('pe', 'matmul', O4[:, 2 + e_, :], PT2[sl][:, c, :], VaF[:, j * 65:j * 65 + 96], start=(c == grp[0]), stop=(c == 2),
                              R=[('PT2', sl), ('Va', j), ('Va', min(j + 1, 31)), 'Va1', 'vpa'], W=['bk7'])
                    else:
                        for e_ in range(2):
                            I('dve', 'reciprocal', rinv[par][:, e_:e_ + 1], O4[:, e_, 64:65], R=['bk7'], W=[('rinv', par, e_)])
                            I('dve', 'tensor_scalar', out=mixp[par][:, 128 + 64 * e_:128 + 64 * e_ + 64], in0=O4[:, e_, 0:64],
                              scalar1=rinv[par][:, e_:e_ + 1], scalar2=None, op0=ALU.mult,
                              R=['bk7', ('rinv', par, e_)], W=[('mixp', par)])
                            I('dve', 'tensor_tensor', out=den[par][:, e_:e_ + 1], in0=O4[:, 2 + e_, 64:65],
                              in1=esink[:, 2 * p + e_:2 * p + e_ + 1], op=ALU.add, R=['bk7', 'esink'], W=[('den', par, e_)])
                            I('dve', 'reciprocal', rinv[par][:, 2 + e_:3 + e_], den[par][:, e_:e_ + 1],
                              R=[('den', par, e_)], W=[('rinv', par, 2 + e_)])
                            I('dve', 'tensor_scalar', out=mixp[par][:, 64 * e_:64 * e_ + 64], in0=O4[:, 2 + e_, 0:64],
                              scalar1=rinv[par][:, 2 + e_:3 + e_], scalar2=None, op0=ALU.mult,
                              R=['bk7', ('rinv', par, 2 + e_)], W=[('mixp', par)])

                def outproj_t(m):
                    par = m % 2
                    I('pe', 'transpose', T2[:, 0:128], mixp[par][:, 0:128], ident, R=[('mixp', par), 'cst'], W=['bk3'])
                    I('pe', 'transpose', T2[:, 128:256], mixp[par][:, 128:256], ident, R=[('mixp', par), 'cst'], W=['bk3'])
                    I('act', 'copy', mixT.rearrange("p a b -> p (a b)"), T2[:, 0:256], R=['bk3'], W=['mixT'])

                def outproj_w(m):
                    for hf, bk in ((0, T0), (1, T1)):
                        for r in range(2):
                            I('pe', 'matmul', bk[:, :], mixT[:, r, :], wo_sb[:, r, hf * 512:(hf + 1) * 512],
                              start=(r == 0), stop=(r == 1), R=['mixT', 'wo'], W=[BN[id(bk)]])
                        I('dve', 'tensor_tensor', out=x2[:, m, hf * 512:(hf + 1) * 512], in0=x2[:, m, hf * 512:(hf + 1) * 512],
                          in1=bk[:, :], op=ALU.add, R=[BN[id(bk)], ('x2', m)], W=[('x2', m)])

                LAG = 2
                pending = []
                n_items = len(items)
                for idx in range(n_items + LAG):
                    if idx < n_items:
                        scores(idx)
                    k = idx - LAG
                    if 0 <= k < n_items:
                        pvs(k)
                        if items[k][0] == 'end':
                            pending.append((idx + 2, outproj_t, items[k][1]))
                            pending.append((idx + 4, outproj_w, items[k][1]))
                    for it in [q for q in pending if q[0] <= idx]:
                        it[1](it[2])
                        pending.remove(it)
                for it in sorted(pending, key=lambda q: q[0]):
                    it[1](it[2])
                S.barrier()
            S.aoff = mark0

        if dbg:
            for m in range(16):
                S.dma('sp', dbg_d[m * 128:(m + 1) * 128, :], x2[:, m, :], reads=[('x2', m)], out=True, sem='d_dbg%d' % m)
            S.barrier()

        if peer:
            S.aoff = 0
            eidx = S.av([128, 16, 128], I32)
            gate = S.av([128, 16, 128])
            gf = S.av([128, D])
            xnA = S.av([128, D])
            xnB = S.av([128, D])
            junk = S.av([128, D], BF16)
            nrm = {tg: (S.av([128, 1]), S.av([128, 1]), S.av([128, 1])) for tg in ('P', 'Q')}
            S.dma('sp', gf, gf_d, writes=['gf'])

            def norm_tile(t, xn, tg):
                ssq, std, rstd = nrm[tg]
                rms_rstd(x2[:, t, :], ('x2', t), ssq, std, rstd, junk, tg, D)
                I('dve', 'scalar_tensor_tensor', out=xn, in0=x2[:, t, :], scalar=rstd[:, 0:1], in1=gf,
                  op0=ALU.mult, op1=ALU.mult, R=[('x2', t), 'rstd' + tg, 'gf'], W=['xn' + tg])

            skT = S.av([128, 2, 128])
            wq_sb = S.av([128, 8, 1024])
            xnT = S.av([128, 8, 128])
            qvT = S.av([128, 8, 128])
            sD = [S.av([128, 8, 128]) for _ in range(2)]; s2 = S.av([128, 8, 128])
            vals = S.av([128, 8, 16]); idxu = S.av([128, 8, 16], U32); idxf = S.av([128, 8, 16])
            cand = S.av([128, 4, 256]); cand2 = S.av([128, 4, 256])
            ts = S.av([128, 4, 16]); pos = S.av([128, 4, 16], U32); posf = S.av([128, 4, 16])
            ge4 = S.av([128, 4, 16, 16])
            af = S.av([128, 4, 16]); bfv = S.av([128, 4, 16]); sel1 = S.av([128, 4, 16]); sel2 = S.av([128, 4, 16])
            eidf = S.av([128, 4, 16]); tsm = S.av([128, 4, 16]); gex = S.av([128, 4, 16])
            gs = S.av([128, 4]); rg = S.av([128, 4])
            v4 = vals.rearrange("p (h a) k -> p h a k", a=2)
            i4 = idxf.rearrange("p (h a) k -> p h a k", a=2)
            qbanks = [P0, SA]
            B4 = [128, 4, 16, 16]
            NGBL = ngb
            ring = [S.av([128, D]) for _ in range(NGBL)]
            acc1 = S.av([128, D])
            pre = S.av([128, 128]); gl = S.av([128, 128]); coef = S.av([128, 128])
            gcount = [0]
            skn = ring[0][:, 0:256].rearrange("p (a b) -> p a b", a=2)
            S.dma('sp', skn, sk_d.rearrange("a n d -> n a d"), writes=['skn'])
            for a in range(2):
                I('pe', 'transpose', T2[:, a * 128:(a + 1) * 128], skn[:, a, :], ident, R=['skn', 'cst'], W=['bk3'])
            I('act', 'copy', skT.rearrange("p a b -> p (a b)"), T2[:, 0:256], R=['bk3'], W=[('ring', 0), 'skT'])

            def load_wq(hh):
                for c in range(8):
                    S.dma('sp', wq_sb[:, c, :], wqy_d[c * 128:(c + 1) * 128, hh * 1024:(hh + 1) * 1024], writes=[('wqy', c)])

            def p1_front(t, hh):
                norm_tile(t, xnA, 'P')
                for c in range(8):
                    bk = T0 if c < 4 else T1
                    I('pe', 'transpose', bk[:, (c % 4) * 128:(c % 4 + 1) * 128], xnA[:, c * 128:(c + 1) * 128], ident,
                      R=['xnP', 'cst'], W=[BN[id(bk)]])
                I('act', 'copy', xnT[:, 0:4, :].rearrange("p a b -> p (a b)"), T0[:, :], R=['bk0'], W=['xnTa'])
                I('act', 'copy', xnT[:, 4:8, :].rearrange("p a b -> p (a b)"), T1[:, :], R=['bk1'], W=['xnTb'])
                for g in range(2):
                    bk = qbanks[g]
                    for cq in range(4):
                        cc = 4 * g + cq
                        for c in range(8):
                            I('pe', 'matmul', bk[:, cq * 128:(cq + 1) * 128], wq_sb[:, c, cc * 128:(cc + 1) * 128],
                              xnT[:, c, :], start=(c == 0), stop=(c == 7),
                              R=[('wqy', c), 'xnTa', 'xnTb'], W=[BN[id(bk)]])
                    I('act', 'copy', qvT[:, 4 * g:4 * g + 4, :].rearrange("p a b -> p (a b)"), bk[:, :],
                      R=[BN[id(bk)]], W=[('qvT', g)])
                for g in range(2):
                    bk = [S2, OB][g]
                    for cq in range(4):
                        cc = 4 * g + cq
                        I('pe', 'matmul', bk[:, cq * 128:(cq + 1) * 128], qvT[:, cc, :], skT[:, cc % 2, :],
                          start=True, stop=True, R=[('qvT', g), 'skT'], W=[BN[id(bk)]])
                    I('act', 'copy', sD[t % 2][:, 4 * g:4 * g + 4, :].rearrange("p a b -> p (a b)"), bk[:, :],
                      R=[BN[id(bk)]], W=[('s', t % 2, g)])

            def p1_chain(t, hh):
                for cc in range(8):
                    I('dve', 'max', vals[:, cc, 0:8], sD[t % 2][:, cc, :], R=[('s', t % 2, cc // 4)], W=[('v', cc, 0)])
                for cc in range(8):
                    I('dve', 'max_index', idxu[:, cc, 0:8], vals[:, cc, 0:8], sD[t % 2][:, cc, :],
                      R=[('s', t % 2, cc // 4), ('v', cc, 0)], W=[('i', cc, 0)])
                for cc in range(8):
                    I('dve', 'match_replace', s2[:, cc, :], vals[:, cc, 0:8], sD[t % 2][:, cc, :], -1e30,
                      R=[('s', t % 2, cc // 4), ('v', cc, 0)], W=[('s2', cc)])
                for cc in range(8):
                    I('dve', 'max', vals[:, cc, 8:16], s2[:, cc, :], R=[('s2', cc)], W=[('v', cc, 1)])
                for cc in range(8):
                    I('dve', 'max_index', idxu[:, cc, 8:16], vals[:, cc, 8:16], s2[:, cc, :],
                      R=[('s2', cc), ('v', cc, 1)], W=[('i', cc, 1)])
                alli = [('i', cc, k) for cc in range(8) for k in range(2)]
                allv = [('v', cc, k) for cc in range(8) for k in range(2)]
                I('dve', 'tensor_copy', idxf, idxu, R=alli, W=['idxf'])
                I('dve', 'tensor_tensor', out=cand.rearrange("p h (a b) -> p h a b", a=16),
                  in0=v4[:, :, 0, :].unsqueeze(3).to_broadcast(B4), in1=v4[:, :, 1, :].unsqueeze(2).to_broadcast(B4),
                  op=ALU.add, R=allv, W=['cand'])
                for h in range(4):
                    I('dve', 'max', ts[:, h, 0:8], cand[:, h, :], R=['cand'], W=[('ts', h, 0)])
                for h in range(4):
                    I('dve', 'max_index', pos[:, h, 0:8], ts[:, h, 0:8], cand[:, h, :],
                      R=['cand', ('ts', h, 0)], W=[('pos', h, 0)])
                for h in range(4):
                    I('dve', 'match_replace', cand2[:, h, :], ts[:, h, 0:8], cand[:, h, :], -1e30,
                      R=['cand', ('ts', h, 0)], W=[('cand2', h)])
                for h in range(4):
                    I('dve', 'max', ts[:, h, 8:16], cand2[:, h, :], R=[('cand2', h)], W=[('ts', h, 1)])
                for h in range(4):
                    I('dve', 'max_index', pos[:, h, 8:16], ts[:, h, 8:16], cand2[:, h, :],
                      R=[('cand2', h), ('ts', h, 1)], W=[('pos', h, 1)])
                allp = [('pos', h, k) for h in range(4) for k in range(2)]
                allt = [('ts', h, k) for h in range(4) for k in range(2)]
                I('dve', 'tensor_copy', posf, pos, R=allp, W=['posf'])
                I('dve', 'tensor_tensor', out=ge4, in0=posf.unsqueeze(3).to_broadcast(B4),
                  in1=iota16.unsqueeze(1).unsqueeze(1).to_broadcast(B4), op=ALU.is_ge, R=['posf', 'cst'], W=['ge4'])
                I('dve', 'tensor_reduce', out=af, in_=ge4, axis=AX.X, op=ALU.add, R=['ge4'], W=['af'])
                I('dve', 'tensor_scalar', out=af, in0=af, scalar1=-1.0, scalar2=None, op0=ALU.add, R=['af'], W=['af'])
                I('dve', 'scalar_tensor_tensor', out=bfv, in0=af, scalar=-16.0, in1=posf, op0=ALU.mult, op1=ALU.add,
                  R=['af', 'posf'], W=['bfv'])
                for (src, ii, dst, nm) in ((af, 0, sel1, 'sel1'), (bfv, 1, sel2, 'sel2')):
                    I('dve', 'tensor_tensor', out=ge4, in0=iota.unsqueeze(1).unsqueeze(1).to_broadcast(B4),
                      in1=src.unsqueeze(3).to_broadcast(B4), op=ALU.is_equal, R=['af', 'bfv', 'cst'], W=['ge4'])
                    I('dve', 'tensor_tensor', out=ge4, in0=ge4, in1=i4[:, :, ii, :].unsqueeze(2).to_broadcast(B4),
                      op=ALU.mult, R=['ge4', 'idxf'], W=['ge4'])
                    I('dve', 'tensor_reduce', out=dst, in_=ge4, axis=AX.X, op=ALU.add, R=['ge4'], W=[nm])
                I('dve', 'scalar_tensor_tensor', out=eidf, in0=sel1, scalar=128.0, in1=sel2, op0=ALU.mult, op1=ALU.add,
                  R=['sel1', 'sel2'], W=['eidf'])
                I('dve', 'tensor_copy', eidx[:, t, hh * 64:(hh + 1) * 64], eidf.rearrange("p a b -> p (a b)"),
                  R=['eidf'], W=[('eidx', t, hh)])
                I('dve', 'tensor_tensor', out=tsm, in0=ts, in1=ts[:, :, 0:1].to_broadcast([128, 4, 16]), op=ALU.subtract,
                  R=allt, W=['tsm'])
                I('act', 'activation', out=gex, in_=tsm, func=AF.Exp, R=['tsm'], W=['gex'])
                I('dve', 'tensor_reduce', out=gs, in_=gex, axis=AX.X, op=ALU.add, R=['gex'], W=['gs'])
                I('dve', 'reciprocal', rg, gs, R=['gs'], W=['rg'])
                I('dve', 'tensor_tensor', out=gate[:, t, hh * 64:(hh + 1) * 64].rearrange("p (a b) -> p a b", a=4),
                  in0=gex, in1=rg.unsqueeze(2).to_broadcast([128, 4, 16]), op=ALU.mult,
                  R=['gex', 'rg'], W=[('gate', t, hh)])


            def p2_tile(t):
                norm_tile(t, xnB, 'Q')
                en = [('eidx', t, 0), ('eidx', t, 1)]
                for sl_ in range(128):
                    r = gcount[0] % NGBL
                    gcount[0] += 1
                    S.dma('pool', ring[r], eu_d, reads=en, writes=[('ring', r)],
                          indirect=bass.IndirectOffsetOnAxis(ap=eidx[:, t, sl_:sl_ + 1], axis=0))
                    if nodve:
                        I('dve', 'tensor_copy', pre[:, sl_:sl_ + 1], ring[r][:, 0:1], R=[('ring', r), 'xnQ'], W=[('pre', sl_)])
                    else:
                        I('dve', 'scalar_tensor_tensor', out=ring[r], in0=ring[r], scalar=1.0, in1=xnB, op0=ALU.mult, op1=ALU.mult,
                          accum_out=pre[:, sl_:sl_ + 1], R=['xnQ'], W=[('ring', r), ('pre', sl_)])
                I('act', 'activation', out=gl, in_=pre, func=AF.Gelu, R=[('pre', k) for k in range(128)], W=['gl'])
                I('dve', 'tensor_tensor', out=coef, in0=gl, in1=gate[:, t, :], op=ALU.mult,
                  R=['gl', ('gate', t, 0), ('gate', t, 1)], W=['coef'])
                for sl_ in range(128):
                    r = gcount[0] % NGBL
                    gcount[0] += 1
                    S.dma('pool', ring[r], ev_d, reads=en, writes=[('ring', r)],
                          indirect=bass.IndirectOffsetOnAxis(ap=eidx[:, t, sl_:sl_ + 1], axis=0))
                    if nodve:
                        I('dve', 'tensor_copy', acc1[:, 0:1], ring[r][:, 0:1], R=[('ring', r), 'coef'], W=['acc1'])
                    elif sl_ == 1:
                        I('dve', 'tensor_scalar', out=acc1, in0=ring[r], scalar1=coef[:, sl_:sl_ + 1], scalar2=None,
                          op0=ALU.mult, R=[('ring', r), 'coef'], W=['acc1'])
                    elif sl_ % 2 == 1:
                        I('dve', 'scalar_tensor_tensor', out=acc1, in0=ring[r], scalar=coef[:, sl_:sl_ + 1], in1=acc1,
                          op0=ALU.mult, op1=ALU.add, R=[('ring', r), 'coef', 'acc1'], W=['acc1'])
                    else:
                        I('dve', 'scalar_tensor_tensor', out=x2[:, t, :], in0=ring[r], scalar=coef[:, sl_:sl_ + 1],
                          in1=x2[:, t, :], op0=ALU.mult, op1=ALU.add, R=[('ring', r), 'coef', ('x2', t)], W=[('x2', t)])
                I('dve', 'tensor_tensor', out=x2[:, t, :], in0=x2[:, t, :], in1=acc1, op=ALU.add,
                  R=['acc1', ('x2', t)], W=[('x2', t)])
                S.dma('sp', y_d[t * 128:(t + 1) * 128, :], x2[:, t, :], reads=[('x2', t)], out=True, sem='d_y%d' % t)

            load_wq(0)
            p1_front(0, 0)
            for t in range(NT):
                if t + 1 < NT:
                    p1_front(t + 1, 0)
                p1_chain(t, 0)
            load_wq(1)
            p1_front(0, 1)
            for t in range(NT):
                if t + 1 < NT:
                    p1_front(t + 1, 1)
                p1_chain(t, 1)
                if p2:
                    p2_tile(t)
        else:
            for t in range(NT):
                S.dma('sp', y_d[t * 128:(t + 1) * 128, :], x2[:, t, :], reads=[('x2', t)], out=True, sem='d_y%d' % t)
        S.finish()
    return nc


def _t5_bucket_table():
    d = np.arange(256)
    nf = np.maximum(d, 1).astype(np.float32)
    large = 16 + (np.log(nf / np.float32(16)) / np.float32(math.log(128 / 16)) * np.float32(16)).astype(np.int32)
    large = np.minimum(large, 31)
    return np.where(d < 16, d, large)


def _consts():
    cst = np.zeros((128, CSTW), np.float32)
    cst[:, C_ID:C_ID + 128] = np.eye(128)
    cst[:, C_J:C_J + 128] = np.eye(128)[::-1]
    cst[:, C_TRI:C_TRI + 128] = np.triu(np.ones((128, 128)))
    cst[:, C_ONE:C_ONE + 128] = 1.0
    cst[:, C_IOTA:C_IOTA + 16] = np.arange(16)
    cst[:, C_IOTA16:C_IOTA16 + 16] = 16 * np.arange(16)
    return cst


def _core_consts(h):
    cc = np.zeros((128, 512), np.float32)
    k = np.arange(128)[:, None]
    q = np.arange(128)[None, :]
    trimask = np.where(k > q, 8 * NEGM, 0.0).astype(np.float32)
    full = np.full((128, 128), 8 * NEGM, np.float32)
    upto = np.zeros((128, 128), np.float32)
    upto[:64, :] = 1.0
    if h == 0:
        cc[:, 0:128] = trimask
        cc[:, 128:256] = full
        cc[:, 256:384] = upto
        cc[:, 384:512] = 0.0
    else:
        cc[:, 0:128] = 0.0
        cc[:, 128:256] = trimask
        cc[:, 256:384] = 1.0
        cc[:, 384:512] = upto
    bt = _t5_bucket_table()
    oh = np.zeros((33, 768), np.float32)
    types = ['prev', 'cur', 'mask'] if h == 0 else ['mask', 'prev', 'cur']
    for c, ty in enumerate(types):
        for mp in range(256):
            col = c * 256 + mp
            if ty == 'prev':
                dist = mp + 1
            elif ty == 'cur':
                dist = mp - 127
            else:
                dist = -1
            if 0 <= dist < 128 and mp <= 254:
                oh[bt[dist], col] = 1.0
            else:
                oh[32, col] = NEGM
    return cc, oh


_NC_CACHE = {}


def prepare(inputs):
    x = np.asarray(inputs["x"], np.float32)
    w_in = np.asarray(inputs["w_in"], np.float32)[0]
    w_out = np.asarray(inputs["w_out"], np.float32)[0]
    rep = lambda v, n=128: np.ascontiguousarray(np.broadcast_to(np.asarray(v, np.float32).reshape(1, -1), (n, np.size(v))))
    wq4 = np.zeros((4, D, 256), np.float32)
    wkv4 = np.zeros((4, D, WKV), np.float32)
    wo4 = np.zeros((4, 256, D), np.float32)
    for p in range(4):
        kvh = p // 2
        qa = w_in[:, 128 * p:128 * p + 128]
        qb = w_in[:, 768 + 128 * p:768 + 128 * p + 128]
        ka = w_in[:, 512 + 64 * kvh:512 + 64 * kvh + 64]
        va = w_in[:, 640 + 64 * kvh:640 + 64 * kvh + 64]
        kb = w_in[:, 1280 + 128 * p:1280 + 128 * p + 128]
        vb = w_in[:, 1792 + 128 * p:1792 + 128 * p + 128]
        fb = w_in[:, 2304 + 2 * p:2304 + 2 * p + 2]
        wq4[p] = np.concatenate([qa, qb], axis=1)
        wkv4[p] = np.concatenate([ka, ka, kb, va, vb, fb], axis=1)
        wo4[p] = np.concatenate([w_out[128 * p:128 * p + 128], w_out[512 + 128 * p:512 + 128 * p + 128]], axis=0)
    prm = np.zeros((128, 32), np.float32)
    prm[:, P_GM:P_GM + 8] = np.asarray(inputs["norm_mix"], np.float32)[0].reshape(8, 128).T
    dup = lambda v: np.concatenate([np.asarray(v, np.float32)[0], np.asarray(v, np.float32)[0]])
    prm[:, P_GC + 0] = dup(inputs["k_norm_a"])
    prm[:, P_GC + 1] = dup(inputs["k_norm_b"])
    prm[:, P_GC + 2] = dup(inputs["q_norm_a"])
    prm[:, P_GC + 3] = dup(inputs["q_norm_b"])
    prm[:, P_BF:P_BF + 8] = rep(inputs["b_forget"])
    prm[:, P_SK:P_SK + 8] = rep(inputs["sinks"])
    cst = _consts()
    shared = {
        "cst": cst, "prm": prm, "wq4": wq4, "wkv4": wkv4, "wo4": wo4,
        "relb": np.asarray(inputs["rel_bias"], np.float32),
        "wqy": np.asarray(inputs["w_query"], np.float32)[0],
        "sk": np.stack([np.asarray(inputs["sub_keys1"], np.float32)[0], np.asarray(inputs["sub_keys2"], np.float32)[0]]),
        "eu": np.asarray(inputs["expert_u"], np.float32)[0],
        "ev": np.asarray(inputs["expert_v"], np.float32)[0],
        "gf": rep(inputs["norm_ffn"]),
    }
    in_maps = []
    for c in range(8):
        b, h = c // 2, c % 2
        xb = x[b]
        xo = np.ascontiguousarray(xb.reshape(16, 2, 128, D)[:, h].reshape(2048, D))
        cc, oh = _core_consts(h)
        m = dict(shared)
        m.update({"xo": xo, "xf": xb, "xoT": np.ascontiguousarray(xo.T), "xfT": np.ascontiguousarray(xb.T),
                  "ccst": cc, "ohtab": oh})
        in_maps.append(m)
    return in_maps


def kernel(**inputs):
    in_maps = prepare(inputs)
    if "nc" not in _NC_CACHE:
        _NC_CACHE["nc"] = build_nc()
    res = run_bass_kernel_spmd(_NC_CACHE["nc"], in_maps, core_ids=list(range(8)))
    out = np.zeros((4, 4096, D), np.float32)
    for c in range(8):
        b, h = c // 2, c % 2
        out[b].reshape(16, 2, 128, D)[:, h] = res.results[c]["y"].reshape(16, 128, D)
    return out
```
